# Optimizing a Trainium2 kernel written in Bass

```python
import jax, jax.numpy as jnp
from jax import lax
import numpy as np


D_MODEL = 1024
BATCH = 4
SEQ = 4096
DEPTH = 1
DEC_BATCH = 2
DEC_SEQ = 16384
PAST_LEN = 128

HEAD_DIM = 64
HEADS_PER_GROUP = 8
DILATION_GROUPS = ((128, 1), (512, 4), (2048, 16))
N_GROUPS = 3
N_ATT_HEADS = N_GROUPS * HEADS_PER_GROUP
ATT_WIDTH = N_ATT_HEADS * HEAD_DIM
ATT_OUT_WIDTH = HEADS_PER_GROUP * HEAD_DIM
CONV_DIM = D_MODEL
CONV_WIDTH = 3
D_FF = 4 * D_MODEL
ROPE_THETA = 10000.0
EPS = 1e-6
NEG = -1e30
SPLIT_SIZES = (ATT_WIDTH, ATT_WIDTH, ATT_WIDTH, CONV_DIM, CONV_DIM, CONV_DIM, D_MODEL, D_MODEL)
IN_WIDTH = 3 * ATT_WIDTH + 3 * CONV_DIM + 2 * D_MODEL

kernel_name = "hybrid_dilated_attn_shortconv_encoder"


def rms_norm(x, g):
    xf = x.astype(jnp.float32)
    y = xf * lax.rsqrt(jnp.mean(xf * xf, axis=-1, keepdims=True) + EPS)
    return (y * g.astype(jnp.float32)).astype(x.dtype)


def rope(x, pos):
    half = HEAD_DIM // 2
    inv = 1.0 / (ROPE_THETA ** (jnp.arange(half, dtype=jnp.float32) / half))
    ang = pos.astype(jnp.float32)[:, None] * inv[None, :]
    cos = jnp.cos(ang)[None, :, None, :]
    sin = jnp.sin(ang)[None, :, None, :]
    xf = x.astype(jnp.float32)
    x1, x2 = xf[..., :half], xf[..., half:]
    return jnp.concatenate([x1 * cos - x2 * sin, x2 * cos + x1 * sin], axis=-1).astype(x.dtype)


def banded_attention(q, k, v, half):
    n, L, H, hd = q.shape
    blk = half
    nb = -(-L // blk)
    pad = nb * blk - L
    qb = jnp.pad(q, ((0, 0), (0, pad), (0, 0), (0, 0))).reshape(n, nb, blk, H, hd)
    kp = jnp.pad(k, ((0, 0), (blk, pad + blk), (0, 0), (0, 0))).reshape(n, nb + 2, blk, H, hd)
    vp = jnp.pad(v, ((0, 0), (blk, pad + blk), (0, 0), (0, 0))).reshape(n, nb + 2, blk, H, hd)
    kb = jnp.concatenate([kp[:, :-2], kp[:, 1:-1], kp[:, 2:]], axis=2)
    vb = jnp.concatenate([vp[:, :-2], vp[:, 1:-1], vp[:, 2:]], axis=2)
    s = jnp.einsum("nbqhd,nbkhd->nbhqk", qb, kb, preferred_element_type=jnp.float32) * (hd ** -0.5)
    qpos = jnp.arange(nb)[:, None] * blk + jnp.arange(blk)[None, :]
    kpos = jnp.arange(nb)[:, None] * blk - blk + jnp.arange(3 * blk)[None, :]
    rel = kpos[:, None, :] - qpos[:, :, None]
    valid = (jnp.abs(rel) <= half) & (kpos[:, None, :] >= 0) & (kpos[:, None, :] < L)
    s = jnp.where(valid[None, :, None, :, :], s, NEG)
    m = jnp.max(s, axis=-1, keepdims=True)
    p = jnp.exp(s - m)
    den = jnp.sum(p, axis=-1)
    o = jnp.einsum("nbhqk,nbkhd->nbqhd", p.astype(v.dtype), vb, preferred_element_type=jnp.float32)
    den_t = jnp.transpose(den, (0, 1, 3, 2))
    o = o / den_t[..., None]
    lse = jnp.transpose(m[..., 0], (0, 1, 3, 2)) + jnp.log(den_t)
    o = o.reshape(n, nb * blk, H, hd)[:, :L]
    lse = lse.reshape(n, nb * blk, H)[:, :L]
    return o, lse


def dilated_group(q, k, v, window, dil):
    B, S, H, hd = q.shape
    L = S // dil
    half = window // (2 * dil)

    def to_res(t):
        return t.reshape(B, L, dil, H, hd).transpose(0, 2, 1, 3, 4).reshape(B * dil, L, H, hd)

    o, lse = banded_attention(to_res(q), to_res(k), to_res(v), half)
    o = o.reshape(B, dil, L, H, hd).transpose(0, 2, 1, 3, 4).reshape(B, S, H, hd)
    lse = lse.reshape(B, dil, L, H).transpose(0, 2, 1, 3).reshape(B, S, H)
    return o, lse


def token_mixer(xn, w_in, conv_w, conv_b, w_attn_out, w_conv_out, w_mix_out):
    B, S, _ = xn.shape
    proj = jnp.einsum("bsd,de->bse", xn, w_in)
    points = []
    acc = 0
    for sz in SPLIT_SIZES[:-1]:
        acc += sz
        points.append(acc)
    q, k, v, h, c_gate, b_gate, g_att, g_conv = jnp.split(proj, points, axis=-1)
    pos = jnp.arange(S)
    q = rope(q.reshape(B, S, N_ATT_HEADS, HEAD_DIM), pos)
    k = rope(k.reshape(B, S, N_ATT_HEADS, HEAD_DIM), pos)
    v = v.reshape(B, S, N_ATT_HEADS, HEAD_DIM)
    outs = []
    lses = []
    for gi, (window, dil) in enumerate(DILATION_GROUPS):
        lo, hi = gi * HEADS_PER_GROUP, (gi + 1) * HEADS_PER_GROUP
        o, lse = dilated_group(q[:, :, lo:hi], k[:, :, lo:hi], v[:, :, lo:hi], window, dil)
        outs.append(o)
        lses.append(lse)
    wts = jax.nn.softmax(jnp.stack(lses, axis=0), axis=0)
    att = jnp.sum(wts[..., None] * jnp.stack(outs, axis=0), axis=0)
    att = att.astype(xn.dtype).reshape(B, S, ATT_OUT_WIDTH)
    att = jnp.einsum("bse,ed->bsd", att, w_attn_out)
    u = c_gate * h
    r = CONV_WIDTH // 2
    up = jnp.pad(u, ((0, 0), (r, r), (0, 0)))
    conv = conv_b[None, None, :]
    for t in range(CONV_WIDTH):
        conv = conv + up[:, t:t + S] * conv_w[t][None, None, :]
    cv = jnp.einsum("bse,ed->bsd", b_gate * conv, w_conv_out)
    merged = jax.nn.sigmoid(g_att) * att + jax.nn.sigmoid(g_conv) * cv
    return jnp.einsum("bsd,de->bse", merged, w_mix_out)


def encoder(x, c, w_ada, b_ada, norm1_g, w_in, conv_w, conv_b, w_attn_out, w_conv_out,
            w_mix_out, norm2_g, w_mlp_in, w_mlp_out, final_norm_g):
    for l in range(DEPTH):
        mod = jnp.einsum("bd,de->be", jax.nn.silu(c), w_ada[l]) + b_ada[l]
        sh1, sc1, gt1, sh2, sc2, gt2 = jnp.split(mod, 6, axis=-1)
        xn = rms_norm(x, norm1_g[l]) * (1.0 + sc1[:, None, :]) + sh1[:, None, :]
        x = x + gt1[:, None, :] * token_mixer(xn, w_in[l], conv_w[l], conv_b[l], w_attn_out[l],
                                              w_conv_out[l], w_mix_out[l])
        xn = rms_norm(x, norm2_g[l]) * (1.0 + sc2[:, None, :]) + sh2[:, None, :]
        hdn = jnp.square(jax.nn.relu(jnp.einsum("bsd,df->bsf", xn, w_mlp_in[l])))
        x = x + gt2[:, None, :] * jnp.einsum("bsf,fd->bsd", hdn, w_mlp_out[l])
    return rms_norm(x, final_norm_g)


def setup_inputs(seed: int = 0) -> dict:
    key = jax.random.key(seed)
    ks = jax.random.split(key, 20)
    f32 = jnp.float32
    nrm = lambda k, shape, s: jax.random.normal(k, shape, f32) * s
    return {
        "x_prompt": nrm(ks[0], (BATCH, SEQ, D_MODEL), 1.0),
        "x_sample": nrm(ks[1], (DEC_BATCH, DEC_SEQ, D_MODEL), 1.0),
        "c_prompt": nrm(ks[2], (BATCH, D_MODEL), 1.0),
        "c_sample": nrm(ks[3], (DEC_BATCH, D_MODEL), 1.0),
        "w_ada": nrm(ks[4], (DEPTH, D_MODEL, 6 * D_MODEL), 0.5 * D_MODEL ** -0.5),
        "b_ada": nrm(ks[5], (DEPTH, 6 * D_MODEL), 0.02),
        "norm1_g": 1.0 + nrm(ks[6], (DEPTH, D_MODEL), 0.02),
        "w_in": nrm(ks[7], (DEPTH, D_MODEL, IN_WIDTH), D_MODEL ** -0.5),
        "conv_w": nrm(ks[8], (DEPTH, CONV_WIDTH, CONV_DIM), CONV_WIDTH ** -0.5),
        "conv_b": nrm(ks[9], (DEPTH, CONV_DIM), 0.02),
        "w_attn_out": nrm(ks[10], (DEPTH, ATT_OUT_WIDTH, D_MODEL), ATT_OUT_WIDTH ** -0.5),
        "w_conv_out": nrm(ks[11], (DEPTH, CONV_DIM, D_MODEL), CONV_DIM ** -0.5),
        "w_mix_out": nrm(ks[12], (DEPTH, D_MODEL, D_MODEL), D_MODEL ** -0.5),
        "norm2_g": 1.0 + nrm(ks[13], (DEPTH, D_MODEL), 0.02),
        "w_mlp_in": nrm(ks[14], (DEPTH, D_MODEL, D_FF), D_MODEL ** -0.5),
        "w_mlp_out": nrm(ks[15], (DEPTH, D_FF, D_MODEL), D_FF ** -0.5),
        "final_norm_g": 1.0 + nrm(ks[16], (D_MODEL,), 0.02),
    }


def reference(x_prompt, x_sample, c_prompt, c_sample, w_ada, b_ada, norm1_g, w_in, conv_w, conv_b,
              w_attn_out, w_conv_out, w_mix_out, norm2_g, w_mlp_in, w_mlp_out, final_norm_g):
    y_prompt = encoder(x_prompt, c_prompt, w_ada, b_ada, norm1_g, w_in, conv_w, conv_b, w_attn_out,
                       w_conv_out, w_mix_out, norm2_g, w_mlp_in, w_mlp_out, final_norm_g)
    y_sample = encoder(x_sample, c_sample, w_ada, b_ada, norm1_g, w_in, conv_w, conv_b, w_attn_out,
                       w_conv_out, w_mix_out, norm2_g, w_mlp_in, w_mlp_out, final_norm_g)
    return (y_prompt, y_sample)
```

```python
import numpy as np
import concourse.bass as bass
import concourse.mybir as mybir
from concourse.bass_utils import run_bass_kernel_spmd

F32 = mybir.dt.float32
BF16 = mybir.dt.bfloat16
ALU = mybir.AluOpType
AF = mybir.ActivationFunctionType

D = 1024
T = 1024
HALO = 1024
EXT = T + 2 * HALO
NCH = 6
NCORES = 8
EPS = 1e-6
NEG = -30000.0
NKT = 45
ENGS = ("pe", "act", "dve", "pool", "sp")
RING = 10


class Prog:
    def __init__(self, nc):
        self.nc = nc
        self.ops = []
        self.last_write = {}
        self.reads_since = {}
        self.alias = {}

    def add_alias(self, a, b):
        self.alias.setdefault(a, set()).add(b)
        self.alias.setdefault(b, set()).add(a)

    def op(self, eng, fn, reads=(), writes=(), dma=None):
        idx = len(self.ops)
        raw, other = set(), set()
        for r in reads:
            if r in self.last_write:
                raw.add(self.last_write[r])
        for w in writes:
            for n in {w} | self.alias.get(w, set()):
                if n in self.last_write:
                    other.add(self.last_write[n])
                for x in self.reads_since.get(n, ()):
                    other.add(x)
        self.ops.append(dict(eng=eng, fn=fn, raw=raw, other=other - raw, dma=dma))
        for r in reads:
            self.reads_since.setdefault(r, []).append(idx)
        for w in writes:
            self.last_write[w] = idx
            self.reads_since[w] = []
        return idx

    def emit(self):
        nc, ops = self.nc, self.ops
        for o in ops:
            keep = set()
            for d in o["raw"]:
                keep.add(d)
            for d in o["other"]:
                od = ops[d]
                if od["eng"] != o["eng"] or od["dma"] is not None:
                    keep.add(d)
            o["deps"] = keep
        signal = [False] * len(ops)
        for o in ops:
            for d in o["deps"]:
                signal[d] = True
        eng_sem = {e: nc.alloc_semaphore(f"S_{e}") for e in ENGS}
        dma_keys = sorted({o["dma"] for o in ops if o["dma"] is not None})
        dma_sem = {k: nc.alloc_semaphore(f"D_{k}") for k in dma_keys}
        cnt = {e: 0 for e in ENGS}
        dcnt = {k: 0 for k in dma_keys}
        for i, o in enumerate(ops):
            if o["dma"] is not None:
                dcnt[o["dma"]] += 16
                o["sig"] = ("d", o["dma"], dcnt[o["dma"]])
            elif signal[i]:
                cnt[o["eng"]] += 1
                o["sig"] = ("e", o["eng"], cnt[o["eng"]])
            else:
                o["sig"] = None
        per_eng = {e: [i for i, o in enumerate(ops) if o["eng"] == e] for e in ENGS}

        def run_engine(ename, eng):
            waited = {}
            for i in per_eng[ename]:
                o = ops[i]
                need = {}
                for d in o["deps"]:
                    kind, key, val = ops[d]["sig"]
                    if kind == "e" and key == ename:
                        if d not in o["raw"]:
                            continue
                    k = (kind, key)
                    if need.get(k, 0) < val:
                        need[k] = val
                for k, val in need.items():
                    if waited.get(k, 0) >= val:
                        continue
                    eng.wait_ge(eng_sem[k[1]] if k[0] == "e" else dma_sem[k[1]], val)
                    waited[k] = val
                ins = o["fn"](eng)
                sv = o["sig"]
                if sv is not None:
                    if sv[0] == "d":
                        ins.then_inc(dma_sem[sv[1]], 16)
                    else:
                        ins.then_inc(eng_sem[sv[1]], 1)
            if ename == "sp":
                for k, v in dcnt.items():
                    if v > 0 and waited.get(("d", k), 0) < v:
                        eng.wait_ge(dma_sem[k], v)

        with nc.Block() as block:
            @block.tensor
            def _(e):
                run_engine("pe", e)

            @block.scalar
            def _(e):
                run_engine("act", e)

            @block.vector
            def _(e):
                run_engine("dve", e)

            @block.gpsimd
            def _(e):
                run_engine("pool", e)

            @block.sync
            def _(e):
                run_engine("sp", e)
        return {e: len(per_eng[e]) for e in ENGS}


def wall_index():
    idx, n = {}, 0
    for u in range(2):
        for g in range(3):
            for nm in ("q1", "q2", "k1", "k2", "va", "vb"):
                idx[(nm, g, u)] = n
                n += 1
    for s in range(8):
        for nm in ("h", "c", "b"):
            idx[(nm, s)] = n
            n += 1
    for m in range(8):
        idx[("ga", m)] = n; n += 1
        idx[("gc", m)] = n; n += 1
        if m % 2 == 0:
            idx[("ao", m // 2)] = n; n += 1
        idx[("co", m)] = n; n += 1
    for m in range(8):
        idx[("mo", m)] = n; n += 1
    for f in range(32):
        idx[("w1", f)] = n; n += 1
    for f in range(32):
        idx[("w2", f)] = n; n += 1
    return idx, n


WIDX, NWT = wall_index()


G_ORDER = (2, 1, 0)


def chunk_tile_order():
    o = []
    for u in range(2):
        for g in G_ORDER:
            for nm in ("q1", "q2", "k1", "k2", "va", "vb"):
                o.append((nm, g, u))
    for s in range(8):
        for nm in ("h", "c", "b"):
            o.append((nm, s))
    for m in range(8):
        o.append(("ga", m)); o.append(("gc", m))
        if m % 2 == 0:
            o.append(("ao", m // 2))
        o.append(("co", m))
    for m in range(8):
        o.append(("mo", m))
    for blk in range(2):
        for f in range(32):
            o.append(("w1", f))
        for f in range(32):
            o.append(("w2", f))
    return o


class WRing:
    def __init__(self, P, wall_ap, slots, plan):
        self.P, self.wall, self.slots = P, wall_ap, slots
        self.plan = list(plan)
        self.pos = 0
        self.loaded = {}
        self.inuse = 0
        self.free = list(range(len(slots)))
        self.R = len(slots)

    def _load_next(self):
        key = self.plan[self.pos]
        self.pos += 1
        s = self.free.pop(0)
        t = WIDX[key]
        dst = self.slots[s]
        self.P.op("pool", lambda e, dst=dst, t=t: e.dma_start(out=dst[:, :], in_=self.wall[t, :, :]),
                  writes=[f"ring{s}"], dma=f"ring{s}")
        self.loaded[key] = s
        self.inuse += 1

    def pump(self):
        while self.pos < len(self.plan) and self.inuse < self.R:
            self._load_next()

    def get(self, key):
        self.pump()
        guard = 0
        while key not in self.loaded:
            assert self.pos < len(self.plan), key
            self._load_next()
            guard += 1
            assert guard < 1000
        s = self.loaded[key]
        return self.slots[s], f"ring{s}"

    def release(self, key):
        s = self.loaded.pop(key)
        self.free.append(s)
        self.inuse -= 1
        self.pump()


def build_program():
    nc = bass.Bass("TRN2", target_bir_lowering=False)
    dt = nc.dram_tensor
    xext = dt("xext", [NCH, EXT, D], F32, kind="ExternalInput").ap()
    cT_d = dt("cT", [128, 8, NCH], F32, kind="ExternalInput").ap()
    cs_d = dt("cs", [NCH, 128, 2, EXT], F32, kind="ExternalInput").ap()
    kb_d = dt("kbias", [NCH, 128, NKT], F32, kind="ExternalInput").ap()
    uv_d = dt("uval", [NCH, 128, 2], F32, kind="ExternalInput").ap()
    wall = dt("wall", [NWT, 128, 1024], F32, kind="ExternalInput").ap()
    wada = dt("wada", [48, 128, 1024], F32, kind="ExternalInput").ap()
    sp_d = dt("smallp", [128, 48 + 8 + 8 + 24 + 8], F32, kind="ExternalInput").ap()
    gfb_d = dt("gfb", [128, 1024], F32, kind="ExternalInput").ap()
    cst_d = dt("cst", [128, 5 * 256 + 128 + 128], F32, kind="ExternalInput").ap()
    y_d = dt("y", [NCH, T, D], F32, kind="ExternalOutput").ap()

    cur = [17408]

    def A(name, shape, dtype, at=None):
        esz = 4 if dtype == F32 else 2
        n = 1
        for s in shape[1:]:
            n *= s
        nbytes = n * esz
        if at is None:
            off = cur[0]
            cur[0] += (nbytes + 63) // 64 * 64
        else:
            off = at
        assert off + nbytes <= 229376, (name, off, nbytes)
        return nc.alloc_sbuf_tensor_at(name, list(shape), dtype, offset=off)

    ring = [A(f"ring{i}", [128, 1024], BF16) for i in range(RING)]
    xnT = A("xnT", [128, 8, EXT], BF16)
    xnT_off = cur[0] - 8 * EXT * 2
    attT = A("attT", [128, 4, T], BF16)
    gtb = A("gtb", [128, 2, D], F32)
    gfb = A("gfb_sb", [128, D], F32)
    smallp = A("smallp_sb", [128, 96], F32)
    cst_bf = A("cst_bf", [128, 5 * 256 + 128 + 128], BF16)
    ident_f = A("ident_f", [128, 128], F32)
    ones_f = A("ones_f", [128, 128], F32)
    dg = A("dg", [128, 2, 128], F32)
    cT = A("cT_sb", [128, 8, NCH], F32)
    scT = A("scT", [128, 8, NCH], BF16)
    modT = A("modT", [128, 48, NCH], F32)
    a1T = A("a1T", [128, 8, NCH], F32)
    a2T = A("a2T", [128, 8, NCH], F32)
    tmp6 = A("tmp6", [128, 8, NCH], F32)
    kbias = A("kbias_sb", [128, NKT], F32)
    uval = A("uval_sb", [128, 2], F32)
    ss = A("ss", [128, 48], F32)
    rs = A("rs", [128, 48], F32)
    epsc = A("epsc", [128, 16], F32)
    xinC2 = A("xinC2", [128, D], F32)
    R1 = cur[0]
    R1_SIZE = 104 * 1024
    assert R1 + R1_SIZE <= 229376, R1
    NWA = 12
    wada_sb = [A(f"wada{i}", [128, 8, 128], BF16, at=R1 + i * 2048) for i in range(NWA)]
    o = R1
    cs_sb = A("cs_sb", [128, 2, EXT], F32, at=o); o += 24 * 1024
    qpad = [A(f"qpad{i}", [128, 2, T], BF16, at=o + i * 4096) for i in range(2)]; o += 8 * 1024
    kpair = [A(f"kpair{i}", [128, EXT], BF16, at=o + i * 6144) for i in range(2)]; o += 12 * 1024
    vbuf = A("vbuf", [128, 32, 2, 192], BF16, at=o); o += 24 * 1024
    rt = [A(f"rt{i}", [128, 512], F32, at=o + i * 2048) for i in range(4)]
    rtw = A("rtw", [128, 1024], F32, at=o); o += 8 * 1024
    acc = A("acc", [128, 2, 2, T], F32, at=o); o += 16 * 1024
    ptb = [A(f"pt{i}", [128, 2, 2, 128], BF16, at=o + i * 1024) for i in range(4)]; o += 4 * 1024
    qpad2 = [A(f"qpadB{i}", [128, 2, T], BF16, at=o + i * 4096) for i in range(2)]; o += 8 * 1024
    assert o <= R1 + R1_SIZE
    xinC = [A(f"xinC{i}", [128, D], F32, at=R1 + 96 * 1024 + i * 4096) for i in range(2)] + [xinC2]
    xsC = A("xsC", [128, 4, D], BF16, at=xnT_off + 8 * EXT * 2)
    x1 = A("x1", [128, 8, D], F32, at=R1)
    xn2T = A("xn2T", [128, 8, T], BF16, at=R1 + 32 * 1024)
    mergedT = A("mergedT", [128, 8, T], BF16, at=R1 + 48 * 1024)
    cvinT = A("cvinT", [128, 8, T], BF16, at=R1)
    ubuf = A("ubuf", [128, 1032], F32, at=R1 + 16 * 1024)
    ctmp = [A(f"ctmp{i}", [128, 512], F32, at=R1 + 16 * 1024 + 4224 + i * 2048) for i in range(1)]
    csb = A("csb", [128, 512], F32, at=R1 + 16 * 1024 + 4224 + 2048)
    hT = A("hT", [128, 32, 512], BF16, at=R1 + 48 * 1024)
    sgt = [A(f"sgt{i}", [128, 512], F32, at=R1 + 80 * 1024 + i * 2048) for i in range(4)]
    xinB = [A(f"xinB{i}", [128, D], F32, at=xnT_off + i * 4096) for i in range(4)]
    xsB = A("xsB", [128, 4, D], BF16, at=xnT_off + 16 * 1024)
    x2s = [A(f"x2s{i}", [128, D], F32, at=R1 + 88 * 1024 + i * 4096) for i in range(2)]

    banks = [nc.alloc_psum_tensor(f"bank{i}", [128, 512], F32) for i in range(8)]
    banks_bf = [b.bitcast(BF16) for b in banks]

    P = Prog(nc)
    R1_ATT = ["cs0", "cs1", "cs2", "qpad0", "qpad1", "qpadB0", "qpadB1", "kpair0", "kpair1", "vbuf", "rt0", "rt1", "rt2", "rt3", "acc",
              "pt0a", "pt1a", "pt2a", "pt3a", "pt0b", "pt1b", "pt2b", "pt3b"]
    R1_POST = ["x1", "xn2T0", "xn2T1", "mergedT", "cvinT", "ubuf", "ctmp0", "csb", "hT", "sgt0", "sgt1", "sgt2", "sgt3"]
    for a in R1_ATT:
        for b in R1_POST:
            P.add_alias(a, b)
    for a in ["hT"]:
        for b in ["mergedT"]:
            P.add_alias(a, b)
    for b in ["cvinT", "ubuf", "ctmp0", "csb"]:
        P.add_alias("x1", b)
    for a in ["sgt0", "sgt1", "sgt2", "sgt3"]:
        for b in ["ubuf", "ctmp0", "csb"]:
            P.add_alias(a, b)
    for b in ["xinB0", "xinB1", "xinB2", "xinB3", "xsB0", "xsB1", "xsB2", "xsB3"]:
        P.add_alias("xnT", b)
    for i in range(NWA):
        for b in ["cs0", "cs1", "cs2", "x1", "qpad0", "qpad1"]:
            P.add_alias(f"wada{i}", b)
    for i in range(2):
        for b in ["acc", "pt0a", "pt1a", "pt2a", "pt3a", "pt0b", "pt1b", "pt2b", "pt3b"]:
            P.add_alias(f"x2s{i}", b)
        for b in ["qpadB0", "qpadB1"]:
            P.add_alias(f"xinC{i}", b)
    for i in range(4):
        P.add_alias(f"xsC{i}", "attT")

    plan = []
    for ci in range(NCH):
        plan += chunk_tile_order()
    W = WRing(P, wall, ring, plan)

    maskb = cst_bf[:, 0:1280].rearrange("p (t h q) -> p t h q", t=5, h=2)
    ident = cst_bf[:, 1280:1408]
    ones_bf = cst_bf[:, 1408:1536]
    b_adaT = smallp[:, 0:48]
    g1T = smallp[:, 48:56]
    g2T = smallp[:, 56:64]
    cwT = smallp[:, 64:88].rearrange("p (s t) -> p s t", t=3)
    cbT = smallp[:, 88:96]

    def mm(out, lhsT, rhs, start, stop, reads, writes):
        P.op("pe", lambda e: e.matmul(out, lhsT=lhsT, rhs=rhs, start=start, stop=stop), reads=reads, writes=writes)

    P.op("pool", lambda e: e.dma_start(out=cst_bf[:, :], in_=cst_d), writes=["cst"], dma="cst")
    P.op("sp", lambda e: e.dma_start(out=smallp[:, :], in_=sp_d), writes=["smallp"], dma="smallp")
    P.op("sp", lambda e: e.dma_start(out=gfb[:, :], in_=gfb_d), writes=["gfb"], dma="gfb")
    P.op("sp", lambda e: e.dma_start(out=cT[:, :, :], in_=cT_d), writes=["cT"], dma="cT")
    P.op("sp", lambda e: e.dma_start(out=ident_f[:, :], in_=cst_d[:, 1280:1408]), writes=["ident_f"], dma="ident_f")
    P.op("sp", lambda e: e.dma_start(out=ones_f[:, :], in_=cst_d[:, 1408:1536]), writes=["ones_f"], dma="ones_f")
    W.pump()
    P.op("dve", lambda e: e.memset(epsc[:, :], EPS), writes=["epsc"])
    P.op("act", lambda e: e.activation(out=scT[:, :, :], in_=cT[:, :, :], func=AF.Silu), reads=["cT"], writes=["scT"])
    def emit_a(dst, dn, which, gT):
        P.op("dve", lambda e: e.tensor_scalar(out=tmp6[:, :, :], in0=modT[:, which * 8:which * 8 + 8, :], scalar1=1.0,
                                              scalar2=None, op0=ALU.add),
             reads=[f"modT{which}"], writes=["tmp6"])
        for cj in range(NCH):
            P.op("dve", lambda e, cj=cj: e.tensor_tensor(out=dst[:, :, cj], in0=tmp6[:, :, cj], in1=gT, op=ALU.mult),
                 reads=["tmp6", "smallp"], writes=[dn])

    def adaln_et(et):
        wb = wada_sb[et % NWA]
        wn = f"wada{et % NWA}"
        P.op("pool", lambda e: e.dma_start(out=wb[:, :, :], in_=wada[et].rearrange("p (k c) -> p k c", k=8)),
             writes=[wn], dma=wn)
        bk = banks[et % 2]
        for k in range(8):
            mm(bk[:, 0:NCH], wb[:, k, :], scT[:, k, :], k == 0, k == 7, [wn, "scT"], [f"bank{et % 2}"])
        P.op("dve", lambda e: e.tensor_scalar(out=modT[:, et, :], in0=bk[:, 0:NCH], scalar1=b_adaT[:, et:et + 1],
                                              scalar2=None, op0=ALU.add),
             reads=[f"bank{et % 2}", "smallp"], writes=[f"modT{et // 8}"])
        if et == 15:
            emit_a(a1T, "a1T", 1, g1T)
        if et == 39:
            emit_a(a2T, "a2T", 4, g2T)

    P.op("dve", lambda e: e.memset(ss[:, :], 0.0), writes=[f"ssc{c_}" for c_ in range(48)])
    for et in range(16):
        adaln_et(et)

    def norm_to_T(ci, ntiles, src_rows, xin, xin_names, xs, xs_name, ss_base, dstT, dst_name_, aT, sh_which, dst_col0,
                  src_is_dram=True, src_sb=None, src_sb_name=None, act_share=2, dst_names=None, sq_dve=False, tr_lag=1):
        def load(tt):
            if src_is_dram and tt < ntiles:
                xt = xin[tt % len(xin)]
                xname = xin_names[tt % len(xin)]
                r0 = src_rows + tt * 128
                P.op("sp", lambda e, xt=xt, r0=r0: e.dma_start(out=xt[:, :], in_=xext[ci, r0:r0 + 128, :]),
                     writes=[xname], dma=xname)

        def stage1(tt):
            tl = tt % 4
            if src_is_dram:
                xt = xin[tt % len(xin)]
                xname = xin_names[tt % len(xin)]
                src = xt[:, :]
            else:
                src = src_sb[:, tt, :]
                xname = src_sb_name + str(tt)
            col = ss_base + tt
            if sq_dve and tt % 2 == 1:
                P.op("dve", lambda e, src=src, col=col, tl=tl: e.scalar_tensor_tensor(out=xs[:, tl, :], in0=src, scalar=1.0, in1=src,
                                                                                    op0=ALU.mult, op1=ALU.mult, accum_out=ss[:, col:col + 1]),
                     reads=[xname, f"ssc{col}"], writes=[f"ssc{col}", xs_name + str(tl)])
            else:
                P.op("act", lambda e, src=src, col=col, tl=tl: e.activation(out=xs[:, tl, :], in_=src, func=AF.Square,
                                                                            accum_out=ss[:, col:col + 1]),
                     reads=[xname, f"ssc{col}"], writes=[f"ssc{col}", xs_name + str(tl)])
            P.op("act", lambda e, col=col: e.activation(out=rs[:, col:col + 1], in_=ss[:, col:col + 1], func=AF.Sqrt,
                                                        bias=epsc[:, 0:1], scale=1.0 / D),
                 reads=[f"ssc{col}", "epsc"], writes=[f"rsc{col}"])
            P.op("dve", lambda e, col=col: e.reciprocal(out=rs[:, col:col + 1], in_=rs[:, col:col + 1]),
                 reads=[f"rsc{col}"], writes=[f"rsc{col}"])
            return src, xname, col

        def stage2(tt, src, xname, col):
            tl = tt % 4
            P.op("act", lambda e, src=src, col=col, tl=tl: e.activation(out=xs[:, tl, :], in_=src, func=AF.Identity,
                                                                        scale=rs[:, col:col + 1]),
                 reads=[xname, f"rsc{col}"], writes=[xs_name + str(tl)])

        def transposes(tt):
            tl = tt % 4
            g_ = tt // 2
            b0 = 4 + 2 * (g_ % 2)
            for k in range(8):
                bi = b0 + k // 4
                c0 = (k % 4) * 256 + (tt % 2) * 128
                P.op("pe", lambda e, bi=bi, c0=c0, tl=tl, k=k: e.transpose(banks_bf[bi][:, c0:c0 + 128],
                                                                             xs[:, tl, k * 128:(k + 1) * 128], ident),
                     reads=[xs_name + str(tl), "cst"], writes=[f"bank{bi}"])

        def evac_one(g_, k):
            e0 = dst_col0 + g_ * 256
            dst_name = dst_names[g_] if dst_names is not None else dst_name_
            bi = 4 + 2 * (g_ % 2) + k // 4
            c0 = (k % 4) * 256
            sh = modT[:, sh_which * 8 + k, ci:ci + 1]
            sc = aT[:, k, ci:ci + 1]
            aT_name = "a1T" if sh_which == 0 else "a2T"
            if act_share == 0 or k % act_share != act_share - 1:
                P.op("dve", lambda e: e.tensor_scalar(
                    out=dstT[:, k, e0:e0 + 256], in0=banks_bf[bi][:, c0:c0 + 256], scalar1=sc, scalar2=sh,
                    op0=ALU.mult, op1=ALU.add), reads=[f"bank{bi}", f"modT{sh_which}", aT_name], writes=[dst_name])
            else:
                P.op("act", lambda e: e.activation(
                    out=dstT[:, k, e0:e0 + 256], in_=banks_bf[bi][:, c0:c0 + 256], func=AF.Identity, bias=sh, scale=sc),
                    reads=[f"bank{bi}", f"modT{sh_which}", aT_name], writes=[dst_name])

        class NormPipe:
            def __init__(self):
                self.n_load = self.n_s1 = self.n_s2 = self.n_tr = 0
                self.info = {}
                self.evq = []
                self.hold = False
                self.groups_done = 0
                self.nsl = len(xin) if src_is_dram else 10 ** 6
                self.avail = ntiles

            @property
            def done(self):
                return self.n_tr == ntiles and not self.evq

            @property
            def clean(self):
                return (not self.evq) and self.n_tr % 2 == 0

            def step(self):
                if src_is_dram:
                    while self.n_load < ntiles and self.n_load <= self.n_s1 + 1 and self.n_load - self.n_s2 < self.nsl:
                        load(self.n_load)
                        self.n_load += 1
                if self.n_s1 < min(ntiles, self.avail) and (not src_is_dram or self.n_s1 < self.n_load) and self.n_s1 - self.n_tr < 4:
                    self.info[self.n_s1] = stage1(self.n_s1)
                    self.n_s1 += 1
                if self.n_s2 < self.n_s1 - 1 or (self.n_s1 == ntiles and self.n_s2 < ntiles):
                    stage2(self.n_s2, *self.info[self.n_s2])
                    self.n_s2 += 1
                for _ in range(4):
                    if self.evq:
                        g_, k = self.evq.pop(0)
                        evac_one(g_, k)
                        if k == 7:
                            self.groups_done += 1
                can_tr = self.n_tr < self.n_s2 - tr_lag or (self.n_s2 == ntiles and self.n_tr < ntiles)
                if can_tr and not (self.hold and self.n_tr % 2 == 0):
                    g_ = self.n_tr // 2
                    if all(q[0] != g_ - 2 for q in self.evq):
                        transposes(self.n_tr)
                        self.n_tr += 1
                        if self.n_tr % 2 == 0:
                            self.evq += [(g_, k) for k in range(8)]

            def run(self, nsteps=None, until=None):
                c_ = 0
                while not self.done and (nsteps is None or c_ < nsteps) and not (until is not None and until()):
                    self.step()
                    c_ += 1

        return NormPipe()

    for ci in range(NCH):
        P.op("sp", lambda e, ci=ci: e.dma_start(out=kbias[:, :], in_=kb_d[ci]), writes=["kbias"], dma="kbias")
        P.op("sp", lambda e, ci=ci: e.dma_start(out=uval[:, :], in_=uv_d[ci]), writes=["uval"], dma="uval")
        def emit_gt(wi, cj, bi):
            which = (2, 5)[wi]
            for k in range(8):
                sl = k % 2
                P.op("dve", lambda e, k=k, sl=sl: e.tensor_scalar(
                    out=dg[:, sl, :], in0=ident_f[:, :], scalar1=modT[:, which * 8 + k, cj:cj + 1], scalar2=None, op0=ALU.mult),
                    reads=["ident_f", f"modT{which}"], writes=[f"dg{sl}"])
                mm(banks[bi][:, (k % 4) * 128:(k % 4) * 128 + 128], ones_f[:, :], dg[:, sl, :], True, True,
                   ["ones_f", f"dg{sl}"], [f"bank{bi}"])
                if k % 4 == 3:
                    h0 = (k // 4) * 512
                    P.op("act", lambda e, h0=h0: e.copy(out=gtb[:, wi, h0:h0 + 512], in_=banks[bi][:, :]),
                         reads=[f"bank{bi}"], writes=[f"gtb{wi}"])
        if ci == 0:
            n1p = norm_to_T(0, EXT // 128, 0, xinC, ["xinC0", "xinC1", "xinC2"], xsC, "xsC", 0, xnT, "xnT", a1T, 0, 0, act_share=0)
            et_ = 16
            while not n1p.done or et_ < 48:
                if not n1p.done:
                    n1p.step()
                if et_ < 48:
                    adaln_et(et_)
                    et_ += 1
            emit_gt(0, 0, 2)
        if n1p is not None:
            n1p.hold = False
            n1p.run()
        P.op("dve", lambda e: e.memset(ss[:, :], 0.0), writes=[f"ssc{c_}" for c_ in range(48)])
        for th_ in (1, 0, 2):
            P.op("sp", lambda e, ci=ci, th_=th_: e.dma_start(out=cs_sb[:, :, th_ * 1024:(th_ + 1) * 1024],
                                                             in_=cs_d[ci][:, :, th_ * 1024:(th_ + 1) * 1024]),
                 writes=[f"cs{th_}"], dma=f"cs{th_}")
        for i in range(2):
            P.op("pool", lambda e, i=i: e.memset(qpad[i][:, :, :], 0.0), writes=[f"qpad{i}"])
        P.op("pool", lambda e: e.memset(vbuf[:, :, :, 64:128], 1.0), writes=["vbuf"])

        def proj_rope(key1, key2, blocks, place):
            w1t, w1n = W.get(key1)
            w2t, w2n = W.get(key2)
            w1v = w1t[:, :].rearrange("p (k c) -> p k c", k=8)
            w2v = w2t[:, :].rearrange("p (k c) -> p k c", k=8)
            for bi_, (e0, wd) in enumerate(blocks):
                b1, b2 = (0, 1) if bi_ % 2 == 0 else (2, 3)
                for k in range(8):
                    mm(banks[b1][:, 0:wd], w1v[:, k, :], xnT[:, k, e0:e0 + wd], k == 0, k == 7, [w1n, "xnT"], [f"bank{b1}"])
                for k in range(8):
                    mm(banks[b2][:, 0:wd], w2v[:, k, :], xnT[:, k, e0:e0 + wd], k == 0, k == 7, [w2n, "xnT"], [f"bank{b2}"])
                cos = cs_sb[:, 0, e0:e0 + wd]
                sin = cs_sb[:, 1, e0:e0 + wd]
                for (ti, bk, tab) in ((0, b1, cos), (1, b2, sin), (2, b2, cos), (3, b1, sin)):
                    P.op("dve", lambda e, ti=ti, bk=bk, tab=tab, wd=wd: e.tensor_tensor(out=rt[ti][:, 0:wd], in0=banks[bk][:, 0:wd],
                                                                                        in1=tab, op=ALU.mult),
                         reads=[f"bank{bk}"] + [f"cs{t_}" for t_ in range(e0 // 1024, (e0 + wd - 1) // 1024 + 1)], writes=[f"rt{ti}"])
                place(e0, wd)
                yield bi_
            W.release(key1)
            W.release(key2)

        def place_q(e0, wd):
            t0 = e0 - HALO
            for j in range(4):
                pi, var = j // 2, j % 2
                r = slice(32 * j, 32 * j + 32)
                d1 = slice(32 * var, 32 * var + 32)
                d2 = slice(64 + 32 * var, 64 + 32 * var + 32)
                P.op("dve", lambda e, pi=pi, var=var, r=r, d1=d1: e.tensor_tensor(
                    out=qpad[pi][d1, var, t0:t0 + wd], in0=rt[0][r, 0:wd], in1=rt[1][r, 0:wd], op=ALU.subtract),
                    reads=["rt0", "rt1"], writes=[f"qpad{pi}"])
                P.op("dve", lambda e, pi=pi, var=var, r=r, d2=d2: e.tensor_tensor(
                    out=qpad[pi][d2, var, t0:t0 + wd], in0=rt[2][r, 0:wd], in1=rt[3][r, 0:wd], op=ALU.add),
                    reads=["rt2", "rt3"], writes=[f"qpad{pi}"])

        def place_k(e0, wd):
            for pi in range(2):
                r = slice(64 * pi, 64 * pi + 64)
                P.op("dve", lambda e, pi=pi, r=r: e.tensor_tensor(out=kpair[pi][0:64, e0:e0 + wd], in0=rt[0][r, 0:wd],
                                                                  in1=rt[1][r, 0:wd], op=ALU.subtract),
                     reads=["rt0", "rt1"], writes=[f"kpair{pi}"])
                P.op("dve", lambda e, pi=pi, r=r: e.tensor_tensor(out=kpair[pi][64:128, e0:e0 + wd], in0=rt[2][r, 0:wd],
                                                                  in1=rt[3][r, 0:wd], op=ALU.add),
                     reads=["rt2", "rt3"], writes=[f"kpair{pi}"])

        def key_tiles(g):
            def strided(es, st):
                return lambda v: v[:, es:es + 127 * st + 1:st]
            if g == 0:
                return [strided(960 + 128 * kt, 1) for kt in range(9)], 0
            if g == 1:
                return [strided(768 + 512 * kk + r, 4) for r in range(4) for kk in range(3)], 9
            out = []
            for cp in range(8):
                out.append(strided(cp, 16))
                out.append(strided(cp + 8, 16))
                out.append(strided(2048 + cp, 8))
            return out, 21

        def fin_att(u):
            for pi in range(2):
                P.op("act", lambda e, pi=pi: e.activation(out=rtw[0:64, :], in_=acc[64:128, pi, 0, :], func=AF.Ln),
                     reads=["acc"], writes=["rt0", "rt1"])
                P.op("act", lambda e, pi=pi: e.activation(out=rtw[64:128, :], in_=acc[0:64, pi, 1, :], func=AF.Ln),
                     reads=["acc"], writes=["rt0", "rt1"])
                P.op("act", lambda e: e.activation(out=rtw[:, :], in_=rtw[:, :], func=AF.Exp, scale=-1.0),
                     reads=["rt0", "rt1"], writes=["rt0", "rt1"])
                P.op("dve", lambda e, pi=pi, u=u: e.tensor_tensor(out=attT[0:64, 2 * u + pi, :], in0=acc[0:64, pi, 0, :], in1=rtw[0:64, :],
                                                                 op=ALU.mult), reads=["acc", "rt0", "rt1"], writes=["attT"])
                P.op("dve", lambda e, pi=pi, u=u: e.tensor_tensor(out=attT[64:128, 2 * u + pi, :], in0=acc[64:128, pi, 1, :], in1=rtw[64:128, :],
                                                                 op=ALU.mult), reads=["acc", "rt0", "rt1"], writes=["attT"])

        pend_fin = [None]
        for u in range(2):
            for gi, g in enumerate(G_ORDER):
                kblocks = {0: [(960, 512), (1472, 512), (1984, 128)], 1: [(768 + 512 * i, 512) for i in range(3)],
                           2: [(512 * i, 512) for i in range(6)]}[g]
                ktl, kb0 = key_tiles(g)
                for _ in proj_rope(("q1", g, u), ("q2", g, u), [(HALO, 512), (HALO + 512, 512)], place_q):
                    pass
                kgen = proj_rope(("k1", g, u), ("k2", g, u), kblocks, place_k)
                wv = []
                for nm_ in ("va", "vb"):
                    t_, n_ = W.get((nm_, g, u))
                    wv.append((t_[:, :].rearrange("p (k c) -> p k c", k=4), n_))
                vgroups = list(range(0, len(ktl), 2))
                per_blk = max(1, len(vgroups) // (len(kblocks) + 1))
                vi = 0
                kdone = False
                while vi < len(vgroups) or not kdone:
                    for _ in range(per_blk):
                        if vi >= len(vgroups):
                            break
                        kt0 = vgroups[vi]
                        vi += 1
                        bi = 4 + (kt0 // 2) % 2
                        nk = min(2, len(ktl) - kt0)
                        for kl in range(nk):
                            kap = ktl[kt0 + kl]
                            c0 = kl * 256
                            for k in range(8):
                                mm(banks[bi][:, c0:c0 + 256], kap(xnT[:, k, :]), wv[k // 4][0][:, k % 4, :], k == 0, k == 7,
                                   [wv[k // 4][1], "xnT"], [f"bank{bi}"])
                        for hh in range(2):
                            P.op("act", lambda e, bi=bi, kt0=kt0, nk=nk, hh=hh: e.copy(
                                out=vbuf[:, kt0:kt0 + nk, :, hh * 128:hh * 128 + 64],
                                in_=banks[bi][:, 0:256 * nk].rearrange("p (a b h d) -> p a b h d", a=nk, b=2, h=2)[:, :, :, hh, :]),
                                reads=[f"bank{bi}"], writes=["vbuf"])
                    if not kdone:
                        try:
                            next(kgen)
                        except StopIteration:
                            kdone = True
                W.release(("va", g, u))
                W.release(("vb", g, u))
                if pend_fin[0] is not None:
                    fin_att(pend_fin[0])
                    pend_fin[0] = None

                if g == 0:
                    qblocks = [[(128 * qt, 1, 128, qt, qt + 1, 0, 1) for qt in (2 * b, 2 * b + 1)] for b in range(4)]
                elif g == 1:
                    qblocks = [[(4 * i0 + r, 4, 128, r * 3 + i0 // 128, r * 3 + i0 // 128 + 1, 0, 1) for i0 in (0, 128)]
                               for r in range(4)]
                else:
                    qblocks = [[(r, 16, 64, (r % 8) * 3 + r // 8, (r % 8) * 3 + 2, 0, 3 + r // 8) for r in range(4 * b, 4 * b + 4)]
                               for b in range(4)]
                steps = []
                for qb_i, qb in enumerate(qblocks):
                    for pi in range(2):
                        for qi, qt_ in enumerate(qb):
                            steps.append((qb_i, pi, qi, qt_, qi == len(qb) - 1, qb))
                LA = 3

                def emit_qk(n):
                    qb_i, pi, qi, (qs, qst, nq, ta, tb, ma, mb), last, qb = steps[n]
                    sb_i = (0, 1, 2, 6)[n % 4]
                    pt = ptb[n % 4]
                    ptn = f"pt{n % 4}"
                    S = banks[sb_i]
                    qap = qpad[pi][:, :, qs:qs + (nq - 1) * qst + 1:qst]
                    sos = []
                    for ti, (tid, mtype) in enumerate(((ta, ma), (tb, mb))):
                        kap = ktl[tid]
                        so = S[:, ti * 2 * nq:(ti + 1) * 2 * nq].rearrange("p (h q) -> p h q", h=2)
                        sos.append(so)
                        mm(so, ident, maskb[:, mtype, :, 0:nq], True, False, ["cst"], [f"bank{sb_i}"])
                        mm(so, kap(kpair[pi][:, :]), qap, False, True, [f"kpair{pi}", f"qpad{pi}"], [f"bank{sb_i}"])
                    for ti, (tid, mtype) in enumerate(((ta, ma), (tb, mb))):
                        so = sos[ti]
                        kcol = kb0 + tid
                        P.op("act", lambda e, so=so, pt=pt, ti=ti, nq=nq, kcol=kcol: e.activation(
                            out=pt[:, ti, :, 0:nq], in_=so, func=AF.Exp,
                            bias=kbias[:, kcol:kcol + 1], scale=0.125),
                            reads=[f"bank{sb_i}", "kbias"], writes=[ptn + "ab"[ti]])

                def emit_pv(n):
                    qb_i, pi, qi, (qs, qst, nq, ta, tb, ma, mb), last, qb = steps[n]
                    pt = ptb[n % 4]
                    ptn = f"pt{n % 4}"
                    nset = (qb_i * 2 + pi) % 3
                    nd = 3 + nset
                    qc = qi * nq
                    for hh in range(2):
                        lA = vbuf[:, ta, pi, hh * 64:hh * 64 + 128]
                        lB = vbuf[:, tb, pi, hh * 64:hh * 64 + 128]
                        oo = banks[nd][:, hh * 256 + qc:hh * 256 + qc + nq]
                        mm(oo, lA, pt[:, 0, hh, 0:nq], True, False, ["vbuf", ptn + "a"], [f"bank{nd}"])
                        mm(oo, lB, pt[:, 1, hh, 0:nq], False, True, ["vbuf", ptn + "b"], [f"bank{nd}"])
                    if not last:
                        return
                    if g == 0:
                        q0 = qb[0][0]
                        d_ = acc[:, pi, :, q0:q0 + 256]
                        s_ = banks[nd][:, 0:512].rearrange("p (h c) -> p h c", h=2)
                    elif g == 1:
                        r = qb[0][0]
                        d_ = acc[:, pi, :, r:r + 1021:4]
                        s_ = banks[nd][:, 0:512].rearrange("p (h c) -> p h c", h=2)
                    else:
                        r0 = qb[0][0]
                        d_ = acc[:, pi, :, :].rearrange("p h (j r) -> p h r j", r=16)[:, :, r0:r0 + 4, :]
                        s_ = banks[nd][:, 0:512].rearrange("p (h r j) -> p h r j", h=2, r=4)
                    if gi == 0:
                        P.op("dve", lambda e, d_=d_, s_=s_: e.tensor_copy(out=d_, in_=s_), reads=[f"bank{nd}"], writes=["acc"])
                    else:
                        P.op("dve", lambda e, d_=d_, s_=s_: e.tensor_tensor(out=d_, in0=s_, in1=d_, op=ALU.add),
                             reads=[f"bank{nd}", "acc"], writes=["acc"])

                for n in range(len(steps) + LA):
                    if n < len(steps):
                        emit_qk(n)
                    if n - LA >= 0:
                        emit_pv(n - LA)
            pend_fin[0] = u

        emit_gt(1, ci, 6)
        cblocks = [(1023, 512), (1535, 512), (2047, 2)]
        for s in range(8):
            (wh, whn), (wc, wcn), (wbt, wbn) = W.get(("h", s)), W.get(("c", s)), W.get(("b", s))
            whv, wcv, wbv = [w_[:, :].rearrange("p (k c) -> p k c", k=8) for w_ in (wh, wc, wbt)]
            for bi_, (e0, wd) in enumerate(cblocks):
                b1, b2 = (0, 1) if bi_ % 2 == 0 else (2, 3)
                for k in range(8):
                    mm(banks[b1][:, 0:wd], whv[:, k, :], xnT[:, k, e0:e0 + wd], k == 0, k == 7, [whn, "xnT"], [f"bank{b1}"])
                for k in range(8):
                    mm(banks[b2][:, 0:wd], wcv[:, k, :], xnT[:, k, e0:e0 + wd], k == 0, k == 7, [wcn, "xnT"], [f"bank{b2}"])
                P.op("act", lambda e, b2=b2, wd=wd: e.copy(out=csb[:, 0:wd], in_=banks[b2][:, 0:wd]), reads=[f"bank{b2}"], writes=["csb"])
                uo = e0 - 1023
                P.op("dve", lambda e, b1=b1, wd=wd, uo=uo: e.tensor_tensor(out=ubuf[:, uo:uo + wd], in0=banks[b1][:, 0:wd],
                                                                            in1=csb[:, 0:wd], op=ALU.mult),
                     reads=[f"bank{b1}", "csb"], writes=["ubuf"])
            for (col, vi) in ((0, 0), (1025, 1)):
                P.op("dve", lambda e, col=col, vi=vi: e.tensor_scalar(out=ubuf[:, col:col + 1], in0=ubuf[:, col:col + 1],
                                                                     scalar1=uval[:, vi:vi + 1], scalar2=None, op0=ALU.mult),
                     reads=["ubuf", "uval"], writes=["ubuf"])
            for cb in range(2):
                t0 = cb * 512
                bb = 4 + cb
                for k in range(8):
                    mm(banks[bb][:, :], wbv[:, k, :], xnT[:, k, HALO + t0:HALO + t0 + 512], k == 0, k == 7, [wbn, "xnT"], [f"bank{bb}"])
                ct = ctmp[0]
                P.op("act", lambda e, t0=t0, s=s, ct=ct: e.activation(out=ct[:, :], in_=ubuf[:, t0 + 1:t0 + 513], func=AF.Identity,
                                                                     bias=cbT[:, s:s + 1], scale=cwT[:, s, 1:2]),
                     reads=["ubuf", "smallp"], writes=["ctmp0"])
                P.op("dve", lambda e, t0=t0, s=s, ct=ct: e.scalar_tensor_tensor(out=ct[:, :], in0=ubuf[:, t0:t0 + 512],
                                                                               scalar=cwT[:, s, 0:1], in1=ct[:, :],
                                                                               op0=ALU.mult, op1=ALU.add),
                     reads=["ubuf", "smallp", "ctmp0"], writes=["ctmp0"])
                P.op("dve", lambda e, t0=t0, s=s, ct=ct: e.scalar_tensor_tensor(out=ct[:, :], in0=ubuf[:, t0 + 2:t0 + 514],
                                                                               scalar=cwT[:, s, 2:3], in1=ct[:, :],
                                                                               op0=ALU.mult, op1=ALU.add),
                     reads=["ubuf", "smallp", "ctmp0"], writes=["ctmp0"])
                P.op("dve", lambda e, t0=t0, s=s, ct=ct, bb=bb: e.tensor_tensor(out=cvinT[:, s, t0:t0 + 512], in0=banks[bb][:, :],
                                                                               in1=ct[:, :], op=ALU.mult),
                     reads=[f"bank{bb}", "ctmp0"], writes=["cvinT"])
            for nm in ("h", "c", "b"):
                W.release((nm, s))
            if pend_fin[0] is not None:
                fin_att(pend_fin[0])
                pend_fin[0] = None

        aot = None
        for m in range(8):
            (wga, wgan), (wgc, wgcn) = W.get(("ga", m)), W.get(("gc", m))
            if m % 2 == 0:
                aot = W.get(("ao", m // 2))
            (wco, wcon) = W.get(("co", m))
            wgav, wgcv, wcov = [w_[:, :].rearrange("p (k c) -> p k c", k=8) for w_ in (wga, wgc, wco)]
            waov = aot[0][:, :].rearrange("p (k c) -> p k c", k=8)
            for blk in range(2):
                t0 = blk * 512
                bs = 0 if (m * 2 + blk) % 2 == 0 else 4
                for k in range(8):
                    mm(banks[bs][:, :], wgav[:, k, :], xnT[:, k, HALO + t0:HALO + t0 + 512], k == 0, k == 7, [wgan, "xnT"], [f"bank{bs}"])
                for k in range(8):
                    mm(banks[bs + 1][:, :], wgcv[:, k, :], xnT[:, k, HALO + t0:HALO + t0 + 512], k == 0, k == 7, [wgcn, "xnT"],
                       [f"bank{bs + 1}"])
                for pi in range(4):
                    mm(banks[bs + 2][:, :], waov[:, (m % 2) * 4 + pi, :], attT[:, pi, t0:t0 + 512], pi == 0, pi == 3, [aot[1], "attT"],
                       [f"bank{bs + 2}"])
                for s in range(8):
                    mm(banks[bs + 3][:, :], wcov[:, s, :], cvinT[:, s, t0:t0 + 512], s == 0, s == 7, [wcon, "cvinT"], [f"bank{bs + 3}"])
                P.op("act", lambda e, bs=bs: e.activation(out=sgt[0][:, :], in_=banks[bs][:, :], func=AF.Sigmoid),
                     reads=[f"bank{bs}"], writes=["sgt0"])
                P.op("act", lambda e, bs=bs: e.activation(out=sgt[1][:, :], in_=banks[bs + 1][:, :], func=AF.Sigmoid),
                     reads=[f"bank{bs + 1}"], writes=["sgt1"])
                P.op("dve", lambda e, bs=bs: e.tensor_tensor(out=sgt[0][:, :], in0=banks[bs + 2][:, :], in1=sgt[0][:, :], op=ALU.mult),
                     reads=[f"bank{bs + 2}", "sgt0"], writes=["sgt0"])
                P.op("dve", lambda e, bs=bs: e.tensor_tensor(out=sgt[1][:, :], in0=banks[bs + 3][:, :], in1=sgt[1][:, :], op=ALU.mult),
                     reads=[f"bank{bs + 3}", "sgt1"], writes=["sgt1"])
                P.op("dve", lambda e, m=m, t0=t0: e.tensor_tensor(out=mergedT[:, m, t0:t0 + 512], in0=sgt[0][:, :], in1=sgt[1][:, :],
                                                                  op=ALU.add),
                     reads=["sgt0", "sgt1"], writes=["mergedT"])
            W.release(("ga", m)); W.release(("gc", m)); W.release(("co", m))
            if m % 2 == 1:
                W.release(("ao", m // 2))

        wmo = [W.get(("mo", m)) for m in range(8)]
        n2p = norm_to_T(ci, 8, 0, None, None, xsB, "xsB", 24, xn2T, None, a2T, 3, 0, src_is_dram=False, src_sb=x1, src_sb_name="x1_",
                        dst_names=["xn2T0", "xn2T0", "xn2T1", "xn2T1"])
        n2p.avail = 0
        for tt in range(8):
            xt = xinB[tt % 4]
            xname = f"xinB{tt % 4}"
            P.op("sp", lambda e, xt=xt, tt=tt, ci=ci: e.dma_start(out=xt[:, :], in_=xext[ci, HALO + tt * 128:HALO + tt * 128 + 128, :]),
                 writes=[xname], dma=xname)
            for half in range(2):
                bk = (tt % 2) * 2 + half
                for m in range(8):
                    mm(banks[bk][:, :], mergedT[:, m, tt * 128:(tt + 1) * 128], wmo[m][0][:, half * 512:(half + 1) * 512], m == 0, m == 7,
                       [wmo[m][1], "mergedT"], [f"bank{bk}"])
                hs = slice(half * 512, half * 512 + 512)
                P.op("dve", lambda e, bk=bk, tt=tt, hs=hs: e.tensor_tensor(out=x1[:, tt, hs], in0=banks[bk][:, :], in1=gtb[:, 0, hs],
                                                                          op=ALU.mult),
                     reads=[f"bank{bk}", "gtb0"], writes=[f"x1_{tt}", "x1"])
                P.op("dve", lambda e, xt=xt, tt=tt, hs=hs: e.tensor_tensor(out=x1[:, tt, hs], in0=x1[:, tt, hs], in1=xt[:, hs], op=ALU.add),
                     reads=[f"x1_{tt}", xname], writes=[f"x1_{tt}", "x1"])
            n2p.avail = tt + 1
            n2p.step()
        for m in range(8):
            W.release(("mo", m))
        n2p.run(until=lambda: n2p.groups_done >= 2)
        if ci + 1 < NCH:
            n1p = norm_to_T(ci + 1, EXT // 128, 0, xinC, ["xinC0", "xinC1", "xinC2"], xsC, "xsC", 0, xnT, "xnT", a1T, 0, 0, act_share=0,
                            sq_dve=True, tr_lag=2)
        else:
            n1p = None
        n1_done = [0]

        def n1_steps(target):
            while n1p is not None and not n1p.done and n1_done[0] < target:
                n1p.step()
                n1_done[0] += 1

        def pipes_to_clean():
            for p_ in (n2p, n1p):
                if p_ is not None:
                    p_.hold = True
                    p_.run(until=lambda: p_.clean)

        def pipes_release():
            for p_ in (n2p, n1p):
                if p_ is not None:
                    p_.hold = False

        def fin_A(blk, tl):
            tt = blk * 4 + tl
            xo, xon = x2s[tl % 2], f"x2s{tl % 2}"
            for half in range(2):
                bk = tl * 2 + half
                hs = slice(half * 512, half * 512 + 512)
                P.op("dve", lambda e, bk=bk, xo=xo, hs=hs: e.tensor_tensor(out=xo[:, hs], in0=banks[bk][:, :], in1=gtb[:, 1, hs],
                                                                          op=ALU.mult),
                     reads=[f"bank{bk}", "gtb1"], writes=[xon])
                P.op("dve", lambda e, xo=xo, hs=hs, tt=tt: e.tensor_tensor(out=xo[:, hs], in0=xo[:, hs], in1=x1[:, tt, hs], op=ALU.add),
                     reads=[xon, "x1", f"x1_{tt}"], writes=[xon])

        def fin_S(blk, tl):
            tt = blk * 4 + tl
            xo, xon = x2s[tl % 2], f"x2s{tl % 2}"
            col = 32 + tt
            P.op("act", lambda e, xo=xo, col=col: e.activation(out=xn2T[:, 0:2, 0:512], in_=xo[:, :].rearrange("p (a b) -> p a b", a=2),
                                                                func=AF.Square, accum_out=ss[:, col:col + 1]),
                 reads=[xon, f"ssc{col}"], writes=[f"ssc{col}", "xn2T0"])
            P.op("act", lambda e, col=col: e.activation(out=rs[:, col:col + 1], in_=ss[:, col:col + 1], func=AF.Sqrt,
                                                        bias=epsc[:, 0:1], scale=1.0 / D),
                 reads=[f"ssc{col}", "epsc"], writes=[f"rsc{col}"])

        def fin_B(blk, tl):
            tt = blk * 4 + tl
            xo, xon = x2s[tl % 2], f"x2s{tl % 2}"
            col = 32 + tt
            P.op("dve", lambda e, col=col: e.reciprocal(out=rs[:, col:col + 1], in_=rs[:, col:col + 1]),
                 reads=[f"rsc{col}"], writes=[f"rsc{col}"])
            P.op("dve", lambda e, xo=xo, col=col: e.scalar_tensor_tensor(out=xo[:, :], in0=xo[:, :], scalar=rs[:, col:col + 1],
                                                                        in1=gfb[:, :], op0=ALU.mult, op1=ALU.mult),
                 reads=[xon, f"rsc{col}", "gfb"], writes=[xon])
            P.op("sp", lambda e, xo=xo, tt=tt, ci=ci: e.dma_start(out=y_d[ci, tt * 128:(tt + 1) * 128, :], in_=xo[:, :]),
                 reads=[xon], dma=xon)

        pending = {}
        for blk in range(2):
            t0 = blk * 512
            for f in range(32):
                (w1t, w1n) = W.get(("w1", f))
                w1v = w1t[:, :].rearrange("p (k c) -> p k c", k=8)
                bk = f % 4
                for k in range(8):
                    mm(banks[bk][:, :], w1v[:, k, :], xn2T[:, k, t0:t0 + 512], k == 0, k == 7, [w1n, f"xn2T{blk}"], [f"bank{bk}"])
                st_ = sgt[f % 4]
                stn = f"sgt{f % 4}"
                P.op("act", lambda e, bk=bk, st_=st_: e.activation(out=st_[:, :], in_=banks[bk][:, :], func=AF.Relu),
                     reads=[f"bank{bk}"], writes=[stn])
                P.op("dve", lambda e, bk=bk, st_=st_, f=f: e.tensor_tensor(out=hT[:, f, :], in0=banks[bk][:, :], in1=st_[:, :], op=ALU.mult),
                     reads=[f"bank{bk}", stn], writes=["hT"])
                W.release(("w1", f))
                for fn_ in pending.pop(f, []):
                    fn_()
                if blk == 1 and f == 4:
                    pipes_release()
                if blk == 0 and f == 12 and ci + 1 < NCH:
                    emit_gt(0, ci + 1, 3)
                if f == 27:
                    for p_ in (n2p, n1p):
                        if p_ is not None:
                            p_.hold = True
                if blk == 0:
                    if f < 8 and not n2p.done:
                        n2p.step()
                    else:
                        n1_steps((f - 7) * 15 // 24)
                else:
                    n1_steps(15 + (f + 1) * 16 // 32)
            if blk == 0:
                n2p.run()
            pipes_to_clean()
            for f in range(32):
                (w2t, w2n) = W.get(("w2", f))
                for tl in range(4):
                    for half in range(2):
                        bk = tl * 2 + half
                        mm(banks[bk][:, :], hT[:, f, tl * 128:(tl + 1) * 128], w2t[:, half * 512:(half + 1) * 512], f == 0, f == 31,
                           [w2n, "hT"], [f"bank{bk}"])
                W.release(("w2", f))
                if f % 8 == 7 and n1p is not None and not n1p.done:
                    n1p.step()
            fin_A(blk, 0); fin_S(blk, 0); fin_A(blk, 1); fin_S(blk, 1); fin_B(blk, 0); fin_B(blk, 1)
            if blk == 0:
                pending = {1: [lambda: (fin_A(0, 2), fin_S(0, 2))], 3: [lambda: (fin_A(0, 3), fin_S(0, 3))],
                           5: [lambda: fin_B(0, 2)], 7: [lambda: fin_B(0, 3)]}
            else:
                fin_A(blk, 2); fin_S(blk, 2); fin_A(blk, 3); fin_S(blk, 3); fin_B(blk, 2); fin_B(blk, 3)
                pipes_release()
    counts = P.emit()
    return nc, counts


def _chunks():
    ch = []
    for b in range(4):
        for j in range(4):
            ch.append((0, b, j))
    for b in range(2):
        for j in range(16):
            ch.append((1, b, j))
    return ch


def _lhsT_tile(Wm, cols):
    t = np.ascontiguousarray(Wm[:, cols])
    return t.reshape(8, 128, 128).transpose(1, 0, 2).reshape(128, 1024)


def _build_wall(w_in, w_ao, w_co, w_mo, w1, w2):
    wall = np.zeros((NWT, 128, 1024), np.float32)
    for u in range(2):
        for g in range(3):
            heads = [8 * g + 4 * u + j for j in range(4)]
            x1c = np.concatenate([np.arange(h * 64, h * 64 + 32) for h in heads])
            x2c = x1c + 32
            wall[WIDX[("q1", g, u)]] = _lhsT_tile(w_in, x1c)
            wall[WIDX[("q2", g, u)]] = _lhsT_tile(w_in, x2c)
            wall[WIDX[("k1", g, u)]] = _lhsT_tile(w_in, 1536 + x1c)
            wall[WIDX[("k2", g, u)]] = _lhsT_tile(w_in, 1536 + x2c)
            c0 = 3072 + (8 * g + 4 * u) * 64
            tv = np.ascontiguousarray(w_in[:, c0:c0 + 256]).reshape(8, 128, 256).transpose(1, 0, 2)
            wall[WIDX[("va", g, u)]] = tv[:, 0:4, :].reshape(128, 1024)
            wall[WIDX[("vb", g, u)]] = tv[:, 4:8, :].reshape(128, 1024)
    for s in range(8):
        wall[WIDX[("h", s)]] = _lhsT_tile(w_in, np.arange(4608 + s * 128, 4608 + s * 128 + 128))
        wall[WIDX[("c", s)]] = _lhsT_tile(w_in, np.arange(5632 + s * 128, 5632 + s * 128 + 128))
        wall[WIDX[("b", s)]] = _lhsT_tile(w_in, np.arange(6656 + s * 128, 6656 + s * 128 + 128))
    for m in range(8):
        wall[WIDX[("ga", m)]] = _lhsT_tile(w_in, np.arange(7680 + m * 128, 7680 + m * 128 + 128))
        wall[WIDX[("gc", m)]] = _lhsT_tile(w_in, np.arange(8704 + m * 128, 8704 + m * 128 + 128))
        wall[WIDX[("co", m)]] = _lhsT_tile(w_co, np.arange(m * 128, m * 128 + 128))
        wall[WIDX[("mo", m)]] = w_mo[m * 128:(m + 1) * 128, :]
    for mp in range(4):
        t = np.zeros((128, 8, 128), np.float32)
        for m2 in range(2):
            for pi in range(4):
                m = 2 * mp + m2
                t[:, m2 * 4 + pi, :] = w_ao[pi * 128:(pi + 1) * 128, m * 128:(m + 1) * 128]
        wall[WIDX[("ao", mp)]] = t.reshape(128, 1024)
    for f in range(32):
        wall[WIDX[("w1", f)]] = _lhsT_tile(w1, np.arange(f * 128, f * 128 + 128))
        wall[WIDX[("w2", f)]] = w2[f * 128:(f + 1) * 128, :]
    return wall


def _key_tile_ext():
    ii = np.arange(128)
    rows = []
    for kt in range(9):
        rows.append(960 + 128 * kt + ii)
    for r in range(4):
        for kk in range(3):
            rows.append(768 + 512 * kk + r + 4 * ii)
    for cp in range(8):
        rows.append(cp + 16 * ii)
        rows.append(cp + 8 + 16 * ii)
        rows.append(2048 + cp + 8 * ii)
    return np.stack(rows)


def _constants():
    i = np.arange(128)[:, None]
    j = np.arange(128)[None, :]
    mA = np.where(i >= j, 0.0, NEG)
    mB = np.where(i <= j, 0.0, NEG)
    mC = np.where((i >= 64) & (i <= j + 64), 0.0, NEG)
    mLo = np.where((i % 2 == 0) & (i // 2 <= j), 0.0, NEG)
    mHi = np.where((i % 2 == 1) & ((i - 1) // 2 <= j), 0.0, NEG)
    masks = np.stack([np.stack([m, m], axis=1) for m in (mA, mB, mC, mLo, mHi)], axis=1)
    cst = np.zeros((128, 1536), np.float32)
    cst[:, 0:1280] = masks.reshape(128, 1280)
    cst[:, 1280:1408] = np.eye(128, dtype=np.float32)
    cst[:, 1408:1536] = 1.0
    return cst


_PROGRAM_CACHE = {}


def kernel(x_prompt, x_sample, c_prompt, c_sample, w_ada, b_ada, norm1_g, w_in, conv_w, conv_b,
           w_attn_out, w_conv_out, w_mix_out, norm2_g, w_mlp_in, w_mlp_out, final_norm_g):
    f32 = np.float32
    xs = [np.asarray(x_prompt, f32), np.asarray(x_sample, f32)]
    cs_in = [np.asarray(c_prompt, f32), np.asarray(c_sample, f32)]
    seqlen = [xs[0].shape[1], xs[1].shape[1]]
    chunks = _chunks()
    wall = _build_wall(np.asarray(w_in, f32)[0], np.asarray(w_attn_out, f32)[0], np.asarray(w_conv_out, f32)[0],
                       np.asarray(w_mix_out, f32)[0], np.asarray(w_mlp_in, f32)[0], np.asarray(w_mlp_out, f32)[0])
    wa = np.asarray(w_ada, f32)[0]
    wada = np.stack([_lhsT_tile(wa, np.arange(et * 128, et * 128 + 128)) for et in range(48)])
    smallp = np.zeros((128, 96), f32)
    smallp[:, 0:48] = np.asarray(b_ada, f32)[0].reshape(48, 128).T
    smallp[:, 48:56] = np.asarray(norm1_g, f32)[0].reshape(8, 128).T
    smallp[:, 56:64] = np.asarray(norm2_g, f32)[0].reshape(8, 128).T
    cw = np.asarray(conv_w, f32)[0]
    smallp[:, 64:88] = cw.reshape(3, 8, 128).transpose(2, 1, 0).reshape(128, 24)
    smallp[:, 88:96] = np.asarray(conv_b, f32)[0].reshape(8, 128).T
    gfb = np.ascontiguousarray(np.broadcast_to(np.asarray(final_norm_g, f32)[None, :], (128, 1024)))
    cst = _constants()
    ktext = _key_tile_ext()
    inv = (1.0 / (np.float32(10000.0) ** (np.arange(32, dtype=f32) / np.float32(32)))).astype(f32)
    invp = inv[np.arange(128) % 32]

    in_maps = []
    for core in range(NCORES):
        xext = np.zeros((NCH, EXT, D), f32)
        cT = np.zeros((128, 8, NCH), f32)
        cst_cs = np.zeros((NCH, 128, 2, EXT), f32)
        kb = np.zeros((NCH, 128, NKT), f32)
        uv = np.zeros((NCH, 128, 2), f32)
        for ci in range(NCH):
            which, b, j = chunks[core * NCH + ci]
            S = seqlen[which]
            p0 = j * T - HALO
            lo, hi = max(p0, 0), min(p0 + EXT, S)
            xext[ci, lo - p0:hi - p0, :] = xs[which][b, lo:hi, :]
            cT[:, :, ci] = cs_in[which][b].reshape(8, 128).T
            pos = (p0 + np.arange(EXT)).astype(f32)
            ang = (pos[None, :] * invp[:, None]).astype(f32)
            cst_cs[ci, :, 0, :] = np.cos(ang)
            cst_cs[ci, :, 1, :] = np.sin(ang)
            kpos = p0 + ktext
            kb[ci] = np.where((kpos >= 0) & (kpos < S), 0.0, NEG).T
            uv[ci, :, 0] = 1.0 if j * T - 1 >= 0 else 0.0
            uv[ci, :, 1] = 1.0 if j * T + T < S else 0.0
        in_maps.append(dict(xext=xext, cT=cT, cs=cst_cs, kbias=kb, uval=uv, wall=wall, wada=wada, smallp=smallp,
                            gfb=gfb, cst=cst))

    if "nc" not in _PROGRAM_CACHE:
        _PROGRAM_CACHE["nc"] = build_program()[0]
    nc = _PROGRAM_CACHE["nc"]
    res = run_bass_kernel_spmd(nc, in_maps, core_ids=list(range(NCORES)))
    y_p = np.zeros_like(xs[0])
    y_s = np.zeros_like(xs[1])
    outs = [y_p, y_s]
    for core in range(NCORES):
        y = res.results[core]["y"]
        for ci in range(NCH):
            which, b, j = chunks[core * NCH + ci]
            outs[which][b, j * T:(j + 1) * T, :] = y[ci]
    return (y_p, y_s)
```

```python
import numpy as np
import concourse.bass as bass
import concourse.mybir as mybir
from concourse.bass_utils import run_bass_kernel_spmd

F32 = mybir.dt.float32
BF16 = mybir.dt.bfloat16
ALU = mybir.AluOpType
AF = mybir.ActivationFunctionType

D = 1024
T = 1024
HALO = 1024
EXT = T + 2 * HALO
NCH = 6
NCORES = 8
EPS = 1e-6
NEG = -30000.0
NKT = 45
ENGS = ("pe", "act", "dve", "pool", "sp")
RING = 10


class Prog:
    def __init__(self, nc):
        self.nc = nc
        self.ops = []
        self.last_write = {}
        self.reads_since = {}
        self.alias = {}

    def add_alias(self, a, b):
        self.alias.setdefault(a, set()).add(b)
        self.alias.setdefault(b, set()).add(a)

    def op(self, eng, fn, reads=(), writes=(), dma=None):
        idx = len(self.ops)
        raw, other = set(), set()
        for r in reads:
            if r in self.last_write:
                raw.add(self.last_write[r])
        for w in writes:
            for n in {w} | self.alias.get(w, set()):
                if n in self.last_write:
                    other.add(self.last_write[n])
                for x in self.reads_since.get(n, ()):
                    other.add(x)
        self.ops.append(dict(eng=eng, fn=fn, raw=raw, other=other - raw, dma=dma))
        for r in reads:
            self.reads_since.setdefault(r, []).append(idx)
        for w in writes:
            self.last_write[w] = idx
            self.reads_since[w] = []
        return idx

    def emit(self):
        nc, ops = self.nc, self.ops
        for o in ops:
            keep = set()
            for d in o["raw"]:
                keep.add(d)
            for d in o["other"]:
                od = ops[d]
                if od["eng"] != o["eng"] or od["dma"] is not None:
                    keep.add(d)
            o["deps"] = keep
        signal = [False] * len(ops)
        for o in ops:
            for d in o["deps"]:
                signal[d] = True
        eng_sem = {e: nc.alloc_semaphore(f"S_{e}") for e in ENGS}
        dma_keys = sorted({o["dma"] for o in ops if o["dma"] is not None})
        dma_sem = {k: nc.alloc_semaphore(f"D_{k}") for k in dma_keys}
        cnt = {e: 0 for e in ENGS}
        dcnt = {k: 0 for k in dma_keys}
        for i, o in enumerate(ops):
            if o["dma"] is not None:
                dcnt[o["dma"]] += 16
                o["sig"] = ("d", o["dma"], dcnt[o["dma"]])
            elif signal[i]:
                cnt[o["eng"]] += 1
                o["sig"] = ("e", o["eng"], cnt[o["eng"]])
            else:
                o["sig"] = None
        per_eng = {e: [i for i, o in enumerate(ops) if o["eng"] == e] for e in ENGS}

        def run_engine(ename, eng):
            waited = {}
            for i in per_eng[ename]:
                o = ops[i]
                need = {}
                for d in o["deps"]:
                    kind, key, val = ops[d]["sig"]
                    if kind == "e" and key == ename:
                        if d not in o["raw"]:
                            continue
                    k = (kind, key)
                    if need.get(k, 0) < val:
                        need[k] = val
                for k, val in need.items():
                    if waited.get(k, 0) >= val:
                        continue
                    eng.wait_ge(eng_sem[k[1]] if k[0] == "e" else dma_sem[k[1]], val)
                    waited[k] = val
                ins = o["fn"](eng)
                sv = o["sig"]
                if sv is not None:
                    if sv[0] == "d":
                        ins.then_inc(dma_sem[sv[1]], 16)
                    else:
                        ins.then_inc(eng_sem[sv[1]], 1)
            if ename == "sp":
                for k, v in dcnt.items():
                    if v > 0 and waited.get(("d", k), 0) < v:
                        eng.wait_ge(dma_sem[k], v)

        with nc.Block() as block:
            @block.tensor
            def _(e):
                run_engine("pe", e)

            @block.scalar
            def _(e):
                run_engine("act", e)

            @block.vector
            def _(e):
                run_engine("dve", e)

            @block.gpsimd
            def _(e):
                run_engine("pool", e)

            @block.sync
            def _(e):
                run_engine("sp", e)
        return {e: len(per_eng[e]) for e in ENGS}


def wall_index():
    idx, n = {}, 0
    for u in range(2):
        for g in range(3):
            for nm in ("q1", "q2", "k1", "k2", "va", "vb"):
                idx[(nm, g, u)] = n
                n += 1
    for s in range(8):
        for nm in ("h", "c", "b"):
            idx[(nm, s)] = n
            n += 1
    for m in range(8):
        idx[("ga", m)] = n; n += 1
        idx[("gc", m)] = n; n += 1
        if m % 2 == 0:
            idx[("ao", m // 2)] = n; n += 1
        idx[("co", m)] = n; n += 1
    for m in range(8):
        idx[("mo", m)] = n; n += 1
    for f in range(32):
        idx[("w1", f)] = n; n += 1
    for f in range(32):
        idx[("w2", f)] = n; n += 1
    return idx, n


WIDX, NWT = wall_index()


G_ORDER = (2, 1, 0)


def chunk_tile_order():
    o = []
    for u in range(2):
        for g in G_ORDER:
            for nm in ("q1", "q2", "k1", "k2", "va", "vb"):
                o.append((nm, g, u))
    for s in range(8):
        for nm in ("h", "c", "b"):
            o.append((nm, s))
    for m in range(8):
        o.append(("ga", m)); o.append(("gc", m))
        if m % 2 == 0:
            o.append(("ao", m // 2))
        o.append(("co", m))
    for m in range(8):
        o.append(("mo", m))
    for blk in range(2):
        for f in range(32):
            o.append(("w1", f))
        for f in range(32):
            o.append(("w2", f))
    return o


class WRing:
    def __init__(self, P, wall_ap, slots, plan):
        self.P, self.wall, self.slots = P, wall_ap, slots
        self.plan = list(plan)
        self.pos = 0
        self.loaded = {}
        self.inuse = 0
        self.free = list(range(len(slots)))
        self.R = len(slots)

    def _load_next(self):
        key = self.plan[self.pos]
        self.pos += 1
        s = self.free.pop(0)
        t = WIDX[key]
        dst = self.slots[s]
        self.P.op("pool", lambda e, dst=dst, t=t: e.dma_start(out=dst[:, :], in_=self.wall[t, :, :]),
                  writes=[f"ring{s}"], dma=f"ring{s}")
        self.loaded[key] = s
        self.inuse += 1

    def pump(self):
        while self.pos < len(self.plan) and self.inuse < self.R:
            self._load_next()

    def get(self, key):
        self.pump()
        guard = 0
        while key not in self.loaded:
            assert self.pos < len(self.plan), key
            self._load_next()
            guard += 1
            assert guard < 1000
        s = self.loaded[key]
        return self.slots[s], f"ring{s}"

    def release(self, key):
        s = self.loaded.pop(key)
        self.free.append(s)
        self.inuse -= 1
        self.pump()


def build_program():
    nc = bass.Bass("TRN2", target_bir_lowering=False)
    dt = nc.dram_tensor
    xext = dt("xext", [NCH, EXT, D], F32, kind="ExternalInput").ap()
    cT_d = dt("cT", [128, 8, NCH], F32, kind="ExternalInput").ap()
    cs_d = dt("cs", [NCH, 128, 2, EXT], F32, kind="ExternalInput").ap()
    kb_d = dt("kbias", [NCH, 128, NKT], F32, kind="ExternalInput").ap()
    uv_d = dt("uval", [NCH, 128, 2], F32, kind="ExternalInput").ap()
    wall = dt("wall", [NWT, 128, 1024], F32, kind="ExternalInput").ap()
    wada = dt("wada", [48, 128, 1024], F32, kind="ExternalInput").ap()
    sp_d = dt("smallp", [128, 48 + 8 + 8 + 24 + 8], F32, kind="ExternalInput").ap()
    gfb_d = dt("gfb", [128, 1024], F32, kind="ExternalInput").ap()
    cst_d = dt("cst", [128, 5 * 256 + 128 + 128], F32, kind="ExternalInput").ap()
    y_d = dt("y", [NCH, T, D], F32, kind="ExternalOutput").ap()

    cur = [17408]

    def A(name, shape, dtype, at=None):
        esz = 4 if dtype == F32 else 2
        n = 1
        for s in shape[1:]:
            n *= s
        nbytes = n * esz
        if at is None:
            off = cur[0]
            cur[0] += (nbytes + 63) // 64 * 64
        else:
            off = at
        assert off + nbytes <= 229376, (name, off, nbytes)
        return nc.alloc_sbuf_tensor_at(name, list(shape), dtype, offset=off)

    ring = [A(f"ring{i}", [128, 1024], BF16) for i in range(RING)]
    xnT = A("xnT", [128, 8, EXT], BF16)
    xnT_off = cur[0] - 8 * EXT * 2
    attT = A("attT", [128, 4, T], BF16)
    gtb = A("gtb", [128, 2, D], F32)
    gfb = A("gfb_sb", [128, D], F32)
    smallp = A("smallp_sb", [128, 96], F32)
    cst_bf = A("cst_bf", [128, 5 * 256 + 128 + 128], BF16)
    ident_f = A("ident_f", [128, 128], F32)
    ones_f = A("ones_f", [128, 128], F32)
    dg = A("dg", [128, 2, 128], F32)
    cT = A("cT_sb", [128, 8, NCH], F32)
    scT = A("scT", [128, 8, NCH], BF16)
    modT = A("modT", [128, 48, NCH], F32)
    a1T = A("a1T", [128, 8, NCH], F32)
    a2T = A("a2T", [128, 8, NCH], F32)
    tmp6 = A("tmp6", [128, 8, NCH], F32)
    kbias = A("kbias_sb", [128, NKT], F32)
    uval = A("uval_sb", [128, 2], F32)
    ss = A("ss", [128, 48], F32)
    rs = A("rs", [128, 48], F32)
    epsc = A("epsc", [128, 16], F32)
    xinC2 = A("xinC2", [128, D], F32)
    R1 = cur[0]
    R1_SIZE = 104 * 1024
    assert R1 + R1_SIZE <= 229376, R1
    NWA = 12
    wada_sb = [A(f"wada{i}", [128, 8, 128], BF16, at=R1 + i * 2048) for i in range(NWA)]
    o = R1
    cs_sb = A("cs_sb", [128, 2, EXT], F32, at=o); o += 24 * 1024
    qpad = [A(f"qpad{i}", [128, 2, T], BF16, at=o + i * 4096) for i in range(2)]; o += 8 * 1024
    kpair = [A(f"kpair{i}", [128, EXT], BF16, at=o + i * 6144) for i in range(2)]; o += 12 * 1024
    vbuf = A("vbuf", [128, 32, 2, 192], BF16, at=o); o += 24 * 1024
    rt = [A(f"rt{i}", [128, 512], F32, at=o + i * 2048) for i in range(4)]
    rtw = A("rtw", [128, 1024], F32, at=o); o += 8 * 1024
    acc = A("acc", [128, 2, 2, T], F32, at=o); o += 16 * 1024
    ptb = [A(f"pt{i}", [128, 2, 2, 128], BF16, at=o + i * 1024) for i in range(4)]; o += 4 * 1024
    qpad2 = [A(f"qpadB{i}", [128, 2, T], BF16, at=o + i * 4096) for i in range(2)]; o += 8 * 1024
    assert o <= R1 + R1_SIZE
    xinC = [A(f"xinC{i}", [128, D], F32, at=R1 + 96 * 1024 + i * 4096) for i in range(2)] + [xinC2]
    xsC = A("xsC", [128, 4, D], BF16, at=xnT_off + 8 * EXT * 2)
    x1 = A("x1", [128, 8, D], F32, at=R1)
    xn2T = A("xn2T", [128, 8, T], BF16, at=R1 + 32 * 1024)
    mergedT = A("mergedT", [128, 8, T], BF16, at=R1 + 48 * 1024)
    cvinT = A("cvinT", [128, 8, T], BF16, at=R1)
    ubuf = A("ubuf", [128, 1032], F32, at=R1 + 16 * 1024)
    ctmp = [A(f"ctmp{i}", [128, 512], F32, at=R1 + 16 * 1024 + 4224 + i * 2048) for i in range(1)]
    csb = A("csb", [128, 512], F32, at=R1 + 16 * 1024 + 4224 + 2048)
    hT = A("hT", [128, 32, 512], BF16, at=R1 + 48 * 1024)
    sgt = [A(f"sgt{i}", [128, 512], F32, at=R1 + 80 * 1024 + i * 2048) for i in range(4)]
    xinB = [A(f"xinB{i}", [128, D], F32, at=xnT_off + i * 4096) for i in range(4)]
    xsB = A("xsB", [128, 4, D], BF16, at=xnT_off + 16 * 1024)
    x2s = [A(f"x2s{i}", [128, D], F32, at=R1 + 88 * 1024 + i * 4096) for i in range(2)]

    banks = [nc.alloc_psum_tensor(f"bank{i}", [128, 512], F32) for i in range(8)]
    banks_bf = [b.bitcast(BF16) for b in banks]

    P = Prog(nc)
    R1_ATT = ["cs0", "cs1", "cs2", "qpad0", "qpad1", "qpadB0", "qpadB1", "kpair0", "kpair1", "vbuf", "rt0", "rt1", "rt2", "rt3", "acc",
              "pt0a", "pt1a", "pt2a", "pt3a", "pt0b", "pt1b", "pt2b", "pt3b"]
    R1_POST = ["x1", "xn2T0", "xn2T1", "mergedT", "cvinT", "ubuf", "ctmp0", "csb", "hT", "sgt0", "sgt1", "sgt2", "sgt3"]
    for a in R1_ATT:
        for b in R1_POST:
            P.add_alias(a, b)
    for a in ["hT"]:
        for b in ["mergedT"]:
            P.add_alias(a, b)
    for b in ["cvinT", "ubuf", "ctmp0", "csb"]:
        P.add_alias("x1", b)
    for a in ["sgt0", "sgt1", "sgt2", "sgt3"]:
        for b in ["ubuf", "ctmp0", "csb"]:
            P.add_alias(a, b)
    for b in ["xinB0", "xinB1", "xinB2", "xinB3", "xsB0", "xsB1", "xsB2", "xsB3"]:
        P.add_alias("xnT", b)
    for i in range(NWA):
        for b in ["cs0", "cs1", "cs2", "x1", "qpad0", "qpad1"]:
            P.add_alias(f"wada{i}", b)
    for i in range(2):
        for b in ["acc", "pt0a", "pt1a", "pt2a", "pt3a", "pt0b", "pt1b", "pt2b", "pt3b"]:
            P.add_alias(f"x2s{i}", b)
        for b in ["qpadB0", "qpadB1"]:
            P.add_alias(f"xinC{i}", b)
    for i in range(4):
        P.add_alias(f"xsC{i}", "attT")

    plan = []
    for ci in range(NCH):
        plan += chunk_tile_order()
    W = WRing(P, wall, ring, plan)

    maskb = cst_bf[:, 0:1280].rearrange("p (t h q) -> p t h q", t=5, h=2)
    ident = cst_bf[:, 1280:1408]
    ones_bf = cst_bf[:, 1408:1536]
    b_adaT = smallp[:, 0:48]
    g1T = smallp[:, 48:56]
    g2T = smallp[:, 56:64]
    cwT = smallp[:, 64:88].rearrange("p (s t) -> p s t", t=3)
    cbT = smallp[:, 88:96]

    def mm(out, lhsT, rhs, start, stop, reads, writes):
        P.op("pe", lambda e: e.matmul(out, lhsT=lhsT, rhs=rhs, start=start, stop=stop), reads=reads, writes=writes)

    P.op("pool", lambda e: e.dma_start(out=cst_bf[:, :], in_=cst_d), writes=["cst"], dma="cst")
    P.op("sp", lambda e: e.dma_start(out=smallp[:, :], in_=sp_d), writes=["smallp"], dma="smallp")
    P.op("sp", lambda e: e.dma_start(out=gfb[:, :], in_=gfb_d), writes=["gfb"], dma="gfb")
    P.op("sp", lambda e: e.dma_start(out=cT[:, :, :], in_=cT_d), writes=["cT"], dma="cT")
    P.op("sp", lambda e: e.dma_start(out=ident_f[:, :], in_=cst_d[:, 1280:1408]), writes=["ident_f"], dma="ident_f")
    P.op("sp", lambda e: e.dma_start(out=ones_f[:, :], in_=cst_d[:, 1408:1536]), writes=["ones_f"], dma="ones_f")
    W.pump()
    P.op("dve", lambda e: e.memset(epsc[:, :], EPS), writes=["epsc"])
    P.op("act", lambda e: e.activation(out=scT[:, :, :], in_=cT[:, :, :], func=AF.Silu), reads=["cT"], writes=["scT"])
    def emit_a(dst, dn, which, gT):
        P.op("dve", lambda e: e.tensor_scalar(out=tmp6[:, :, :], in0=modT[:, which * 8:which * 8 + 8, :], scalar1=1.0,
                                              scalar2=None, op0=ALU.add),
             reads=[f"modT{which}"], writes=["tmp6"])
        for cj in range(NCH):
            P.op("dve", lambda e, cj=cj: e.tensor_tensor(out=dst[:, :, cj], in0=tmp6[:, :, cj], in1=gT, op=ALU.mult),
                 reads=["tmp6", "smallp"], writes=[dn])

    def adaln_et(et):
        wb = wada_sb[et % NWA]
        wn = f"wada{et % NWA}"
        P.op("pool", lambda e: e.dma_start(out=wb[:, :, :], in_=wada[et].rearrange("p (k c) -> p k c", k=8)),
             writes=[wn], dma=wn)
        bk = banks[et % 2]
        for k in range(8):
            mm(bk[:, 0:NCH], wb[:, k, :], scT[:, k, :], k == 0, k == 7, [wn, "scT"], [f"bank{et % 2}"])
        P.op("dve", lambda e: e.tensor_scalar(out=modT[:, et, :], in0=bk[:, 0:NCH], scalar1=b_adaT[:, et:et + 1],
                                              scalar2=None, op0=ALU.add),
             reads=[f"bank{et % 2}", "smallp"], writes=[f"modT{et // 8}"])
        if et == 15:
            emit_a(a1T, "a1T", 1, g1T)
        if et == 39:
            emit_a(a2T, "a2T", 4, g2T)

    P.op("dve", lambda e: e.memset(ss[:, :], 0.0), writes=[f"ssc{c_}" for c_ in range(48)])
    for et in range(16):
        adaln_et(et)

    def norm_to_T(ci, ntiles, src_rows, xin, xin_names, xs, xs_name, ss_base, dstT, dst_name_, aT, sh_which, dst_col0,
                  src_is_dram=True, src_sb=None, src_sb_name=None, act_share=2, dst_names=None, sq_dve=False, tr_lag=1):
        def load(tt):
            if src_is_dram and tt < ntiles:
                xt = xin[tt % len(xin)]
                xname = xin_names[tt % len(xin)]
                r0 = src_rows + tt * 128
                P.op("sp", lambda e, xt=xt, r0=r0: e.dma_start(out=xt[:, :], in_=xext[ci, r0:r0 + 128, :]),
                     writes=[xname], dma=xname)

        def stage1(tt):
            tl = tt % 4
            if src_is_dram:
                xt = xin[tt % len(xin)]
                xname = xin_names[tt % len(xin)]
                src = xt[:, :]
            else:
                src = src_sb[:, tt, :]
                xname = src_sb_name + str(tt)
            col = ss_base + tt
            if sq_dve and tt % 2 == 1:
                P.op("dve", lambda e, src=src, col=col, tl=tl: e.scalar_tensor_tensor(out=xs[:, tl, :], in0=src, scalar=1.0, in1=src,
                                                                                    op0=ALU.mult, op1=ALU.mult, accum_out=ss[:, col:col + 1]),
                     reads=[xname, f"ssc{col}"], writes=[f"ssc{col}", xs_name + str(tl)])
            else:
                P.op("act", lambda e, src=src, col=col, tl=tl: e.activation(out=xs[:, tl, :], in_=src, func=AF.Square,
                                                                            accum_out=ss[:, col:col + 1]),
                     reads=[xname, f"ssc{col}"], writes=[f"ssc{col}", xs_name + str(tl)])
            P.op("act", lambda e, col=col: e.activation(out=rs[:, col:col + 1], in_=ss[:, col:col + 1], func=AF.Sqrt,
                                                        bias=epsc[:, 0:1], scale=1.0 / D),
                 reads=[f"ssc{col}", "epsc"], writes=[f"rsc{col}"])
            P.op("dve", lambda e, col=col: e.reciprocal(out=rs[:, col:col + 1], in_=rs[:, col:col + 1]),
                 reads=[f"rsc{col}"], writes=[f"rsc{col}"])
            return src, xname, col

        def stage2(tt, src, xname, col):
            tl = tt % 4
            P.op("act", lambda e, src=src, col=col, tl=tl: e.activation(out=xs[:, tl, :], in_=src, func=AF.Identity,
                                                                        scale=rs[:, col:col + 1]),
                 reads=[xname, f"rsc{col}"], writes=[xs_name + str(tl)])

        def transposes(tt):
            tl = tt % 4
            g_ = tt // 2
            b0 = 4 + 2 * (g_ % 2)
            for k in range(8):
                bi = b0 + k // 4
                c0 = (k % 4) * 256 + (tt % 2) * 128
                P.op("pe", lambda e, bi=bi, c0=c0, tl=tl, k=k: e.transpose(banks_bf[bi][:, c0:c0 + 128],
                                                                             xs[:, tl, k * 128:(k + 1) * 128], ident),
                     reads=[xs_name + str(tl), "cst"], writes=[f"bank{bi}"])

        def evac_one(g_, k):
            e0 = dst_col0 + g_ * 256
            dst_name = dst_names[g_] if dst_names is not None else dst_name_
            bi = 4 + 2 * (g_ % 2) + k // 4
            c0 = (k % 4) * 256
            sh = modT[:, sh_which * 8 + k, ci:ci + 1]
            sc = aT[:, k, ci:ci + 1]
            aT_name = "a1T" if sh_which == 0 else "a2T"
            if act_share == 0 or k % act_share != act_share - 1:
                P.op("dve", lambda e: e.tensor_scalar(
                    out=dstT[:, k, e0:e0 + 256], in0=banks_bf[bi][:, c0:c0 + 256], scalar1=sc, scalar2=sh,
                    op0=ALU.mult, op1=ALU.add), reads=[f"bank{bi}", f"modT{sh_which}", aT_name], writes=[dst_name])
            else:
                P.op("act", lambda e: e.activation(
                    out=dstT[:, k, e0:e0 + 256], in_=banks_bf[bi][:, c0:c0 + 256], func=AF.Identity, bias=sh, scale=sc),
                    reads=[f"bank{bi}", f"modT{sh_which}", aT_name], writes=[dst_name])

        class NormPipe:
            def __init__(self):
                self.n_load = self.n_s1 = self.n_s2 = self.n_tr = 0
                self.info = {}
                self.evq = []
                self.hold = False
                self.groups_done = 0
                self.nsl = len(xin) if src_is_dram else 10 ** 6
                self.avail = ntiles

            @property
            def done(self):
                return self.n_tr == ntiles and not self.evq

            @property
            def clean(self):
                return (not self.evq) and self.n_tr % 2 == 0

            def step(self):
                if src_is_dram:
                    while self.n_load < ntiles and self.n_load <= self.n_s1 + 1 and self.n_load - self.n_s2 < self.nsl:
                        load(self.n_load)
                        self.n_load += 1
                if self.n_s1 < min(ntiles, self.avail) and (not src_is_dram or self.n_s1 < self.n_load) and self.n_s1 - self.n_tr < 4:
                    self.info[self.n_s1] = stage1(self.n_s1)
                    self.n_s1 += 1
                if self.n_s2 < self.n_s1 - 1 or (self.n_s1 == ntiles and self.n_s2 < ntiles):
                    stage2(self.n_s2, *self.info[self.n_s2])
                    self.n_s2 += 1
                for _ in range(4):
                    if self.evq:
                        g_, k = self.evq.pop(0)
                        evac_one(g_, k)
                        if k == 7:
                            self.groups_done += 1
                can_tr = self.n_tr < self.n_s2 - tr_lag or (self.n_s2 == ntiles and self.n_tr < ntiles)
                if can_tr and not (self.hold and self.n_tr % 2 == 0):
                    g_ = self.n_tr // 2
                    if all(q[0] != g_ - 2 for q in self.evq):
                        transposes(self.n_tr)
                        self.n_tr += 1
                        if self.n_tr % 2 == 0:
                            self.evq += [(g_, k) for k in range(8)]

            def run(self, nsteps=None, until=None):
                c_ = 0
                while not self.done and (nsteps is None or c_ < nsteps) and not (until is not None and until()):
                    self.step()
                    c_ += 1

        return NormPipe()

    for ci in range(NCH):
        P.op("sp", lambda e, ci=ci: e.dma_start(out=kbias[:, :], in_=kb_d[ci]), writes=["kbias"], dma="kbias")
        P.op("sp", lambda e, ci=ci: e.dma_start(out=uval[:, :], in_=uv_d[ci]), writes=["uval"], dma="uval")
        def emit_gt(wi, cj, bi):
            which = (2, 5)[wi]
            for k in range(8):
                sl = k % 2
                P.op("dve", lambda e, k=k, sl=sl: e.tensor_scalar(
                    out=dg[:, sl, :], in0=ident_f[:, :], scalar1=modT[:, which * 8 + k, cj:cj + 1], scalar2=None, op0=ALU.mult),
                    reads=["ident_f", f"modT{which}"], writes=[f"dg{sl}"])
                mm(banks[bi][:, (k % 4) * 128:(k % 4) * 128 + 128], ones_f[:, :], dg[:, sl, :], True, True,
                   ["ones_f", f"dg{sl}"], [f"bank{bi}"])
                if k % 4 == 3:
                    h0 = (k // 4) * 512
                    P.op("act", lambda e, h0=h0: e.copy(out=gtb[:, wi, h0:h0 + 512], in_=banks[bi][:, :]),
                         reads=[f"bank{bi}"], writes=[f"gtb{wi}"])
        if ci == 0:
            n1p = norm_to_T(0, EXT // 128, 0, xinC, ["xinC0", "xinC1", "xinC2"], xsC, "xsC", 0, xnT, "xnT", a1T, 0, 0, act_share=0)
            et_ = 16
            while not n1p.done or et_ < 48:
                if not n1p.done:
                    n1p.step()
                if et_ < 48:
                    adaln_et(et_)
                    et_ += 1
            emit_gt(0, 0, 2)
        if n1p is not None:
            n1p.hold = False
            n1p.run()
        P.op("dve", lambda e: e.memset(ss[:, :], 0.0), writes=[f"ssc{c_}" for c_ in range(48)])
        for th_ in (1, 0, 2):
            P.op("sp", lambda e, ci=ci, th_=th_: e.dma_start(out=cs_sb[:, :, th_ * 1024:(th_ + 1) * 1024],
                                                             in_=cs_d[ci][:, :, th_ * 1024:(th_ + 1) * 1024]),
                 writes=[f"cs{th_}"], dma=f"cs{th_}")
        for i in range(2):
            P.op("pool", lambda e, i=i: e.memset(qpad[i][:, :, :], 0.0), writes=[f"qpad{i}"])
        P.op("pool", lambda e: e.memset(vbuf[:, :, :, 64:128], 1.0), writes=["vbuf"])

        def proj_rope(key1, key2, blocks, place):
            w1t, w1n = W.get(key1)
            w2t, w2n = W.get(key2)
            w1v = w1t[:, :].rearrange("p (k c) -> p k c", k=8)
            w2v = w2t[:, :].rearrange("p (k c) -> p k c", k=8)
            for bi_, (e0, wd) in enumerate(blocks):
                b1, b2 = (0, 1) if bi_ % 2 == 0 else (2, 3)
                for k in range(8):
                    mm(banks[b1][:, 0:wd], w1v[:, k, :], xnT[:, k, e0:e0 + wd], k == 0, k == 7, [w1n, "xnT"], [f"bank{b1}"])
                for k in range(8):
                    mm(banks[b2][:, 0:wd], w2v[:, k, :], xnT[:, k, e0:e0 + wd], k == 0, k == 7, [w2n, "xnT"], [f"bank{b2}"])
                cos = cs_sb[:, 0, e0:e0 + wd]
                sin = cs_sb[:, 1, e0:e0 + wd]
                for (ti, bk, tab) in ((0, b1, cos), (1, b2, sin), (2, b2, cos), (3, b1, sin)):
                    P.op("dve", lambda e, ti=ti, bk=bk, tab=tab, wd=wd: e.tensor_tensor(out=rt[ti][:, 0:wd], in0=banks[bk][:, 0:wd],
                                                                                        in1=tab, op=ALU.mult),
                         reads=[f"bank{bk}"] + [f"cs{t_}" for t_ in range(e0 // 1024, (e0 + wd - 1) // 1024 + 1)], writes=[f"rt{ti}"])
                place(e0, wd)
                yield bi_
            W.release(key1)
            W.release(key2)

        def place_q(e0, wd):
            t0 = e0 - HALO
            for j in range(4):
                pi, var = j // 2, j % 2
                r = slice(32 * j, 32 * j + 32)
                d1 = slice(32 * var, 32 * var + 32)
                d2 = slice(64 + 32 * var, 64 + 32 * var + 32)
                P.op("dve", lambda e, pi=pi, var=var, r=r, d1=d1: e.tensor_tensor(
                    out=qpad[pi][d1, var, t0:t0 + wd], in0=rt[0][r, 0:wd], in1=rt[1][r, 0:wd], op=ALU.subtract),
                    reads=["rt0", "rt1"], writes=[f"qpad{pi}"])
                P.op("dve", lambda e, pi=pi, var=var, r=r, d2=d2: e.tensor_tensor(
                    out=qpad[pi][d2, var, t0:t0 + wd], in0=rt[2][r, 0:wd], in1=rt[3][r, 0:wd], op=ALU.add),
                    reads=["rt2", "rt3"], writes=[f"qpad{pi}"])

        def place_k(e0, wd):
            for pi in range(2):
                r = slice(64 * pi, 64 * pi + 64)
                P.op("dve", lambda e, pi=pi, r=r: e.tensor_tensor(out=kpair[pi][0:64, e0:e0 + wd], in0=rt[0][r, 0:wd],
                                                                  in1=rt[1][r, 0:wd], op=ALU.subtract),
                     reads=["rt0", "rt1"], writes=[f"kpair{pi}"])
                P.op("dve", lambda e, pi=pi, r=r: e.tensor_tensor(out=kpair[pi][64:128, e0:e0 + wd], in0=rt[2][r, 0:wd],
                                                                  in1=rt[3][r, 0:wd], op=ALU.add),
                     reads=["rt2", "rt3"], writes=[f"kpair{pi}"])

        def key_tiles(g):
            def strided(es, st):
                return lambda v: v[:, es:es + 127 * st + 1:st]
            if g == 0:
                return [strided(960 + 128 * kt, 1) for kt in range(9)], 0
            if g == 1:
                return [strided(768 + 512 * kk + r, 4) for r in range(4) for kk in range(3)], 9
            out = []
            for cp in range(8):
                out.append(strided(cp, 16))
                out.append(strided(cp + 8, 16))
                out.append(strided(2048 + cp, 8))
            return out, 21

        def fin_att(u):
            for pi in range(2):
                P.op("act", lambda e, pi=pi: e.activation(out=rtw[0:64, :], in_=acc[64:128, pi, 0, :], func=AF.Ln),
                     reads=["acc"], writes=["rt0", "rt1"])
                P.op("act", lambda e, pi=pi: e.activation(out=rtw[64:128, :], in_=acc[0:64, pi, 1, :], func=AF.Ln),
                     reads=["acc"], writes=["rt0", "rt1"])
                P.op("act", lambda e: e.activation(out=rtw[:, :], in_=rtw[:, :], func=AF.Exp, scale=-1.0),
                     reads=["rt0", "rt1"], writes=["rt0", "rt1"])
                P.op("dve", lambda e, pi=pi, u=u: e.tensor_tensor(out=attT[0:64, 2 * u + pi, :], in0=acc[0:64, pi, 0, :], in1=rtw[0:64, :],
                                                                 op=ALU.mult), reads=["acc", "rt0", "rt1"], writes=["attT"])
                P.op("dve", lambda e, pi=pi, u=u: e.tensor_tensor(out=attT[64:128, 2 * u + pi, :], in0=acc[64:128, pi, 1, :], in1=rtw[64:128, :],
                                                                 op=ALU.mult), reads=["acc", "rt0", "rt1"], writes=["attT"])

        pend_fin = [None]
        for u in range(2):
            for gi, g in enumerate(G_ORDER):
                kblocks = {0: [(960, 512), (1472, 512), (1984, 128)], 1: [(768 + 512 * i, 512) for i in range(3)],
                           2: [(512 * i, 512) for i in range(6)]}[g]
                ktl, kb0 = key_tiles(g)
                for _ in proj_rope(("q1", g, u), ("q2", g, u), [(HALO, 512), (HALO + 512, 512)], place_q):
                    pass
                kgen = proj_rope(("k1", g, u), ("k2", g, u), kblocks, place_k)
                wv = []
                for nm_ in ("va", "vb"):
                    t_, n_ = W.get((nm_, g, u))
                    wv.append((t_[:, :].rearrange("p (k c) -> p k c", k=4), n_))
                vgroups = list(range(0, len(ktl), 2))
                per_blk = max(1, len(vgroups) // (len(kblocks) + 1))
                vi = 0
                kdone = False
                while vi < len(vgroups) or not kdone:
                    for _ in range(per_blk):
                        if vi >= len(vgroups):
                            break
                        kt0 = vgroups[vi]
                        vi += 1
                        bi = 4 + (kt0 // 2) % 2
                        nk = min(2, len(ktl) - kt0)
                        for kl in range(nk):
                            kap = ktl[kt0 + kl]
                            c0 = kl * 256
                            for k in range(8):
                                mm(banks[bi][:, c0:c0 + 256], kap(xnT[:, k, :]), wv[k // 4][0][:, k % 4, :], k == 0, k == 7,
                                   [wv[k // 4][1], "xnT"], [f"bank{bi}"])
                        for hh in range(2):
                            P.op("act", lambda e, bi=bi, kt0=kt0, nk=nk, hh=hh: e.copy(
                                out=vbuf[:, kt0:kt0 + nk, :, hh * 128:hh * 128 + 64],
                                in_=banks[bi][:, 0:256 * nk].rearrange("p (a b h d) -> p a b h d", a=nk, b=2, h=2)[:, :, :, hh, :]),
                                reads=[f"bank{bi}"], writes=["vbuf"])
                    if not kdone:
                        try:
                            next(kgen)
                        except StopIteration:
                            kdone = True
                W.release(("va", g, u))
                W.release(("vb", g, u))
                if pend_fin[0] is not None:
                    fin_att(pend_fin[0])
                    pend_fin[0] = None

                if g == 0:
                    qblocks = [[(128 * qt, 1, 128, qt, qt + 1, 0, 1) for qt in (2 * b, 2 * b + 1)] for b in range(4)]
                elif g == 1:
                    qblocks = [[(4 * i0 + r, 4, 128, r * 3 + i0 // 128, r * 3 + i0 // 128 + 1, 0, 1) for i0 in (0, 128)]
                               for r in range(4)]
                else:
                    qblocks = [[(r, 16, 64, (r % 8) * 3 + r // 8, (r % 8) * 3 + 2, 0, 3 + r // 8) for r in range(4 * b, 4 * b + 4)]
                               for b in range(4)]
                steps = []
                for qb_i, qb in enumerate(qblocks):
                    for pi in range(2):
                        for qi, qt_ in enumerate(qb):
                            steps.append((qb_i, pi, qi, qt_, qi == len(qb) - 1, qb))
                LA = 3

                def emit_qk(n):
                    qb_i, pi, qi, (qs, qst, nq, ta, tb, ma, mb), last, qb = steps[n]
                    sb_i = (0, 1, 2, 6)[n % 4]
                    pt = ptb[n % 4]
                    ptn = f"pt{n % 4}"
                    S = banks[sb_i]
                    qap = qpad[pi][:, :, qs:qs + (nq - 1) * qst + 1:qst]
                    sos = []
                    for ti, (tid, mtype) in enumerate(((ta, ma), (tb, mb))):
                        kap = ktl[tid]
                        so = S[:, ti * 2 * nq:(ti + 1) * 2 * nq].rearrange("p (h q) -> p h q", h=2)
                        sos.append(so)
                        mm(so, ident, maskb[:, mtype, :, 0:nq], True, False, ["cst"], [f"bank{sb_i}"])
                        mm(so, kap(kpair[pi][:, :]), qap, False, True, [f"kpair{pi}", f"qpad{pi}"], [f"bank{sb_i}"])
                    for ti, (tid, mtype) in enumerate(((ta, ma), (tb, mb))):
                        so = sos[ti]
                        kcol = kb0 + tid
                        P.op("act", lambda e, so=so, pt=pt, ti=ti, nq=nq, kcol=kcol: e.activation(
                            out=pt[:, ti, :, 0:nq], in_=so, func=AF.Exp,
                            bias=kbias[:, kcol:kcol + 1], scale=0.125),
                            reads=[f"bank{sb_i}", "kbias"], writes=[ptn + "ab"[ti]])

                def emit_pv(n):
                    qb_i, pi, qi, (qs, qst, nq, ta, tb, ma, mb), last, qb = steps[n]
                    pt = ptb[n % 4]
                    ptn = f"pt{n % 4}"
                    nset = (qb_i * 2 + pi) % 3
                    nd = 3 + nset
                    qc = qi * nq
                    for hh in range(2):
                        lA = vbuf[:, ta, pi, hh * 64:hh * 64 + 128]
                        lB = vbuf[:, tb, pi, hh * 64:hh * 64 + 128]
                        oo = banks[nd][:, hh * 256 + qc:hh * 256 + qc + nq]
                        mm(oo, lA, pt[:, 0, hh, 0:nq], True, False, ["vbuf", ptn + "a"], [f"bank{nd}"])
                        mm(oo, lB, pt[:, 1, hh, 0:nq], False, True, ["vbuf", ptn + "b"], [f"bank{nd}"])
                    if not last:
                        return
                    if g == 0:
                        q0 = qb[0][0]
                        d_ = acc[:, pi, :, q0:q0 + 256]
                        s_ = banks[nd][:, 0:512].rearrange("p (h c) -> p h c", h=2)
                    elif g == 1:
                        r = qb[0][0]
                        d_ = acc[:, pi, :, r:r + 1021:4]
                        s_ = banks[nd][:, 0:512].rearrange("p (h c) -> p h c", h=2)
                    else:
                        r0 = qb[0][0]
                        d_ = acc[:, pi, :, :].rearrange("p h (j r) -> p h r j", r=16)[:, :, r0:r0 + 4, :]
                        s_ = banks[nd][:, 0:512].rearrange("p (h r j) -> p h r j", h=2, r=4)
                    if gi == 0:
                        P.op("dve", lambda e, d_=d_, s_=s_: e.tensor_copy(out=d_, in_=s_), reads=[f"bank{nd}"], writes=["acc"])
                    else:
                        P.op("dve", lambda e, d_=d_, s_=s_: e.tensor_tensor(out=d_, in0=s_, in1=d_, op=ALU.add),
                             reads=[f"bank{nd}", "acc"], writes=["acc"])

                for n in range(len(steps) + LA):
                    if n < len(steps):
                        emit_qk(n)
                    if n - LA >= 0:
                        emit_pv(n - LA)
            pend_fin[0] = u

        emit_gt(1, ci, 6)
        cblocks = [(1023, 512), (1535, 512), (2047, 2)]
        for s in range(8):
            (wh, whn), (wc, wcn), (wbt, wbn) = W.get(("h", s)), W.get(("c", s)), W.get(("b", s))
            whv, wcv, wbv = [w_[:, :].rearrange("p (k c) -> p k c", k=8) for w_ in (wh, wc, wbt)]
            for bi_, (e0, wd) in enumerate(cblocks):
                b1, b2 = (0, 1) if bi_ % 2 == 0 else (2, 3)
                for k in range(8):
                    mm(banks[b1][:, 0:wd], whv[:, k, :], xnT[:, k, e0:e0 + wd], k == 0, k == 7, [whn, "xnT"], [f"bank{b1}"])
                for k in range(8):
                    mm(banks[b2][:, 0:wd], wcv[:, k, :], xnT[:, k, e0:e0 + wd], k == 0, k == 7, [wcn, "xnT"], [f"bank{b2}"])
                P.op("act", lambda e, b2=b2, wd=wd: e.copy(out=csb[:, 0:wd], in_=banks[b2][:, 0:wd]), reads=[f"bank{b2}"], writes=["csb"])
                uo = e0 - 1023
                P.op("dve", lambda e, b1=b1, wd=wd, uo=uo: e.tensor_tensor(out=ubuf[:, uo:uo + wd], in0=banks[b1][:, 0:wd],
                                                                            in1=csb[:, 0:wd], op=ALU.mult),
                     reads=[f"bank{b1}", "csb"], writes=["ubuf"])
            for (col, vi) in ((0, 0), (1025, 1)):
                P.op("dve", lambda e, col=col, vi=vi: e.tensor_scalar(out=ubuf[:, col:col + 1], in0=ubuf[:, col:col + 1],
                                                                     scalar1=uval[:, vi:vi + 1], scalar2=None, op0=ALU.mult),
                     reads=["ubuf", "uval"], writes=["ubuf"])
            for cb in range(2):
                t0 = cb * 512
                bb = 4 + cb
                for k in range(8):
                    mm(banks[bb][:, :], wbv[:, k, :], xnT[:, k, HALO + t0:HALO + t0 + 512], k == 0, k == 7, [wbn, "xnT"], [f"bank{bb}"])
                ct = ctmp[0]
                P.op("act", lambda e, t0=t0, s=s, ct=ct: e.activation(out=ct[:, :], in_=ubuf[:, t0 + 1:t0 + 513], func=AF.Identity,
                                                                     bias=cbT[:, s:s + 1], scale=cwT[:, s, 1:2]),
                     reads=["ubuf", "smallp"], writes=["ctmp0"])
                P.op("dve", lambda e, t0=t0, s=s, ct=ct: e.scalar_tensor_tensor(out=ct[:, :], in0=ubuf[:, t0:t0 + 512],
                                                                               scalar=cwT[:, s, 0:1], in1=ct[:, :],
                                                                               op0=ALU.mult, op1=ALU.add),
                     reads=["ubuf", "smallp", "ctmp0"], writes=["ctmp0"])
                P.op("dve", lambda e, t0=t0, s=s, ct=ct: e.scalar_tensor_tensor(out=ct[:, :], in0=ubuf[:, t0 + 2:t0 + 514],
                                                                               scalar=cwT[:, s, 2:3], in1=ct[:, :],
                                                                               op0=ALU.mult, op1=ALU.add),
                     reads=["ubuf", "smallp", "ctmp0"], writes=["ctmp0"])
                P.op("dve", lambda e, t0=t0, s=s, ct=ct, bb=bb: e.tensor_tensor(out=cvinT[:, s, t0:t0 + 512], in0=banks[bb][:, :],
                                                                               in1=ct[:, :], op=ALU.mult),
                     reads=[f"bank{bb}", "ctmp0"], writes=["cvinT"])
            for nm in ("h", "c", "b"):
                W.release((nm, s))
            if pend_fin[0] is not None:
                fin_att(pend_fin[0])
                pend_fin[0] = None

        aot = None
        for m in range(8):
            (wga, wgan), (wgc, wgcn) = W.get(("ga", m)), W.get(("gc", m))
            if m % 2 == 0:
                aot = W.get(("ao", m // 2))
            (wco, wcon) = W.get(("co", m))
            wgav, wgcv, wcov = [w_[:, :].rearrange("p (k c) -> p k c", k=8) for w_ in (wga, wgc, wco)]
            waov = aot[0][:, :].rearrange("p (k c) -> p k c", k=8)
            for blk in range(2):
                t0 = blk * 512
                bs = 0 if (m * 2 + blk) % 2 == 0 else 4
                for k in range(8):
                    mm(banks[bs][:, :], wgav[:, k, :], xnT[:, k, HALO + t0:HALO + t0 + 512], k == 0, k == 7, [wgan, "xnT"], [f"bank{bs}"])
                for k in range(8):
                    mm(banks[bs + 1][:, :], wgcv[:, k, :], xnT[:, k, HALO + t0:HALO + t0 + 512], k == 0, k == 7, [wgcn, "xnT"],
                       [f"bank{bs + 1}"])
                for pi in range(4):
                    mm(banks[bs + 2][:, :], waov[:, (m % 2) * 4 + pi, :], attT[:, pi, t0:t0 + 512], pi == 0, pi == 3, [aot[1], "attT"],
                       [f"bank{bs + 2}"])
                for s in range(8):
                    mm(banks[bs + 3][:, :], wcov[:, s, :], cvinT[:, s, t0:t0 + 512], s == 0, s == 7, [wcon, "cvinT"], [f"bank{bs + 3}"])
                P.op("act", lambda e, bs=bs: e.activation(out=sgt[0][:, :], in_=banks[bs][:, :], func=AF.Sigmoid),
                     reads=[f"bank{bs}"], writes=["sgt0"])
                P.op("act", lambda e, bs=bs: e.activation(out=sgt[1][:, :], in_=banks[bs + 1][:, :], func=AF.Sigmoid),
                     reads=[f"bank{bs + 1}"], writes=["sgt1"])
                P.op("dve", lambda e, bs=bs: e.tensor_tensor(out=sgt[0][:, :], in0=banks[bs + 2][:, :], in1=sgt[0][:, :], op=ALU.mult),
                     reads=[f"bank{bs + 2}", "sgt0"], writes=["sgt0"])
                P.op("dve", lambda e, bs=bs: e.tensor_tensor(out=sgt[1][:, :], in0=banks[bs + 3][:, :], in1=sgt[1][:, :], op=ALU.mult),
                     reads=[f"bank{bs + 3}", "sgt1"], writes=["sgt1"])
                P.op("dve", lambda e, m=m, t0=t0: e.tensor_tensor(out=mergedT[:, m, t0:t0 + 512], in0=sgt[0][:, :], in1=sgt[1][:, :],
                                                                  op=ALU.add),
                     reads=["sgt0", "sgt1"], writes=["mergedT"])
            W.release(("ga", m)); W.release(("gc", m)); W.release(("co", m))
            if m % 2 == 1:
                W.release(("ao", m // 2))

        wmo = [W.get(("mo", m)) for m in range(8)]
        n2p = norm_to_T(ci, 8, 0, None, None, xsB, "xsB", 24, xn2T, None, a2T, 3, 0, src_is_dram=False, src_sb=x1, src_sb_name="x1_",
                        dst_names=["xn2T0", "xn2T0", "xn2T1", "xn2T1"])
        n2p.avail = 0
        for tt in range(8):
            xt = xinB[tt % 4]
            xname = f"xinB{tt % 4}"
            P.op("sp", lambda e, xt=xt, tt=tt, ci=ci: e.dma_start(out=xt[:, :], in_=xext[ci, HALO + tt * 128:HALO + tt * 128 + 128, :]),
                 writes=[xname], dma=xname)
            for half in range(2):
                bk = (tt % 2) * 2 + half
                for m in range(8):
                    mm(banks[bk][:, :], mergedT[:, m, tt * 128:(tt + 1) * 128], wmo[m][0][:, half * 512:(half + 1) * 512], m == 0, m == 7,
                       [wmo[m][1], "mergedT"], [f"bank{bk}"])
                hs = slice(half * 512, half * 512 + 512)
                P.op("dve", lambda e, bk=bk, tt=tt, hs=hs: e.tensor_tensor(out=x1[:, tt, hs], in0=banks[bk][:, :], in1=gtb[:, 0, hs],
                                                                          op=ALU.mult),
                     reads=[f"bank{bk}", "gtb0"], writes=[f"x1_{tt}", "x1"])
                P.op("dve", lambda e, xt=xt, tt=tt, hs=hs: e.tensor_tensor(out=x1[:, tt, hs], in0=x1[:, tt, hs], in1=xt[:, hs], op=ALU.add),
                     reads=[f"x1_{tt}", xname], writes=[f"x1_{tt}", "x1"])
            n2p.avail = tt + 1
            n2p.step()
        for m in range(8):
            W.release(("mo", m))
        n2p.run(until=lambda: n2p.groups_done >= 2)
        if ci + 1 < NCH:
            n1p = norm_to_T(ci + 1, EXT // 128, 0, xinC, ["xinC0", "xinC1", "xinC2"], xsC, "xsC", 0, xnT, "xnT", a1T, 0, 0, act_share=0,
                            sq_dve=True, tr_lag=2)
        else:
            n1p = None
        n1_done = [0]

        def n1_steps(target):
            while n1p is not None and not n1p.done and n1_done[0] < target:
                n1p.step()
                n1_done[0] += 1

        def pipes_to_clean():
            for p_ in (n2p, n1p):
                if p_ is not None:
                    p_.hold = True
                    p_.run(until=lambda: p_.clean)

        def pipes_release():
            for p_ in (n2p, n1p):
                if p_ is not None:
                    p_.hold = False

        def fin_A(blk, tl):
            tt = blk * 4 + tl
            xo, xon = x2s[tl % 2], f"x2s{tl % 2}"
            for half in range(2):
                bk = tl * 2 + half
                hs = slice(half * 512, half * 512 + 512)
                P.op("dve", lambda e, bk=bk, xo=xo, hs=hs: e.tensor_tensor(out=xo[:, hs], in0=banks[bk][:, :], in1=gtb[:, 1, hs],
                                                                          op=ALU.mult),
                     reads=[f"bank{bk}", "gtb1"], writes=[xon])
                P.op("dve", lambda e, xo=xo, hs=hs, tt=tt: e.tensor_tensor(out=xo[:, hs], in0=xo[:, hs], in1=x1[:, tt, hs], op=ALU.add),
                     reads=[xon, "x1", f"x1_{tt}"], writes=[xon])

        def fin_S(blk, tl):
            tt = blk * 4 + tl
            xo, xon = x2s[tl % 2], f"x2s{tl % 2}"
            col = 32 + tt
            P.op("act", lambda e, xo=xo, col=col: e.activation(out=xn2T[:, 0:2, 0:512], in_=xo[:, :].rearrange("p (a b) -> p a b", a=2),
                                                                func=AF.Square, accum_out=ss[:, col:col + 1]),
                 reads=[xon, f"ssc{col}"], writes=[f"ssc{col}", "xn2T0"])
            P.op("act", lambda e, col=col: e.activation(out=rs[:, col:col + 1], in_=ss[:, col:col + 1], func=AF.Sqrt,
                                                        bias=epsc[:, 0:1], scale=1.0 / D),
                 reads=[f"ssc{col}", "epsc"], writes=[f"rsc{col}"])

        def fin_B(blk, tl):
            tt = blk * 4 + tl
            xo, xon = x2s[tl % 2], f"x2s{tl % 2}"
            col = 32 + tt
            P.op("dve", lambda e, col=col: e.reciprocal(out=rs[:, col:col + 1], in_=rs[:, col:col + 1]),
                 reads=[f"rsc{col}"], writes=[f"rsc{col}"])
            P.op("dve", lambda e, xo=xo, col=col: e.scalar_tensor_tensor(out=xo[:, :], in0=xo[:, :], scalar=rs[:, col:col + 1],
                                                                        in1=gfb[:, :], op0=ALU.mult, op1=ALU.mult),
                 reads=[xon, f"rsc{col}", "gfb"], writes=[xon])
            P.op("sp", lambda e, xo=xo, tt=tt, ci=ci: e.dma_start(out=y_d[ci, tt * 128:(tt + 1) * 128, :], in_=xo[:, :]),
                 reads=[xon], dma=xon)

        pending = {}
        for blk in range(2):
            t0 = blk * 512
            for f in range(32):
                (w1t, w1n) = W.get(("w1", f))
                w1v = w1t[:, :].rearrange("p (k c) -> p k c", k=8)
                bk = f % 4
                for k in range(8):
                    mm(banks[bk][:, :], w1v[:, k, :], xn2T[:, k, t0:t0 + 512], k == 0, k == 7, [w1n, f"xn2T{blk}"], [f"bank{bk}"])
                st_ = sgt[f % 4]
                stn = f"sgt{f % 4}"
                P.op("act", lambda e, bk=bk, st_=st_: e.activation(out=st_[:, :], in_=banks[bk][:, :], func=AF.Relu),
                     reads=[f"bank{bk}"], writes=[stn])
                P.op("dve", lambda e, bk=bk, st_=st_, f=f: e.tensor_tensor(out=hT[:, f, :], in0=banks[bk][:, :], in1=st_[:, :], op=ALU.mult),
                     reads=[f"bank{bk}", stn], writes=["hT"])
                W.release(("w1", f))
                for fn_ in pending.pop(f, []):
                    fn_()
                if blk == 1 and f == 4:
                    pipes_release()
                if blk == 0 and f == 12 and ci + 1 < NCH:
                    emit_gt(0, ci + 1, 3)
                if f == 27:
                    for p_ in (n2p, n1p):
                        if p_ is not None:
                            p_.hold = True
                if blk == 0:
                    if f < 8 and not n2p.done:
                        n2p.step()
                    else:
                        n1_steps((f - 7) * 11 // 24)
                else:
                    n1_steps(11 + (f + 1) * 21 // 32)
            if blk == 0:
                n2p.run()
            pipes_to_clean()
            for f in range(32):
                (w2t, w2n) = W.get(("w2", f))
                for tl in range(4):
                    for half in range(2):
                        bk = tl * 2 + half
                        mm(banks[bk][:, :], hT[:, f, tl * 128:(tl + 1) * 128], w2t[:, half * 512:(half + 1) * 512], f == 0, f == 31,
                           [w2n, "hT"], [f"bank{bk}"])
                W.release(("w2", f))
                if f % 8 == 7 and n1p is not None and not n1p.done:
                    n1p.step()
            fin_A(blk, 0); fin_S(blk, 0); fin_A(blk, 1); fin_S(blk, 1); fin_B(blk, 0); fin_B(blk, 1)
            if blk == 0:
                pending = {1: [lambda: (fin_A(0, 2), fin_S(0, 2))], 3: [lambda: (fin_A(0, 3), fin_S(0, 3))],
                           5: [lambda: fin_B(0, 2)], 7: [lambda: fin_B(0, 3)]}
            else:
                fin_A(blk, 2); fin_S(blk, 2); fin_A(blk, 3); fin_S(blk, 3); fin_B(blk, 2); fin_B(blk, 3)
                pipes_release()
    counts = P.emit()
    return nc, counts


def _chunks():
    ch = []
    for b in range(4):
        for j in range(4):
            ch.append((0, b, j))
    for b in range(2):
        for j in range(16):
            ch.append((1, b, j))
    return ch


def _lhsT_tile(Wm, cols):
    t = np.ascontiguousarray(Wm[:, cols])
    return t.reshape(8, 128, 128).transpose(1, 0, 2).reshape(128, 1024)


def _build_wall(w_in, w_ao, w_co, w_mo, w1, w2):
    wall = np.zeros((NWT, 128, 1024), np.float32)
    for u in range(2):
        for g in range(3):
            heads = [8 * g + 4 * u + j for j in range(4)]
            x1c = np.concatenate([np.arange(h * 64, h * 64 + 32) for h in heads])
            x2c = x1c + 32
            wall[WIDX[("q1", g, u)]] = _lhsT_tile(w_in, x1c)
            wall[WIDX[("q2", g, u)]] = _lhsT_tile(w_in, x2c)
            wall[WIDX[("k1", g, u)]] = _lhsT_tile(w_in, 1536 + x1c)
            wall[WIDX[("k2", g, u)]] = _lhsT_tile(w_in, 1536 + x2c)
            c0 = 3072 + (8 * g + 4 * u) * 64
            tv = np.ascontiguousarray(w_in[:, c0:c0 + 256]).reshape(8, 128, 256).transpose(1, 0, 2)
            wall[WIDX[("va", g, u)]] = tv[:, 0:4, :].reshape(128, 1024)
            wall[WIDX[("vb", g, u)]] = tv[:, 4:8, :].reshape(128, 1024)
    for s in range(8):
        wall[WIDX[("h", s)]] = _lhsT_tile(w_in, np.arange(4608 + s * 128, 4608 + s * 128 + 128))
        wall[WIDX[("c", s)]] = _lhsT_tile(w_in, np.arange(5632 + s * 128, 5632 + s * 128 + 128))
        wall[WIDX[("b", s)]] = _lhsT_tile(w_in, np.arange(6656 + s * 128, 6656 + s * 128 + 128))
    for m in range(8):
        wall[WIDX[("ga", m)]] = _lhsT_tile(w_in, np.arange(7680 + m * 128, 7680 + m * 128 + 128))
        wall[WIDX[("gc", m)]] = _lhsT_tile(w_in, np.arange(8704 + m * 128, 8704 + m * 128 + 128))
        wall[WIDX[("co", m)]] = _lhsT_tile(w_co, np.arange(m * 128, m * 128 + 128))
        wall[WIDX[("mo", m)]] = w_mo[m * 128:(m + 1) * 128, :]
    for mp in range(4):
        t = np.zeros((128, 8, 128), np.float32)
        for m2 in range(2):
            for pi in range(4):
                m = 2 * mp + m2
                t[:, m2 * 4 + pi, :] = w_ao[pi * 128:(pi + 1) * 128, m * 128:(m + 1) * 128]
        wall[WIDX[("ao", mp)]] = t.reshape(128, 1024)
    for f in range(32):
        wall[WIDX[("w1", f)]] = _lhsT_tile(w1, np.arange(f * 128, f * 128 + 128))
        wall[WIDX[("w2", f)]] = w2[f * 128:(f + 1) * 128, :]
    return wall


def _key_tile_ext():
    ii = np.arange(128)
    rows = []
    for kt in range(9):
        rows.append(960 + 128 * kt + ii)
    for r in range(4):
        for kk in range(3):
            rows.append(768 + 512 * kk + r + 4 * ii)
    for cp in range(8):
        rows.append(cp + 16 * ii)
        rows.append(cp + 8 + 16 * ii)
        rows.append(2048 + cp + 8 * ii)
    return np.stack(rows)


def _constants():
    i = np.arange(128)[:, None]
    j = np.arange(128)[None, :]
    mA = np.where(i >= j, 0.0, NEG)
    mB = np.where(i <= j, 0.0, NEG)
    mC = np.where((i >= 64) & (i <= j + 64), 0.0, NEG)
    mLo = np.where((i % 2 == 0) & (i // 2 <= j), 0.0, NEG)
    mHi = np.where((i % 2 == 1) & ((i - 1) // 2 <= j), 0.0, NEG)
    masks = np.stack([np.stack([m, m], axis=1) for m in (mA, mB, mC, mLo, mHi)], axis=1)
    cst = np.zeros((128, 1536), np.float32)
    cst[:, 0:1280] = masks.reshape(128, 1280)
    cst[:, 1280:1408] = np.eye(128, dtype=np.float32)
    cst[:, 1408:1536] = 1.0
    return cst


_PROGRAM_CACHE = {}


def kernel(x_prompt, x_sample, c_prompt, c_sample, w_ada, b_ada, norm1_g, w_in, conv_w, conv_b,
           w_attn_out, w_conv_out, w_mix_out, norm2_g, w_mlp_in, w_mlp_out, final_norm_g):
    f32 = np.float32
    xs = [np.asarray(x_prompt, f32), np.asarray(x_sample, f32)]
    cs_in = [np.asarray(c_prompt, f32), np.asarray(c_sample, f32)]
    seqlen = [xs[0].shape[1], xs[1].shape[1]]
    chunks = _chunks()
    wall = _build_wall(np.asarray(w_in, f32)[0], np.asarray(w_attn_out, f32)[0], np.asarray(w_conv_out, f32)[0],
                       np.asarray(w_mix_out, f32)[0], np.asarray(w_mlp_in, f32)[0], np.asarray(w_mlp_out, f32)[0])
    wa = np.asarray(w_ada, f32)[0]
    wada = np.stack([_lhsT_tile(wa, np.arange(et * 128, et * 128 + 128)) for et in range(48)])
    smallp = np.zeros((128, 96), f32)
    smallp[:, 0:48] = np.asarray(b_ada, f32)[0].reshape(48, 128).T
    smallp[:, 48:56] = np.asarray(norm1_g, f32)[0].reshape(8, 128).T
    smallp[:, 56:64] = np.asarray(norm2_g, f32)[0].reshape(8, 128).T
    cw = np.asarray(conv_w, f32)[0]
    smallp[:, 64:88] = cw.reshape(3, 8, 128).transpose(2, 1, 0).reshape(128, 24)
    smallp[:, 88:96] = np.asarray(conv_b, f32)[0].reshape(8, 128).T
    gfb = np.ascontiguousarray(np.broadcast_to(np.asarray(final_norm_g, f32)[None, :], (128, 1024)))
    cst = _constants()
    ktext = _key_tile_ext()
    inv = (1.0 / (np.float32(10000.0) ** (np.arange(32, dtype=f32) / np.float32(32)))).astype(f32)
    invp = inv[np.arange(128) % 32]

    in_maps = []
    for core in range(NCORES):
        xext = np.zeros((NCH, EXT, D), f32)
        cT = np.zeros((128, 8, NCH), f32)
        cst_cs = np.zeros((NCH, 128, 2, EXT), f32)
        kb = np.zeros((NCH, 128, NKT), f32)
        uv = np.zeros((NCH, 128, 2), f32)
        for ci in range(NCH):
            which, b, j = chunks[core * NCH + ci]
            S = seqlen[which]
            p0 = j * T - HALO
            lo, hi = max(p0, 0), min(p0 + EXT, S)
            xext[ci, lo - p0:hi - p0, :] = xs[which][b, lo:hi, :]
            cT[:, :, ci] = cs_in[which][b].reshape(8, 128).T
            pos = (p0 + np.arange(EXT)).astype(f32)
            ang = (pos[None, :] * invp[:, None]).astype(f32)
            cst_cs[ci, :, 0, :] = np.cos(ang)
            cst_cs[ci, :, 1, :] = np.sin(ang)
            kpos = p0 + ktext
            kb[ci] = np.where((kpos >= 0) & (kpos < S), 0.0, NEG).T
            uv[ci, :, 0] = 1.0 if j * T - 1 >= 0 else 0.0
            uv[ci, :, 1] = 1.0 if j * T + T < S else 0.0
        in_maps.append(dict(xext=xext, cT=cT, cs=cst_cs, kbias=kb, uval=uv, wall=wall, wada=wada, smallp=smallp,
                            gfb=gfb, cst=cst))

    if "nc" not in _PROGRAM_CACHE:
        _PROGRAM_CACHE["nc"] = build_program()[0]
    nc = _PROGRAM_CACHE["nc"]
    res = run_bass_kernel_spmd(nc, in_maps, core_ids=list(range(NCORES)))
    y_p = np.zeros_like(xs[0])
    y_s = np.zeros_like(xs[1])
    outs = [y_p, y_s]
    for core in range(NCORES):
        y = res.results[core]["y"]
        for ci in range(NCH):
            which, b, j = chunks[core * NCH + ci]
            outs[which][b, j * T:(j + 1) * T, :] = y[ci]
    return (y_p, y_s)
```

```python
import numpy as np
import concourse.bass as bass
import concourse.mybir as mybir
from concourse.bass_utils import run_bass_kernel_spmd

F32 = mybir.dt.float32
BF16 = mybir.dt.bfloat16
ALU = mybir.AluOpType
AF = mybir.ActivationFunctionType

D = 1024
T = 1024
HALO = 1024
EXT = T + 2 * HALO
NCH = 6
NCORES = 8
EPS = 1e-6
NEG = -30000.0
NKT = 45
ENGS = ("pe", "act", "dve", "pool", "sp")
RING = 10


class Prog:
    def __init__(self, nc):
        self.nc = nc
        self.ops = []
        self.last_write = {}
        self.reads_since = {}
        self.alias = {}

    def add_alias(self, a, b):
        self.alias.setdefault(a, set()).add(b)
        self.alias.setdefault(b, set()).add(a)

    def op(self, eng, fn, reads=(), writes=(), dma=None):
        idx = len(self.ops)
        raw, other = set(), set()
        for r in reads:
            if r in self.last_write:
                raw.add(self.last_write[r])
        for w in writes:
            for n in {w} | self.alias.get(w, set()):
                if n in self.last_write:
                    other.add(self.last_write[n])
                for x in self.reads_since.get(n, ()):
                    other.add(x)
        self.ops.append(dict(eng=eng, fn=fn, raw=raw, other=other - raw, dma=dma))
        for r in reads:
            self.reads_since.setdefault(r, []).append(idx)
        for w in writes:
            self.last_write[w] = idx
            self.reads_since[w] = []
        return idx

    def emit(self):
        nc, ops = self.nc, self.ops
        for o in ops:
            keep = set()
            for d in o["raw"]:
                keep.add(d)
            for d in o["other"]:
                od = ops[d]
                if od["eng"] != o["eng"] or od["dma"] is not None:
                    keep.add(d)
            o["deps"] = keep
        signal = [False] * len(ops)
        for o in ops:
            for d in o["deps"]:
                signal[d] = True
        eng_sem = {e: nc.alloc_semaphore(f"S_{e}") for e in ENGS}
        dma_keys = sorted({o["dma"] for o in ops if o["dma"] is not None})
        dma_sem = {k: nc.alloc_semaphore(f"D_{k}") for k in dma_keys}
        cnt = {e: 0 for e in ENGS}
        dcnt = {k: 0 for k in dma_keys}
        for i, o in enumerate(ops):
            if o["dma"] is not None:
                dcnt[o["dma"]] += 16
                o["sig"] = ("d", o["dma"], dcnt[o["dma"]])
            elif signal[i]:
                cnt[o["eng"]] += 1
                o["sig"] = ("e", o["eng"], cnt[o["eng"]])
            else:
                o["sig"] = None
        per_eng = {e: [i for i, o in enumerate(ops) if o["eng"] == e] for e in ENGS}

        def run_engine(ename, eng):
            waited = {}
            for i in per_eng[ename]:
                o = ops[i]
                need = {}
                for d in o["deps"]:
                    kind, key, val = ops[d]["sig"]
                    if kind == "e" and key == ename:
                        if d not in o["raw"]:
                            continue
                    k = (kind, key)
                    if need.get(k, 0) < val:
                        need[k] = val
                for k, val in need.items():
                    if waited.get(k, 0) >= val:
                        continue
                    eng.wait_ge(eng_sem[k[1]] if k[0] == "e" else dma_sem[k[1]], val)
                    waited[k] = val
                ins = o["fn"](eng)
                sv = o["sig"]
                if sv is not None:
                    if sv[0] == "d":
                        ins.then_inc(dma_sem[sv[1]], 16)
                    else:
                        ins.then_inc(eng_sem[sv[1]], 1)
            if ename == "sp":
                for k, v in dcnt.items():
                    if v > 0 and waited.get(("d", k), 0) < v:
                        eng.wait_ge(dma_sem[k], v)

        with nc.Block() as block:
            @block.tensor
            def _(e):
                run_engine("pe", e)

            @block.scalar
            def _(e):
                run_engine("act", e)

            @block.vector
            def _(e):
                run_engine("dve", e)

            @block.gpsimd
            def _(e):
                run_engine("pool", e)

            @block.sync
            def _(e):
                run_engine("sp", e)
        return {e: len(per_eng[e]) for e in ENGS}


def wall_index():
    idx, n = {}, 0
    for u in range(2):
        for g in range(3):
            for nm in ("q1", "q2", "k1", "k2", "va", "vb"):
                idx[(nm, g, u)] = n
                n += 1
    for s in range(8):
        for nm in ("h", "c", "b"):
            idx[(nm, s)] = n
            n += 1
    for m in range(8):
        idx[("ga", m)] = n; n += 1
        idx[("gc", m)] = n; n += 1
        if m % 2 == 0:
            idx[("ao", m // 2)] = n; n += 1
        idx[("co", m)] = n; n += 1
    for m in range(8):
        idx[("mo", m)] = n; n += 1
    for f in range(32):
        idx[("w1", f)] = n; n += 1
    for f in range(32):
        idx[("w2", f)] = n; n += 1
    return idx, n


WIDX, NWT = wall_index()


G_ORDER = (2, 1, 0)


def chunk_tile_order():
    o = []
    for u in range(2):
        for g in G_ORDER:
            for nm in ("q1", "q2", "k1", "k2", "va", "vb"):
                o.append((nm, g, u))
    for s in range(8):
        for nm in ("h", "c", "b"):
            o.append((nm, s))
    for m in range(8):
        o.append(("ga", m)); o.append(("gc", m))
        if m % 2 == 0:
            o.append(("ao", m // 2))
        o.append(("co", m))
    for m in range(8):
        o.append(("mo", m))
    for blk in range(2):
        for f in range(32):
            o.append(("w1", f))
        for f in range(32):
            o.append(("w2", f))
    return o


class WRing:
    def __init__(self, P, wall_ap, slots, plan):
        self.P, self.wall, self.slots = P, wall_ap, slots
        self.plan = list(plan)
        self.pos = 0
        self.loaded = {}
        self.inuse = 0
        self.free = list(range(len(slots)))
        self.R = len(slots)

    def _load_next(self):
        key = self.plan[self.pos]
        self.pos += 1
        s = self.free.pop(0)
        t = WIDX[key]
        dst = self.slots[s]
        self.P.op("pool", lambda e, dst=dst, t=t: e.dma_start(out=dst[:, :], in_=self.wall[t, :, :]),
                  writes=[f"ring{s}"], dma=f"ring{s}")
        self.loaded[key] = s
        self.inuse += 1

    def pump(self):
        while self.pos < len(self.plan) and self.inuse < self.R:
            self._load_next()

    def get(self, key):
        self.pump()
        guard = 0
        while key not in self.loaded:
            assert self.pos < len(self.plan), key
            self._load_next()
            guard += 1
            assert guard < 1000
        s = self.loaded[key]
        return self.slots[s], f"ring{s}"

    def release(self, key):
        s = self.loaded.pop(key)
        self.free.append(s)
        self.inuse -= 1
        self.pump()


def build_program():
    nc = bass.Bass("TRN2", target_bir_lowering=False)
    dt = nc.dram_tensor
    xext = dt("xext", [NCH, EXT, D], F32, kind="ExternalInput").ap()
    cT_d = dt("cT", [128, 8, NCH], F32, kind="ExternalInput").ap()
    cs_d = dt("cs", [NCH, 128, 2, EXT], F32, kind="ExternalInput").ap()
    kb_d = dt("kbias", [NCH, 128, NKT], F32, kind="ExternalInput").ap()
    uv_d = dt("uval", [NCH, 128, 2], F32, kind="ExternalInput").ap()
    wall = dt("wall", [NWT, 128, 1024], F32, kind="ExternalInput").ap()
    wada = dt("wada", [48, 128, 1024], F32, kind="ExternalInput").ap()
    sp_d = dt("smallp", [128, 48 + 8 + 8 + 24 + 8], F32, kind="ExternalInput").ap()
    gfb_d = dt("gfb", [128, 1024], F32, kind="ExternalInput").ap()
    cst_d = dt("cst", [128, 5 * 256 + 128 + 128], F32, kind="ExternalInput").ap()
    y_d = dt("y", [NCH, T, D], F32, kind="ExternalOutput").ap()

    cur = [17408]

    def A(name, shape, dtype, at=None):
        esz = 4 if dtype == F32 else 2
        n = 1
        for s in shape[1:]:
            n *= s
        nbytes = n * esz
        if at is None:
            off = cur[0]
            cur[0] += (nbytes + 63) // 64 * 64
        else:
            off = at
        assert off + nbytes <= 229376, (name, off, nbytes)
        return nc.alloc_sbuf_tensor_at(name, list(shape), dtype, offset=off)

    ring = [A(f"ring{i}", [128, 1024], BF16) for i in range(RING)]
    xnT = A("xnT", [128, 8, EXT], BF16)
    xnT_off = cur[0] - 8 * EXT * 2
    attT = A("attT", [128, 4, T], BF16)
    gtb = A("gtb", [128, 2, D], F32)
    gfb = A("gfb_sb", [128, D], F32)
    smallp = A("smallp_sb", [128, 96], F32)
    cst_bf = A("cst_bf", [128, 5 * 256 + 128 + 128], BF16)
    ident_f = A("ident_f", [128, 128], F32)
    ones_f = A("ones_f", [128, 128], F32)
    dg = A("dg", [128, 2, 128], F32)
    cT = A("cT_sb", [128, 8, NCH], F32)
    scT = A("scT", [128, 8, NCH], BF16)
    modT = A("modT", [128, 48, NCH], F32)
    a1T = A("a1T", [128, 8, NCH], F32)
    a2T = A("a2T", [128, 8, NCH], F32)
    tmp6 = A("tmp6", [128, 8, NCH], F32)
    kbias = A("kbias_sb", [128, NKT], F32)
    uval = A("uval_sb", [128, 2], F32)
    ss = A("ss", [128, 48], F32)
    rs = A("rs", [128, 48], F32)
    epsc = A("epsc", [128, 16], F32)
    xinC2 = A("xinC2", [128, D], F32)
    R1 = cur[0]
    R1_SIZE = 104 * 1024
    assert R1 + R1_SIZE <= 229376, R1
    NWA = 12
    wada_sb = [A(f"wada{i}", [128, 8, 128], BF16, at=R1 + i * 2048) for i in range(NWA)]
    o = R1
    cs_sb = A("cs_sb", [128, 2, EXT], F32, at=o); o += 24 * 1024
    qpad = [A(f"qpad{i}", [128, 2, T], BF16, at=o + i * 4096) for i in range(2)]; o += 8 * 1024
    kpair = [A(f"kpair{i}", [128, EXT], BF16, at=o + i * 6144) for i in range(2)]; o += 12 * 1024
    vbuf = A("vbuf", [128, 32, 2, 192], BF16, at=o); o += 24 * 1024
    rt = [A(f"rt{i}", [128, 512], F32, at=o + i * 2048) for i in range(4)]
    rtw = A("rtw", [128, 1024], F32, at=o); o += 8 * 1024
    acc = A("acc", [128, 2, 2, T], F32, at=o); o += 16 * 1024
    ptb = [A(f"pt{i}", [128, 2, 2, 128], BF16, at=o + i * 1024) for i in range(4)]; o += 4 * 1024
    qpad2 = [A(f"qpadB{i}", [128, 2, T], BF16, at=o + i * 4096) for i in range(2)]; o += 8 * 1024
    assert o <= R1 + R1_SIZE
    xinC = [A(f"xinC{i}", [128, D], F32, at=R1 + 96 * 1024 + i * 4096) for i in range(2)] + [xinC2]
    xsC = A("xsC", [128, 4, D], BF16, at=xnT_off + 8 * EXT * 2)
    x1 = A("x1", [128, 8, D], F32, at=R1)
    xn2T = A("xn2T", [128, 8, T], BF16, at=R1 + 32 * 1024)
    mergedT = A("mergedT", [128, 8, T], BF16, at=R1 + 48 * 1024)
    cvinT = A("cvinT", [128, 8, T], BF16, at=R1)
    ubuf = A("ubuf", [128, 1032], F32, at=R1 + 16 * 1024)
    ctmp = [A(f"ctmp{i}", [128, 512], F32, at=R1 + 16 * 1024 + 4224 + i * 2048) for i in range(1)]
    csb = A("csb", [128, 512], F32, at=R1 + 16 * 1024 + 4224 + 2048)
    hT = A("hT", [128, 32, 512], BF16, at=R1 + 48 * 1024)
    sgt = [A(f"sgt{i}", [128, 512], F32, at=R1 + 80 * 1024 + i * 2048) for i in range(4)]
    xinB = [A(f"xinB{i}", [128, D], F32, at=xnT_off + i * 4096) for i in range(4)]
    xsB = A("xsB", [128, 4, D], BF16, at=xnT_off + 16 * 1024)
    x2s = [A(f"x2s{i}", [128, D], F32, at=R1 + 88 * 1024 + i * 4096) for i in range(2)]

    banks = [nc.alloc_psum_tensor(f"bank{i}", [128, 512], F32) for i in range(8)]
    banks_bf = [b.bitcast(BF16) for b in banks]

    P = Prog(nc)
    R1_ATT = ["cs0", "cs1", "cs2", "qpad0", "qpad1", "qpadB0", "qpadB1", "kpair0", "kpair1", "vbuf", "rt0", "rt1", "rt2", "rt3", "acc",
              "pt0a", "pt1a", "pt2a", "pt3a", "pt0b", "pt1b", "pt2b", "pt3b"]
    R1_POST = ["x1", "xn2T0", "xn2T1", "mergedT", "cvinT", "ubuf", "ctmp0", "csb", "hT", "sgt0", "sgt1", "sgt2", "sgt3"]
    for a in R1_ATT:
        for b in R1_POST:
            P.add_alias(a, b)
    for a in ["hT"]:
        for b in ["mergedT"]:
            P.add_alias(a, b)
    for b in ["cvinT", "ubuf", "ctmp0", "csb"]:
        P.add_alias("x1", b)
    for a in ["sgt0", "sgt1", "sgt2", "sgt3"]:
        for b in ["ubuf", "ctmp0", "csb"]:
            P.add_alias(a, b)
    for b in ["xinB0", "xinB1", "xinB2", "xinB3", "xsB0", "xsB1", "xsB2", "xsB3"]:
        P.add_alias("xnT", b)
    for i in range(NWA):
        for b in ["cs0", "cs1", "cs2", "x1", "qpad0", "qpad1"]:
            P.add_alias(f"wada{i}", b)
    for i in range(2):
        for b in ["acc", "pt0a", "pt1a", "pt2a", "pt3a", "pt0b", "pt1b", "pt2b", "pt3b"]:
            P.add_alias(f"x2s{i}", b)
        for b in ["qpadB0", "qpadB1"]:
            P.add_alias(f"xinC{i}", b)
    for i in range(4):
        P.add_alias(f"xsC{i}", "attT")

    plan = []
    for ci in range(NCH):
        plan += chunk_tile_order()
    W = WRing(P, wall, ring, plan)

    maskb = cst_bf[:, 0:1280].rearrange("p (t h q) -> p t h q", t=5, h=2)
    ident = cst_bf[:, 1280:1408]
    ones_bf = cst_bf[:, 1408:1536]
    b_adaT = smallp[:, 0:48]
    g1T = smallp[:, 48:56]
    g2T = smallp[:, 56:64]
    cwT = smallp[:, 64:88].rearrange("p (s t) -> p s t", t=3)
    cbT = smallp[:, 88:96]

    def mm(out, lhsT, rhs, start, stop, reads, writes):
        P.op("pe", lambda e: e.matmul(out, lhsT=lhsT, rhs=rhs, start=start, stop=stop), reads=reads, writes=writes)

    P.op("pool", lambda e: e.dma_start(out=cst_bf[:, :], in_=cst_d), writes=["cst"], dma="cst")
    P.op("sp", lambda e: e.dma_start(out=smallp[:, :], in_=sp_d), writes=["smallp"], dma="smallp")
    P.op("sp", lambda e: e.dma_start(out=gfb[:, :], in_=gfb_d), writes=["gfb"], dma="gfb")
    P.op("sp", lambda e: e.dma_start(out=cT[:, :, :], in_=cT_d), writes=["cT"], dma="cT")
    P.op("sp", lambda e: e.dma_start(out=ident_f[:, :], in_=cst_d[:, 1280:1408]), writes=["ident_f"], dma="ident_f")
    P.op("sp", lambda e: e.dma_start(out=ones_f[:, :], in_=cst_d[:, 1408:1536]), writes=["ones_f"], dma="ones_f")
    W.pump()
    P.op("dve", lambda e: e.memset(epsc[:, :], EPS), writes=["epsc"])
    P.op("act", lambda e: e.activation(out=scT[:, :, :], in_=cT[:, :, :], func=AF.Silu), reads=["cT"], writes=["scT"])
    def emit_a(dst, dn, which, gT):
        P.op("dve", lambda e: e.tensor_scalar(out=tmp6[:, :, :], in0=modT[:, which * 8:which * 8 + 8, :], scalar1=1.0,
                                              scalar2=None, op0=ALU.add),
             reads=[f"modT{which}"], writes=["tmp6"])
        for cj in range(NCH):
            P.op("dve", lambda e, cj=cj: e.tensor_tensor(out=dst[:, :, cj], in0=tmp6[:, :, cj], in1=gT, op=ALU.mult),
                 reads=["tmp6", "smallp"], writes=[dn])

    def adaln_et(et):
        wb = wada_sb[et % NWA]
        wn = f"wada{et % NWA}"
        P.op("pool", lambda e: e.dma_start(out=wb[:, :, :], in_=wada[et].rearrange("p (k c) -> p k c", k=8)),
             writes=[wn], dma=wn)
        bk = banks[et % 2]
        for k in range(8):
            mm(bk[:, 0:NCH], wb[:, k, :], scT[:, k, :], k == 0, k == 7, [wn, "scT"], [f"bank{et % 2}"])
        P.op("dve", lambda e: e.tensor_scalar(out=modT[:, et, :], in0=bk[:, 0:NCH], scalar1=b_adaT[:, et:et + 1],
                                              scalar2=None, op0=ALU.add),
             reads=[f"bank{et % 2}", "smallp"], writes=[f"modT{et // 8}"])
        if et == 15:
            emit_a(a1T, "a1T", 1, g1T)
        if et == 39:
            emit_a(a2T, "a2T", 4, g2T)

    P.op("dve", lambda e: e.memset(ss[:, :], 0.0), writes=[f"ssc{c_}" for c_ in range(48)])
    for et in range(16):
        adaln_et(et)

    def norm_to_T(ci, ntiles, src_rows, xin, xin_names, xs, xs_name, ss_base, dstT, dst_name_, aT, sh_which, dst_col0,
                  src_is_dram=True, src_sb=None, src_sb_name=None, act_share=2, dst_names=None, sq_dve=False, tr_lag=1):
        def load(tt):
            if src_is_dram and tt < ntiles:
                xt = xin[tt % len(xin)]
                xname = xin_names[tt % len(xin)]
                r0 = src_rows + tt * 128
                P.op("sp", lambda e, xt=xt, r0=r0: e.dma_start(out=xt[:, :], in_=xext[ci, r0:r0 + 128, :]),
                     writes=[xname], dma=xname)

        def stage1(tt):
            tl = tt % 4
            if src_is_dram:
                xt = xin[tt % len(xin)]
                xname = xin_names[tt % len(xin)]
                src = xt[:, :]
            else:
                src = src_sb[:, tt, :]
                xname = src_sb_name + str(tt)
            col = ss_base + tt
            if sq_dve and tt % 2 == 1:
                P.op("dve", lambda e, src=src, col=col, tl=tl: e.scalar_tensor_tensor(out=xs[:, tl, :], in0=src, scalar=1.0, in1=src,
                                                                                    op0=ALU.mult, op1=ALU.mult, accum_out=ss[:, col:col + 1]),
                     reads=[xname, f"ssc{col}"], writes=[f"ssc{col}", xs_name + str(tl)])
            else:
                P.op("act", lambda e, src=src, col=col, tl=tl: e.activation(out=xs[:, tl, :], in_=src, func=AF.Square,
                                                                            accum_out=ss[:, col:col + 1]),
                     reads=[xname, f"ssc{col}"], writes=[f"ssc{col}", xs_name + str(tl)])
            P.op("act", lambda e, col=col: e.activation(out=rs[:, col:col + 1], in_=ss[:, col:col + 1], func=AF.Sqrt,
                                                        bias=epsc[:, 0:1], scale=1.0 / D),
                 reads=[f"ssc{col}", "epsc"], writes=[f"rsc{col}"])
            P.op("dve", lambda e, col=col: e.reciprocal(out=rs[:, col:col + 1], in_=rs[:, col:col + 1]),
                 reads=[f"rsc{col}"], writes=[f"rsc{col}"])
            return src, xname, col

        def stage2(tt, src, xname, col):
            tl = tt % 4
            P.op("act", lambda e, src=src, col=col, tl=tl: e.activation(out=xs[:, tl, :], in_=src, func=AF.Identity,
                                                                        scale=rs[:, col:col + 1]),
                 reads=[xname, f"rsc{col}"], writes=[xs_name + str(tl)])

        def transposes(tt):
            tl = tt % 4
            g_ = tt // 2
            b0 = 4 + 2 * (g_ % 2)
            for k in range(8):
                bi = b0 + k // 4
                c0 = (k % 4) * 256 + (tt % 2) * 128
                P.op("pe", lambda e, bi=bi, c0=c0, tl=tl, k=k: e.transpose(banks_bf[bi][:, c0:c0 + 128],
                                                                             xs[:, tl, k * 128:(k + 1) * 128], ident),
                     reads=[xs_name + str(tl), "cst"], writes=[f"bank{bi}"])

        def evac_one(g_, k):
            e0 = dst_col0 + g_ * 256
            dst_name = dst_names[g_] if dst_names is not None else dst_name_
            bi = 4 + 2 * (g_ % 2) + k // 4
            c0 = (k % 4) * 256
            sh = modT[:, sh_which * 8 + k, ci:ci + 1]
            sc = aT[:, k, ci:ci + 1]
            aT_name = "a1T" if sh_which == 0 else "a2T"
            if act_share == 0 or k % act_share != act_share - 1:
                P.op("dve", lambda e: e.tensor_scalar(
                    out=dstT[:, k, e0:e0 + 256], in0=banks_bf[bi][:, c0:c0 + 256], scalar1=sc, scalar2=sh,
                    op0=ALU.mult, op1=ALU.add), reads=[f"bank{bi}", f"modT{sh_which}", aT_name], writes=[dst_name])
            else:
                P.op("act", lambda e: e.activation(
                    out=dstT[:, k, e0:e0 + 256], in_=banks_bf[bi][:, c0:c0 + 256], func=AF.Identity, bias=sh, scale=sc),
                    reads=[f"bank{bi}", f"modT{sh_which}", aT_name], writes=[dst_name])

        class NormPipe:
            def __init__(self):
                self.n_load = self.n_s1 = self.n_s2 = self.n_tr = 0
                self.info = {}
                self.evq = []
                self.hold = False
                self.groups_done = 0
                self.nsl = len(xin) if src_is_dram else 10 ** 6
                self.avail = ntiles

            @property
            def done(self):
                return self.n_tr == ntiles and not self.evq

            @property
            def clean(self):
                return (not self.evq) and self.n_tr % 2 == 0

            def step(self):
                if src_is_dram:
                    while self.n_load < ntiles and self.n_load <= self.n_s1 + 1 and self.n_load - self.n_s2 < self.nsl:
                        load(self.n_load)
                        self.n_load += 1
                if self.n_s1 < min(ntiles, self.avail) and (not src_is_dram or self.n_s1 < self.n_load) and self.n_s1 - self.n_tr < 4:
                    self.info[self.n_s1] = stage1(self.n_s1)
                    self.n_s1 += 1
                if self.n_s2 < self.n_s1 - 1 or (self.n_s1 == ntiles and self.n_s2 < ntiles):
                    stage2(self.n_s2, *self.info[self.n_s2])
                    self.n_s2 += 1
                for _ in range(4):
                    if self.evq:
                        g_, k = self.evq.pop(0)
                        evac_one(g_, k)
                        if k == 7:
                            self.groups_done += 1
                can_tr = self.n_tr < self.n_s2 - tr_lag or (self.n_s2 == ntiles and self.n_tr < ntiles)
                if can_tr and not (self.hold and self.n_tr % 2 == 0):
                    g_ = self.n_tr // 2
                    if all(q[0] != g_ - 2 for q in self.evq):
                        transposes(self.n_tr)
                        self.n_tr += 1
                        if self.n_tr % 2 == 0:
                            self.evq += [(g_, k) for k in range(8)]

            def run(self, nsteps=None, until=None):
                c_ = 0
                while not self.done and (nsteps is None or c_ < nsteps) and not (until is not None and until()):
                    self.step()
                    c_ += 1

        return NormPipe()

    for ci in range(NCH):
        P.op("sp", lambda e, ci=ci: e.dma_start(out=kbias[:, :], in_=kb_d[ci]), writes=["kbias"], dma="kbias")
        P.op("sp", lambda e, ci=ci: e.dma_start(out=uval[:, :], in_=uv_d[ci]), writes=["uval"], dma="uval")
        def emit_gt(wi, cj, bi):
            which = (2, 5)[wi]
            for k in range(8):
                sl = k % 2
                P.op("dve", lambda e, k=k, sl=sl: e.tensor_scalar(
                    out=dg[:, sl, :], in0=ident_f[:, :], scalar1=modT[:, which * 8 + k, cj:cj + 1], scalar2=None, op0=ALU.mult),
                    reads=["ident_f", f"modT{which}"], writes=[f"dg{sl}"])
                mm(banks[bi][:, (k % 4) * 128:(k % 4) * 128 + 128], ones_f[:, :], dg[:, sl, :], True, True,
                   ["ones_f", f"dg{sl}"], [f"bank{bi}"])
                if k % 4 == 3:
                    h0 = (k // 4) * 512
                    P.op("act", lambda e, h0=h0: e.copy(out=gtb[:, wi, h0:h0 + 512], in_=banks[bi][:, :]),
                         reads=[f"bank{bi}"], writes=[f"gtb{wi}"])
        if ci == 0:
            n1p = norm_to_T(0, EXT // 128, 0, xinC, ["xinC0", "xinC1", "xinC2"], xsC, "xsC", 0, xnT, "xnT", a1T, 0, 0, act_share=0)
            et_ = 16
            while not n1p.done or et_ < 48:
                if not n1p.done:
                    n1p.step()
                if et_ < 48:
                    adaln_et(et_)
                    et_ += 1
            emit_gt(0, 0, 2)
        if n1p is not None:
            n1p.hold = False
            n1p.run()
        P.op("dve", lambda e: e.memset(ss[:, :], 0.0), writes=[f"ssc{c_}" for c_ in range(48)])
        for th_ in (1, 0, 2):
            P.op("sp", lambda e, ci=ci, th_=th_: e.dma_start(out=cs_sb[:, :, th_ * 1024:(th_ + 1) * 1024],
                                                             in_=cs_d[ci][:, :, th_ * 1024:(th_ + 1) * 1024]),
                 writes=[f"cs{th_}"], dma=f"cs{th_}")
        for i in range(2):
            P.op("pool", lambda e, i=i: e.memset(qpad[i][:, :, :], 0.0), writes=[f"qpad{i}"])
        P.op("pool", lambda e: e.memset(vbuf[:, :, :, 64:128], 1.0), writes=["vbuf"])

        def proj_rope(key1, key2, blocks, place):
            w1t, w1n = W.get(key1)
            w2t, w2n = W.get(key2)
            w1v = w1t[:, :].rearrange("p (k c) -> p k c", k=8)
            w2v = w2t[:, :].rearrange("p (k c) -> p k c", k=8)
            for bi_, (e0, wd) in enumerate(blocks):
                b1, b2 = (0, 1) if bi_ % 2 == 0 else (2, 3)
                for k in range(8):
                    mm(banks[b1][:, 0:wd], w1v[:, k, :], xnT[:, k, e0:e0 + wd], k == 0, k == 7, [w1n, "xnT"], [f"bank{b1}"])
                for k in range(8):
                    mm(banks[b2][:, 0:wd], w2v[:, k, :], xnT[:, k, e0:e0 + wd], k == 0, k == 7, [w2n, "xnT"], [f"bank{b2}"])
                cos = cs_sb[:, 0, e0:e0 + wd]
                sin = cs_sb[:, 1, e0:e0 + wd]
                for (ti, bk, tab) in ((0, b1, cos), (1, b2, sin), (2, b2, cos), (3, b1, sin)):
                    P.op("dve", lambda e, ti=ti, bk=bk, tab=tab, wd=wd: e.tensor_tensor(out=rt[ti][:, 0:wd], in0=banks[bk][:, 0:wd],
                                                                                        in1=tab, op=ALU.mult),
                         reads=[f"bank{bk}"] + [f"cs{t_}" for t_ in range(e0 // 1024, (e0 + wd - 1) // 1024 + 1)], writes=[f"rt{ti}"])
                place(e0, wd)
                yield bi_
            W.release(key1)
            W.release(key2)

        def place_q(e0, wd):
            t0 = e0 - HALO
            for j in range(4):
                pi, var = j // 2, j % 2
                r = slice(32 * j, 32 * j + 32)
                d1 = slice(32 * var, 32 * var + 32)
                d2 = slice(64 + 32 * var, 64 + 32 * var + 32)
                P.op("dve", lambda e, pi=pi, var=var, r=r, d1=d1: e.tensor_tensor(
                    out=qpad[pi][d1, var, t0:t0 + wd], in0=rt[0][r, 0:wd], in1=rt[1][r, 0:wd], op=ALU.subtract),
                    reads=["rt0", "rt1"], writes=[f"qpad{pi}"])
                P.op("dve", lambda e, pi=pi, var=var, r=r, d2=d2: e.tensor_tensor(
                    out=qpad[pi][d2, var, t0:t0 + wd], in0=rt[2][r, 0:wd], in1=rt[3][r, 0:wd], op=ALU.add),
                    reads=["rt2", "rt3"], writes=[f"qpad{pi}"])

        def place_k(e0, wd):
            for pi in range(2):
                r = slice(64 * pi, 64 * pi + 64)
                P.op("dve", lambda e, pi=pi, r=r: e.tensor_tensor(out=kpair[pi][0:64, e0:e0 + wd], in0=rt[0][r, 0:wd],
                                                                  in1=rt[1][r, 0:wd], op=ALU.subtract),
                     reads=["rt0", "rt1"], writes=[f"kpair{pi}"])
                P.op("dve", lambda e, pi=pi, r=r: e.tensor_tensor(out=kpair[pi][64:128, e0:e0 + wd], in0=rt[2][r, 0:wd],
                                                                  in1=rt[3][r, 0:wd], op=ALU.add),
                     reads=["rt2", "rt3"], writes=[f"kpair{pi}"])

        def key_tiles(g):
            def strided(es, st):
                return lambda v: v[:, es:es + 127 * st + 1:st]
            if g == 0:
                return [strided(960 + 128 * kt, 1) for kt in range(9)], 0
            if g == 1:
                return [strided(768 + 512 * kk + r, 4) for r in range(4) for kk in range(3)], 9
            out = []
            for cp in range(8):
                out.append(strided(cp, 16))
                out.append(strided(cp + 8, 16))
                out.append(strided(2048 + cp, 8))
            return out, 21

        def fin_att(u):
            for pi in range(2):
                P.op("act", lambda e, pi=pi: e.activation(out=rtw[0:64, :], in_=acc[64:128, pi, 0, :], func=AF.Ln),
                     reads=["acc"], writes=["rt0", "rt1"])
                P.op("act", lambda e, pi=pi: e.activation(out=rtw[64:128, :], in_=acc[0:64, pi, 1, :], func=AF.Ln),
                     reads=["acc"], writes=["rt0", "rt1"])
                P.op("act", lambda e: e.activation(out=rtw[:, :], in_=rtw[:, :], func=AF.Exp, scale=-1.0),
                     reads=["rt0", "rt1"], writes=["rt0", "rt1"])
                P.op("dve", lambda e, pi=pi, u=u: e.tensor_tensor(out=attT[0:64, 2 * u + pi, :], in0=acc[0:64, pi, 0, :], in1=rtw[0:64, :],
                                                                 op=ALU.mult), reads=["acc", "rt0", "rt1"], writes=["attT"])
                P.op("dve", lambda e, pi=pi, u=u: e.tensor_tensor(out=attT[64:128, 2 * u + pi, :], in0=acc[64:128, pi, 1, :], in1=rtw[64:128, :],
                                                                 op=ALU.mult), reads=["acc", "rt0", "rt1"], writes=["attT"])

        pend_fin = [None]
        for u in range(2):
            for gi, g in enumerate(G_ORDER):
                kblocks = {0: [(960, 512), (1472, 512), (1984, 128)], 1: [(768 + 512 * i, 512) for i in range(3)],
                           2: [(512 * i, 512) for i in range(6)]}[g]
                ktl, kb0 = key_tiles(g)
                for _ in proj_rope(("q1", g, u), ("q2", g, u), [(HALO, 512), (HALO + 512, 512)], place_q):
                    pass
                kgen = proj_rope(("k1", g, u), ("k2", g, u), kblocks, place_k)
                wv = []
                for nm_ in ("va", "vb"):
                    t_, n_ = W.get((nm_, g, u))
                    wv.append((t_[:, :].rearrange("p (k c) -> p k c", k=4), n_))
                vgroups = list(range(0, len(ktl), 2))
                per_blk = max(1, len(vgroups) // (len(kblocks) + 1))
                vi = 0
                kdone = False
                while vi < len(vgroups) or not kdone:
                    for _ in range(per_blk):
                        if vi >= len(vgroups):
                            break
                        kt0 = vgroups[vi]
                        vi += 1
                        bi = 4 + (kt0 // 2) % 2
                        nk = min(2, len(ktl) - kt0)
                        for kl in range(nk):
                            kap = ktl[kt0 + kl]
                            c0 = kl * 256
                            for k in range(8):
                                mm(banks[bi][:, c0:c0 + 256], kap(xnT[:, k, :]), wv[k // 4][0][:, k % 4, :], k == 0, k == 7,
                                   [wv[k // 4][1], "xnT"], [f"bank{bi}"])
                        for hh in range(2):
                            P.op("act", lambda e, bi=bi, kt0=kt0, nk=nk, hh=hh: e.copy(
                                out=vbuf[:, kt0:kt0 + nk, :, hh * 128:hh * 128 + 64],
                                in_=banks[bi][:, 0:256 * nk].rearrange("p (a b h d) -> p a b h d", a=nk, b=2, h=2)[:, :, :, hh, :]),
                                reads=[f"bank{bi}"], writes=["vbuf"])
                    if not kdone:
                        try:
                            next(kgen)
                        except StopIteration:
                            kdone = True
                W.release(("va", g, u))
                W.release(("vb", g, u))
                if pend_fin[0] is not None:
                    fin_att(pend_fin[0])
                    pend_fin[0] = None

                if g == 0:
                    qblocks = [[(128 * qt, 1, 128, qt, qt + 1, 0, 1) for qt in (2 * b, 2 * b + 1)] for b in range(4)]
                elif g == 1:
                    qblocks = [[(4 * i0 + r, 4, 128, r * 3 + i0 // 128, r * 3 + i0 // 128 + 1, 0, 1) for i0 in (0, 128)]
                               for r in range(4)]
                else:
                    qblocks = [[(r, 16, 64, (r % 8) * 3 + r // 8, (r % 8) * 3 + 2, 0, 3 + r // 8) for r in range(4 * b, 4 * b + 4)]
                               for b in range(4)]
                steps = []
                for qb_i, qb in enumerate(qblocks):
                    for pi in range(2):
                        for qi, qt_ in enumerate(qb):
                            steps.append((qb_i, pi, qi, qt_, qi == len(qb) - 1, qb))
                LA = 3

                def emit_qk(n):
                    qb_i, pi, qi, (qs, qst, nq, ta, tb, ma, mb), last, qb = steps[n]
                    sb_i = (0, 1, 2, 6)[n % 4]
                    pt = ptb[n % 4]
                    ptn = f"pt{n % 4}"
                    S = banks[sb_i]
                    qap = qpad[pi][:, :, qs:qs + (nq - 1) * qst + 1:qst]
                    sos = []
                    for ti, (tid, mtype) in enumerate(((ta, ma), (tb, mb))):
                        kap = ktl[tid]
                        so = S[:, ti * 2 * nq:(ti + 1) * 2 * nq].rearrange("p (h q) -> p h q", h=2)
                        sos.append(so)
                        mm(so, ident, maskb[:, mtype, :, 0:nq], True, False, ["cst"], [f"bank{sb_i}"])
                        mm(so, kap(kpair[pi][:, :]), qap, False, True, [f"kpair{pi}", f"qpad{pi}"], [f"bank{sb_i}"])
                    for ti, (tid, mtype) in enumerate(((ta, ma), (tb, mb))):
                        so = sos[ti]
                        kcol = kb0 + tid
                        P.op("act", lambda e, so=so, pt=pt, ti=ti, nq=nq, kcol=kcol: e.activation(
                            out=pt[:, ti, :, 0:nq], in_=so, func=AF.Exp,
                            bias=kbias[:, kcol:kcol + 1], scale=0.125),
                            reads=[f"bank{sb_i}", "kbias"], writes=[ptn + "ab"[ti]])

                def emit_pv(n):
                    qb_i, pi, qi, (qs, qst, nq, ta, tb, ma, mb), last, qb = steps[n]
                    pt = ptb[n % 4]
                    ptn = f"pt{n % 4}"
                    nset = (qb_i * 2 + pi) % 3
                    nd = 3 + nset
                    qc = qi * nq
                    for hh in range(2):
                        lA = vbuf[:, ta, pi, hh * 64:hh * 64 + 128]
                        lB = vbuf[:, tb, pi, hh * 64:hh * 64 + 128]
                        oo = banks[nd][:, hh * 256 + qc:hh * 256 + qc + nq]
                        mm(oo, lA, pt[:, 0, hh, 0:nq], True, False, ["vbuf", ptn + "a"], [f"bank{nd}"])
                        mm(oo, lB, pt[:, 1, hh, 0:nq], False, True, ["vbuf", ptn + "b"], [f"bank{nd}"])
                    if not last:
                        return
                    if g == 0:
                        q0 = qb[0][0]
                        d_ = acc[:, pi, :, q0:q0 + 256]
                        s_ = banks[nd][:, 0:512].rearrange("p (h c) -> p h c", h=2)
                    elif g == 1:
                        r = qb[0][0]
                        d_ = acc[:, pi, :, r:r + 1021:4]
                        s_ = banks[nd][:, 0:512].rearrange("p (h c) -> p h c", h=2)
                    else:
                        r0 = qb[0][0]
                        d_ = acc[:, pi, :, :].rearrange("p h (j r) -> p h r j", r=16)[:, :, r0:r0 + 4, :]
                        s_ = banks[nd][:, 0:512].rearrange("p (h r j) -> p h r j", h=2, r=4)
                    if gi == 0:
                        P.op("dve", lambda e, d_=d_, s_=s_: e.tensor_copy(out=d_, in_=s_), reads=[f"bank{nd}"], writes=["acc"])
                    else:
                        P.op("dve", lambda e, d_=d_, s_=s_: e.tensor_tensor(out=d_, in0=s_, in1=d_, op=ALU.add),
                             reads=[f"bank{nd}", "acc"], writes=["acc"])

                for n in range(len(steps) + LA):
                    if n < len(steps):
                        emit_qk(n)
                    if n - LA >= 0:
                        emit_pv(n - LA)
            pend_fin[0] = u

        emit_gt(1, ci, 6)
        cblocks = [(1023, 512), (1535, 512), (2047, 2)]
        for s in range(8):
            (wh, whn), (wc, wcn), (wbt, wbn) = W.get(("h", s)), W.get(("c", s)), W.get(("b", s))
            whv, wcv, wbv = [w_[:, :].rearrange("p (k c) -> p k c", k=8) for w_ in (wh, wc, wbt)]
            for bi_, (e0, wd) in enumerate(cblocks):
                b1, b2 = (0, 1) if bi_ % 2 == 0 else (2, 3)
                for k in range(8):
                    mm(banks[b1][:, 0:wd], whv[:, k, :], xnT[:, k, e0:e0 + wd], k == 0, k == 7, [whn, "xnT"], [f"bank{b1}"])
                for k in range(8):
                    mm(banks[b2][:, 0:wd], wcv[:, k, :], xnT[:, k, e0:e0 + wd], k == 0, k == 7, [wcn, "xnT"], [f"bank{b2}"])
                P.op("act", lambda e, b2=b2, wd=wd: e.copy(out=csb[:, 0:wd], in_=banks[b2][:, 0:wd]), reads=[f"bank{b2}"], writes=["csb"])
                uo = e0 - 1023
                P.op("dve", lambda e, b1=b1, wd=wd, uo=uo: e.tensor_tensor(out=ubuf[:, uo:uo + wd], in0=banks[b1][:, 0:wd],
                                                                            in1=csb[:, 0:wd], op=ALU.mult),
                     reads=[f"bank{b1}", "csb"], writes=["ubuf"])
            for (col, vi) in ((0, 0), (1025, 1)):
                P.op("dve", lambda e, col=col, vi=vi: e.tensor_scalar(out=ubuf[:, col:col + 1], in0=ubuf[:, col:col + 1],
                                                                     scalar1=uval[:, vi:vi + 1], scalar2=None, op0=ALU.mult),
                     reads=["ubuf", "uval"], writes=["ubuf"])
            for cb in range(2):
                t0 = cb * 512
                bb = 4 + cb
                for k in range(8):
                    mm(banks[bb][:, :], wbv[:, k, :], xnT[:, k, HALO + t0:HALO + t0 + 512], k == 0, k == 7, [wbn, "xnT"], [f"bank{bb}"])
                ct = ctmp[0]
                P.op("act", lambda e, t0=t0, s=s, ct=ct: e.activation(out=ct[:, :], in_=ubuf[:, t0 + 1:t0 + 513], func=AF.Identity,
                                                                     bias=cbT[:, s:s + 1], scale=cwT[:, s, 1:2]),
                     reads=["ubuf", "smallp"], writes=["ctmp0"])
                P.op("dve", lambda e, t0=t0, s=s, ct=ct: e.scalar_tensor_tensor(out=ct[:, :], in0=ubuf[:, t0:t0 + 512],
                                                                               scalar=cwT[:, s, 0:1], in1=ct[:, :],
                                                                               op0=ALU.mult, op1=ALU.add),
                     reads=["ubuf", "smallp", "ctmp0"], writes=["ctmp0"])
                P.op("dve", lambda e, t0=t0, s=s, ct=ct: e.scalar_tensor_tensor(out=ct[:, :], in0=ubuf[:, t0 + 2:t0 + 514],
                                                                               scalar=cwT[:, s, 2:3], in1=ct[:, :],
                                                                               op0=ALU.mult, op1=ALU.add),
                     reads=["ubuf", "smallp", "ctmp0"], writes=["ctmp0"])
                P.op("dve", lambda e, t0=t0, s=s, ct=ct, bb=bb: e.tensor_tensor(out=cvinT[:, s, t0:t0 + 512], in0=banks[bb][:, :],
                                                                               in1=ct[:, :], op=ALU.mult),
                     reads=[f"bank{bb}", "ctmp0"], writes=["cvinT"])
            for nm in ("h", "c", "b"):
                W.release((nm, s))
            if pend_fin[0] is not None:
                fin_att(pend_fin[0])
                pend_fin[0] = None

        aot = None
        for m in range(8):
            (wga, wgan), (wgc, wgcn) = W.get(("ga", m)), W.get(("gc", m))
            if m % 2 == 0:
                aot = W.get(("ao", m // 2))
            (wco, wcon) = W.get(("co", m))
            wgav, wgcv, wcov = [w_[:, :].rearrange("p (k c) -> p k c", k=8) for w_ in (wga, wgc, wco)]
            waov = aot[0][:, :].rearrange("p (k c) -> p k c", k=8)
            for blk in range(2):
                t0 = blk * 512
                bs = 0 if (m * 2 + blk) % 2 == 0 else 4
                for k in range(8):
                    mm(banks[bs][:, :], wgav[:, k, :], xnT[:, k, HALO + t0:HALO + t0 + 512], k == 0, k == 7, [wgan, "xnT"], [f"bank{bs}"])
                for k in range(8):
                    mm(banks[bs + 1][:, :], wgcv[:, k, :], xnT[:, k, HALO + t0:HALO + t0 + 512], k == 0, k == 7, [wgcn, "xnT"],
                       [f"bank{bs + 1}"])
                for pi in range(4):
                    mm(banks[bs + 2][:, :], waov[:, (m % 2) * 4 + pi, :], attT[:, pi, t0:t0 + 512], pi == 0, pi == 3, [aot[1], "attT"],
                       [f"bank{bs + 2}"])
                for s in range(8):
                    mm(banks[bs + 3][:, :], wcov[:, s, :], cvinT[:, s, t0:t0 + 512], s == 0, s == 7, [wcon, "cvinT"], [f"bank{bs + 3}"])
                P.op("act", lambda e, bs=bs: e.activation(out=sgt[0][:, :], in_=banks[bs][:, :], func=AF.Sigmoid),
                     reads=[f"bank{bs}"], writes=["sgt0"])
                P.op("act", lambda e, bs=bs: e.activation(out=sgt[1][:, :], in_=banks[bs + 1][:, :], func=AF.Sigmoid),
                     reads=[f"bank{bs + 1}"], writes=["sgt1"])
                P.op("dve", lambda e, bs=bs: e.tensor_tensor(out=sgt[0][:, :], in0=banks[bs + 2][:, :], in1=sgt[0][:, :], op=ALU.mult),
                     reads=[f"bank{bs + 2}", "sgt0"], writes=["sgt0"])
                P.op("dve", lambda e, bs=bs: e.tensor_tensor(out=sgt[1][:, :], in0=banks[bs + 3][:, :], in1=sgt[1][:, :], op=ALU.mult),
                     reads=[f"bank{bs + 3}", "sgt1"], writes=["sgt1"])
                P.op("dve", lambda e, m=m, t0=t0: e.tensor_tensor(out=mergedT[:, m, t0:t0 + 512], in0=sgt[0][:, :], in1=sgt[1][:, :],
                                                                  op=ALU.add),
                     reads=["sgt0", "sgt1"], writes=["mergedT"])
            W.release(("ga", m)); W.release(("gc", m)); W.release(("co", m))
            if m % 2 == 1:
                W.release(("ao", m // 2))

        wmo = [W.get(("mo", m)) for m in range(8)]
        n2p = norm_to_T(ci, 8, 0, None, None, xsB, "xsB", 24, xn2T, None, a2T, 3, 0, src_is_dram=False, src_sb=x1, src_sb_name="x1_",
                        dst_names=["xn2T0", "xn2T0", "xn2T1", "xn2T1"])
        n2p.avail = 0
        for tt in range(8):
            xt = xinB[tt % 4]
            xname = f"xinB{tt % 4}"
            P.op("sp", lambda e, xt=xt, tt=tt, ci=ci: e.dma_start(out=xt[:, :], in_=xext[ci, HALO + tt * 128:HALO + tt * 128 + 128, :]),
                 writes=[xname], dma=xname)
            for half in range(2):
                bk = (tt % 2) * 2 + half
                for m in range(8):
                    mm(banks[bk][:, :], mergedT[:, m, tt * 128:(tt + 1) * 128], wmo[m][0][:, half * 512:(half + 1) * 512], m == 0, m == 7,
                       [wmo[m][1], "mergedT"], [f"bank{bk}"])
                hs = slice(half * 512, half * 512 + 512)
                P.op("dve", lambda e, bk=bk, tt=tt, hs=hs: e.tensor_tensor(out=x1[:, tt, hs], in0=banks[bk][:, :], in1=gtb[:, 0, hs],
                                                                          op=ALU.mult),
                     reads=[f"bank{bk}", "gtb0"], writes=[f"x1_{tt}", "x1"])
                P.op("dve", lambda e, xt=xt, tt=tt, hs=hs: e.tensor_tensor(out=x1[:, tt, hs], in0=x1[:, tt, hs], in1=xt[:, hs], op=ALU.add),
                     reads=[f"x1_{tt}", xname], writes=[f"x1_{tt}", "x1"])
            n2p.avail = tt + 1
            n2p.step()
        for m in range(8):
            W.release(("mo", m))
        n2p.run(until=lambda: n2p.groups_done >= 2)
        if ci + 1 < NCH:
            n1p = norm_to_T(ci + 1, EXT // 128, 0, xinC, ["xinC0", "xinC1", "xinC2"], xsC, "xsC", 0, xnT, "xnT", a1T, 0, 0, act_share=0,
                            sq_dve=True, tr_lag=2)
        else:
            n1p = None
        n1_done = [0]

        def n1_steps(target):
            while n1p is not None and not n1p.done and n1_done[0] < target:
                n1p.step()
                n1_done[0] += 1

        def pipes_to_clean():
            for p_ in (n2p, n1p):
                if p_ is not None:
                    p_.hold = True
                    p_.run(until=lambda: p_.clean)

        def pipes_release():
            for p_ in (n2p, n1p):
                if p_ is not None:
                    p_.hold = False

        def fin_A(blk, tl):
            tt = blk * 4 + tl
            xo, xon = x2s[tl % 2], f"x2s{tl % 2}"
            for half in range(2):
                bk = tl * 2 + half
                hs = slice(half * 512, half * 512 + 512)
                P.op("dve", lambda e, bk=bk, xo=xo, hs=hs: e.tensor_tensor(out=xo[:, hs], in0=banks[bk][:, :], in1=gtb[:, 1, hs],
                                                                          op=ALU.mult),
                     reads=[f"bank{bk}", "gtb1"], writes=[xon])
                P.op("dve", lambda e, xo=xo, hs=hs, tt=tt: e.tensor_tensor(out=xo[:, hs], in0=xo[:, hs], in1=x1[:, tt, hs], op=ALU.add),
                     reads=[xon, "x1", f"x1_{tt}"], writes=[xon])

        def fin_S(blk, tl):
            tt = blk * 4 + tl
            xo, xon = x2s[tl % 2], f"x2s{tl % 2}"
            col = 32 + tt
            P.op("act", lambda e, xo=xo, col=col: e.activation(out=xn2T[:, 0:2, 0:512], in_=xo[:, :].rearrange("p (a b) -> p a b", a=2),
                                                                func=AF.Square, accum_out=ss[:, col:col + 1]),
                 reads=[xon, f"ssc{col}"], writes=[f"ssc{col}", "xn2T0"])
            P.op("act", lambda e, col=col: e.activation(out=rs[:, col:col + 1], in_=ss[:, col:col + 1], func=AF.Sqrt,
                                                        bias=epsc[:, 0:1], scale=1.0 / D),
                 reads=[f"ssc{col}", "epsc"], writes=[f"rsc{col}"])

        def fin_B(blk, tl):
            tt = blk * 4 + tl
            xo, xon = x2s[tl % 2], f"x2s{tl % 2}"
            col = 32 + tt
            P.op("dve", lambda e, col=col: e.reciprocal(out=rs[:, col:col + 1], in_=rs[:, col:col + 1]),
                 reads=[f"rsc{col}"], writes=[f"rsc{col}"])
            P.op("dve", lambda e, xo=xo, col=col: e.scalar_tensor_tensor(out=xo[:, :], in0=xo[:, :], scalar=rs[:, col:col + 1],
                                                                        in1=gfb[:, :], op0=ALU.mult, op1=ALU.mult),
                 reads=[xon, f"rsc{col}", "gfb"], writes=[xon])
            P.op("sp", lambda e, xo=xo, tt=tt, ci=ci: e.dma_start(out=y_d[ci, tt * 128:(tt + 1) * 128, :], in_=xo[:, :]),
                 reads=[xon], dma=xon)

        pending = {}
        for blk in range(2):
            t0 = blk * 512
            for f in range(32):
                (w1t, w1n) = W.get(("w1", f))
                w1v = w1t[:, :].rearrange("p (k c) -> p k c", k=8)
                bk = f % 4
                for k in range(8):
                    mm(banks[bk][:, :], w1v[:, k, :], xn2T[:, k, t0:t0 + 512], k == 0, k == 7, [w1n, f"xn2T{blk}"], [f"bank{bk}"])
                st_ = sgt[f % 4]
                stn = f"sgt{f % 4}"
                P.op("act", lambda e, bk=bk, st_=st_: e.activation(out=st_[:, :], in_=banks[bk][:, :], func=AF.Relu),
                     reads=[f"bank{bk}"], writes=[stn])
                P.op("dve", lambda e, bk=bk, st_=st_, f=f: e.tensor_tensor(out=hT[:, f, :], in0=banks[bk][:, :], in1=st_[:, :], op=ALU.mult),
                     reads=[f"bank{bk}", stn], writes=["hT"])
                W.release(("w1", f))
                for fn_ in pending.pop(f, []):
                    fn_()
                if blk == 1 and f == 4:
                    pipes_release()
                if blk == 0 and f == 12 and ci + 1 < NCH:
                    emit_gt(0, ci + 1, 3)
                if f == 27:
                    for p_ in (n2p, n1p):
                        if p_ is not None:
                            p_.hold = True
                if blk == 0:
                    if f < 8 and not n2p.done:
                        n2p.step()
                    else:
                        n1_steps((f - 7) * 18 // 24)
                else:
                    n1_steps(18 + (f + 1) * 14 // 32)
            if blk == 0:
                n2p.run()
            pipes_to_clean()
            for f in range(32):
                (w2t, w2n) = W.get(("w2", f))
                for tl in range(4):
                    for half in range(2):
                        bk = tl * 2 + half
                        mm(banks[bk][:, :], hT[:, f, tl * 128:(tl + 1) * 128], w2t[:, half * 512:(half + 1) * 512], f == 0, f == 31,
                           [w2n, "hT"], [f"bank{bk}"])
                W.release(("w2", f))
                if f % 8 == 7 and n1p is not None and not n1p.done:
                    n1p.step()
            fin_A(blk, 0); fin_S(blk, 0); fin_A(blk, 1); fin_S(blk, 1); fin_B(blk, 0); fin_B(blk, 1)
            if blk == 0:
                pending = {1: [lambda: (fin_A(0, 2), fin_S(0, 2))], 3: [lambda: (fin_A(0, 3), fin_S(0, 3))],
                           5: [lambda: fin_B(0, 2)], 7: [lambda: fin_B(0, 3)]}
            else:
                fin_A(blk, 2); fin_S(blk, 2); fin_A(blk, 3); fin_S(blk, 3); fin_B(blk, 2); fin_B(blk, 3)
                pipes_release()
    counts = P.emit()
    return nc, counts


def _chunks():
    ch = []
    for b in range(4):
        for j in range(4):
            ch.append((0, b, j))
    for b in range(2):
        for j in range(16):
            ch.append((1, b, j))
    return ch


def _lhsT_tile(Wm, cols):
    t = np.ascontiguousarray(Wm[:, cols])
    return t.reshape(8, 128, 128).transpose(1, 0, 2).reshape(128, 1024)


def _build_wall(w_in, w_ao, w_co, w_mo, w1, w2):
    wall = np.zeros((NWT, 128, 1024), np.float32)
    for u in range(2):
        for g in range(3):
            heads = [8 * g + 4 * u + j for j in range(4)]
            x1c = np.concatenate([np.arange(h * 64, h * 64 + 32) for h in heads])
            x2c = x1c + 32
            wall[WIDX[("q1", g, u)]] = _lhsT_tile(w_in, x1c)
            wall[WIDX[("q2", g, u)]] = _lhsT_tile(w_in, x2c)
            wall[WIDX[("k1", g, u)]] = _lhsT_tile(w_in, 1536 + x1c)
            wall[WIDX[("k2", g, u)]] = _lhsT_tile(w_in, 1536 + x2c)
            c0 = 3072 + (8 * g + 4 * u) * 64
            tv = np.ascontiguousarray(w_in[:, c0:c0 + 256]).reshape(8, 128, 256).transpose(1, 0, 2)
            wall[WIDX[("va", g, u)]] = tv[:, 0:4, :].reshape(128, 1024)
            wall[WIDX[("vb", g, u)]] = tv[:, 4:8, :].reshape(128, 1024)
    for s in range(8):
        wall[WIDX[("h", s)]] = _lhsT_tile(w_in, np.arange(4608 + s * 128, 4608 + s * 128 + 128))
        wall[WIDX[("c", s)]] = _lhsT_tile(w_in, np.arange(5632 + s * 128, 5632 + s * 128 + 128))
        wall[WIDX[("b", s)]] = _lhsT_tile(w_in, np.arange(6656 + s * 128, 6656 + s * 128 + 128))
    for m in range(8):
        wall[WIDX[("ga", m)]] = _lhsT_tile(w_in, np.arange(7680 + m * 128, 7680 + m * 128 + 128))
        wall[WIDX[("gc", m)]] = _lhsT_tile(w_in, np.arange(8704 + m * 128, 8704 + m * 128 + 128))
        wall[WIDX[("co", m)]] = _lhsT_tile(w_co, np.arange(m * 128, m * 128 + 128))
        wall[WIDX[("mo", m)]] = w_mo[m * 128:(m + 1) * 128, :]
    for mp in range(4):
        t = np.zeros((128, 8, 128), np.float32)
        for m2 in range(2):
            for pi in range(4):
                m = 2 * mp + m2
                t[:, m2 * 4 + pi, :] = w_ao[pi * 128:(pi + 1) * 128, m * 128:(m + 1) * 128]
        wall[WIDX[("ao", mp)]] = t.reshape(128, 1024)
    for f in range(32):
        wall[WIDX[("w1", f)]] = _lhsT_tile(w1, np.arange(f * 128, f * 128 + 128))
        wall[WIDX[("w2", f)]] = w2[f * 128:(f + 1) * 128, :]
    return wall


def _key_tile_ext():
    ii = np.arange(128)
    rows = []
    for kt in range(9):
        rows.append(960 + 128 * kt + ii)
    for r in range(4):
        for kk in range(3):
            rows.append(768 + 512 * kk + r + 4 * ii)
    for cp in range(8):
        rows.append(cp + 16 * ii)
        rows.append(cp + 8 + 16 * ii)
        rows.append(2048 + cp + 8 * ii)
    return np.stack(rows)


def _constants():
    i = np.arange(128)[:, None]
    j = np.arange(128)[None, :]
    mA = np.where(i >= j, 0.0, NEG)
    mB = np.where(i <= j, 0.0, NEG)
    mC = np.where((i >= 64) & (i <= j + 64), 0.0, NEG)
    mLo = np.where((i % 2 == 0) & (i // 2 <= j), 0.0, NEG)
    mHi = np.where((i % 2 == 1) & ((i - 1) // 2 <= j), 0.0, NEG)
    masks = np.stack([np.stack([m, m], axis=1) for m in (mA, mB, mC, mLo, mHi)], axis=1)
    cst = np.zeros((128, 1536), np.float32)
    cst[:, 0:1280] = masks.reshape(128, 1280)
    cst[:, 1280:1408] = np.eye(128, dtype=np.float32)
    cst[:, 1408:1536] = 1.0
    return cst


_PROGRAM_CACHE = {}


def kernel(x_prompt, x_sample, c_prompt, c_sample, w_ada, b_ada, norm1_g, w_in, conv_w, conv_b,
           w_attn_out, w_conv_out, w_mix_out, norm2_g, w_mlp_in, w_mlp_out, final_norm_g):
    f32 = np.float32
    xs = [np.asarray(x_prompt, f32), np.asarray(x_sample, f32)]
    cs_in = [np.asarray(c_prompt, f32), np.asarray(c_sample, f32)]
    seqlen = [xs[0].shape[1], xs[1].shape[1]]
    chunks = _chunks()
    wall = _build_wall(np.asarray(w_in, f32)[0], np.asarray(w_attn_out, f32)[0], np.asarray(w_conv_out, f32)[0],
                       np.asarray(w_mix_out, f32)[0], np.asarray(w_mlp_in, f32)[0], np.asarray(w_mlp_out, f32)[0])
    wa = np.asarray(w_ada, f32)[0]
    wada = np.stack([_lhsT_tile(wa, np.arange(et * 128, et * 128 + 128)) for et in range(48)])
    smallp = np.zeros((128, 96), f32)
    smallp[:, 0:48] = np.asarray(b_ada, f32)[0].reshape(48, 128).T
    smallp[:, 48:56] = np.asarray(norm1_g, f32)[0].reshape(8, 128).T
    smallp[:, 56:64] = np.asarray(norm2_g, f32)[0].reshape(8, 128).T
    cw = np.asarray(conv_w, f32)[0]
    smallp[:, 64:88] = cw.reshape(3, 8, 128).transpose(2, 1, 0).reshape(128, 24)
    smallp[:, 88:96] = np.asarray(conv_b, f32)[0].reshape(8, 128).T
    gfb = np.ascontiguousarray(np.broadcast_to(np.asarray(final_norm_g, f32)[None, :], (128, 1024)))
    cst = _constants()
    ktext = _key_tile_ext()
    inv = (1.0 / (np.float32(10000.0) ** (np.arange(32, dtype=f32) / np.float32(32)))).astype(f32)
    invp = inv[np.arange(128) % 32]

    in_maps = []
    for core in range(NCORES):
        xext = np.zeros((NCH, EXT, D), f32)
        cT = np.zeros((128, 8, NCH), f32)
        cst_cs = np.zeros((NCH, 128, 2, EXT), f32)
        kb = np.zeros((NCH, 128, NKT), f32)
        uv = np.zeros((NCH, 128, 2), f32)
        for ci in range(NCH):
            which, b, j = chunks[core * NCH + ci]
            S = seqlen[which]
            p0 = j * T - HALO
            lo, hi = max(p0, 0), min(p0 + EXT, S)
            xext[ci, lo - p0:hi - p0, :] = xs[which][b, lo:hi, :]
            cT[:, :, ci] = cs_in[which][b].reshape(8, 128).T
            pos = (p0 + np.arange(EXT)).astype(f32)
            ang = (pos[None, :] * invp[:, None]).astype(f32)
            cst_cs[ci, :, 0, :] = np.cos(ang)
            cst_cs[ci, :, 1, :] = np.sin(ang)
            kpos = p0 + ktext
            kb[ci] = np.where((kpos >= 0) & (kpos < S), 0.0, NEG).T
            uv[ci, :, 0] = 1.0 if j * T - 1 >= 0 else 0.0
            uv[ci, :, 1] = 1.0 if j * T + T < S else 0.0
        in_maps.append(dict(xext=xext, cT=cT, cs=cst_cs, kbias=kb, uval=uv, wall=wall, wada=wada, smallp=smallp,
                            gfb=gfb, cst=cst))

    if "nc" not in _PROGRAM_CACHE:
        _PROGRAM_CACHE["nc"] = build_program()[0]
    nc = _PROGRAM_CACHE["nc"]
    res = run_bass_kernel_spmd(nc, in_maps, core_ids=list(range(NCORES)))
    y_p = np.zeros_like(xs[0])
    y_s = np.zeros_like(xs[1])
    outs = [y_p, y_s]
    for core in range(NCORES):
        y = res.results[core]["y"]
        for ci in range(NCH):
            which, b, j = chunks[core * NCH + ci]
            outs[which][b, j * T:(j + 1) * T, :] = y[ci]
    return (y_p, y_s)
```

```python
import numpy as np
import concourse.bass as bass
import concourse.mybir as mybir
from concourse.bass_utils import run_bass_kernel_spmd

F32 = mybir.dt.float32
BF16 = mybir.dt.bfloat16
ALU = mybir.AluOpType
AF = mybir.ActivationFunctionType

D = 1024
T = 1024
HALO = 1024
EXT = T + 2 * HALO
NCH = 6
NCORES = 8
EPS = 1e-6
NEG = -30000.0
NKT = 45
ENGS = ("pe", "act", "dve", "pool", "sp")
RING = 10


class Prog:
    def __init__(self, nc):
        self.nc = nc
        self.ops = []
        self.last_write = {}
        self.reads_since = {}
        self.alias = {}

    def add_alias(self, a, b):
        self.alias.setdefault(a, set()).add(b)
        self.alias.setdefault(b, set()).add(a)

    def op(self, eng, fn, reads=(), writes=(), dma=None):
        idx = len(self.ops)
        raw, other = set(), set()
        for r in reads:
            if r in self.last_write:
                raw.add(self.last_write[r])
        for w in writes:
            for n in {w} | self.alias.get(w, set()):
                if n in self.last_write:
                    other.add(self.last_write[n])
                for x in self.reads_since.get(n, ()):
                    other.add(x)
        self.ops.append(dict(eng=eng, fn=fn, raw=raw, other=other - raw, dma=dma))
        for r in reads:
            self.reads_since.setdefault(r, []).append(idx)
        for w in writes:
            self.last_write[w] = idx
            self.reads_since[w] = []
        return idx

    def emit(self):
        nc, ops = self.nc, self.ops
        for o in ops:
            keep = set()
            for d in o["raw"]:
                keep.add(d)
            for d in o["other"]:
                od = ops[d]
                if od["eng"] != o["eng"] or od["dma"] is not None:
                    keep.add(d)
            o["deps"] = keep
        signal = [False] * len(ops)
        for o in ops:
            for d in o["deps"]:
                signal[d] = True
        eng_sem = {e: nc.alloc_semaphore(f"S_{e}") for e in ENGS}
        dma_keys = sorted({o["dma"] for o in ops if o["dma"] is not None})
        dma_sem = {k: nc.alloc_semaphore(f"D_{k}") for k in dma_keys}
        cnt = {e: 0 for e in ENGS}
        dcnt = {k: 0 for k in dma_keys}
        for i, o in enumerate(ops):
            if o["dma"] is not None:
                dcnt[o["dma"]] += 16
                o["sig"] = ("d", o["dma"], dcnt[o["dma"]])
            elif signal[i]:
                cnt[o["eng"]] += 1
                o["sig"] = ("e", o["eng"], cnt[o["eng"]])
            else:
                o["sig"] = None
        per_eng = {e: [i for i, o in enumerate(ops) if o["eng"] == e] for e in ENGS}

        def run_engine(ename, eng):
            waited = {}
            for i in per_eng[ename]:
                o = ops[i]
                need = {}
                for d in o["deps"]:
                    kind, key, val = ops[d]["sig"]
                    if kind == "e" and key == ename:
                        if d not in o["raw"]:
                            continue
                    k = (kind, key)
                    if need.get(k, 0) < val:
                        need[k] = val
                for k, val in need.items():
                    if waited.get(k, 0) >= val:
                        continue
                    eng.wait_ge(eng_sem[k[1]] if k[0] == "e" else dma_sem[k[1]], val)
                    waited[k] = val
                ins = o["fn"](eng)
                sv = o["sig"]
                if sv is not None:
                    if sv[0] == "d":
                        ins.then_inc(dma_sem[sv[1]], 16)
                    else:
                        ins.then_inc(eng_sem[sv[1]], 1)
            if ename == "sp":
                for k, v in dcnt.items():
                    if v > 0 and waited.get(("d", k), 0) < v:
                        eng.wait_ge(dma_sem[k], v)

        with nc.Block() as block:
            @block.tensor
            def _(e):
                run_engine("pe", e)

            @block.scalar
            def _(e):
                run_engine("act", e)

            @block.vector
            def _(e):
                run_engine("dve", e)

            @block.gpsimd
            def _(e):
                run_engine("pool", e)

            @block.sync
            def _(e):
                run_engine("sp", e)
        return {e: len(per_eng[e]) for e in ENGS}


def wall_index():
    idx, n = {}, 0
    for u in range(2):
        for g in range(3):
            for nm in ("q1", "q2", "k1", "k2", "va", "vb"):
                idx[(nm, g, u)] = n
                n += 1
    for s in range(8):
        for nm in ("h", "c", "b"):
            idx[(nm, s)] = n
            n += 1
    for m in range(8):
        idx[("ga", m)] = n; n += 1
        idx[("gc", m)] = n; n += 1
        if m % 2 == 0:
            idx[("ao", m // 2)] = n; n += 1
        idx[("co", m)] = n; n += 1
    for m in range(8):
        idx[("mo", m)] = n; n += 1
    for f in range(32):
        idx[("w1", f)] = n; n += 1
    for f in range(32):
        idx[("w2", f)] = n; n += 1
    return idx, n


WIDX, NWT = wall_index()


G_ORDER = (2, 1, 0)


def chunk_tile_order():
    o = []
    for u in range(2):
        for g in G_ORDER:
            for nm in ("q1", "q2", "k1", "k2", "va", "vb"):
                o.append((nm, g, u))
    for s in range(8):
        for nm in ("h", "c", "b"):
            o.append((nm, s))
    for m in range(8):
        o.append(("ga", m)); o.append(("gc", m))
        if m % 2 == 0:
            o.append(("ao", m // 2))
        o.append(("co", m))
    for m in range(8):
        o.append(("mo", m))
    for blk in range(2):
        for f in range(32):
            o.append(("w1", f))
        for f in range(32):
            o.append(("w2", f))
    return o


class WRing:
    def __init__(self, P, wall_ap, slots, plan):
        self.P, self.wall, self.slots = P, wall_ap, slots
        self.plan = list(plan)
        self.pos = 0
        self.loaded = {}
        self.inuse = 0
        self.free = list(range(len(slots)))
        self.R = len(slots)

    def _load_next(self):
        key = self.plan[self.pos]
        self.pos += 1
        s = self.free.pop(0)
        t = WIDX[key]
        dst = self.slots[s]
        self.P.op("pool", lambda e, dst=dst, t=t: e.dma_start(out=dst[:, :], in_=self.wall[t, :, :]),
                  writes=[f"ring{s}"], dma=f"ring{s}")
        self.loaded[key] = s
        self.inuse += 1

    def pump(self):
        while self.pos < len(self.plan) and self.inuse < self.R:
            self._load_next()

    def get(self, key):
        self.pump()
        guard = 0
        while key not in self.loaded:
            assert self.pos < len(self.plan), key
            self._load_next()
            guard += 1
            assert guard < 1000
        s = self.loaded[key]
        return self.slots[s], f"ring{s}"

    def release(self, key):
        s = self.loaded.pop(key)
        self.free.append(s)
        self.inuse -= 1
        self.pump()


def build_program():
    nc = bass.Bass("TRN2", target_bir_lowering=False)
    dt = nc.dram_tensor
    xext = dt("xext", [NCH, EXT, D], F32, kind="ExternalInput").ap()
    cT_d = dt("cT", [128, 8, NCH], F32, kind="ExternalInput").ap()
    cs_d = dt("cs", [NCH, 128, 2, EXT], F32, kind="ExternalInput").ap()
    kb_d = dt("kbias", [NCH, 128, NKT], F32, kind="ExternalInput").ap()
    uv_d = dt("uval", [NCH, 128, 2], F32, kind="ExternalInput").ap()
    wall = dt("wall", [NWT, 128, 1024], F32, kind="ExternalInput").ap()
    wada = dt("wada", [48, 128, 1024], F32, kind="ExternalInput").ap()
    sp_d = dt("smallp", [128, 48 + 8 + 8 + 24 + 8], F32, kind="ExternalInput").ap()
    gfb_d = dt("gfb", [128, 1024], F32, kind="ExternalInput").ap()
    cst_d = dt("cst", [128, 5 * 256 + 128 + 128], F32, kind="ExternalInput").ap()
    y_d = dt("y", [NCH, T, D], F32, kind="ExternalOutput").ap()

    cur = [17408]

    def A(name, shape, dtype, at=None):
        esz = 4 if dtype == F32 else 2
        n = 1
        for s in shape[1:]:
            n *= s
        nbytes = n * esz
        if at is None:
            off = cur[0]
            cur[0] += (nbytes + 63) // 64 * 64
        else:
            off = at
        assert off + nbytes <= 229376, (name, off, nbytes)
        return nc.alloc_sbuf_tensor_at(name, list(shape), dtype, offset=off)

    ring = [A(f"ring{i}", [128, 1024], BF16) for i in range(RING)]
    xnT = A("xnT", [128, 8, EXT], BF16)
    xnT_off = cur[0] - 8 * EXT * 2
    attT = A("attT", [128, 4, T], BF16)
    gtb = A("gtb", [128, 2, D], F32)
    gfb = A("gfb_sb", [128, D], F32)
    smallp = A("smallp_sb", [128, 96], F32)
    cst_bf = A("cst_bf", [128, 5 * 256 + 128 + 128], BF16)
    ident_f = A("ident_f", [128, 128], F32)
    ones_f = A("ones_f", [128, 128], F32)
    dg = A("dg", [128, 2, 128], F32)
    cT = A("cT_sb", [128, 8, NCH], F32)
    scT = A("scT", [128, 8, NCH], BF16)
    modT = A("modT", [128, 48, NCH], F32)
    a1T = A("a1T", [128, 8, NCH], F32)
    a2T = A("a2T", [128, 8, NCH], F32)
    tmp6 = A("tmp6", [128, 8, NCH], F32)
    kbias = A("kbias_sb", [128, NKT], F32)
    uval = A("uval_sb", [128, 2], F32)
    ss = A("ss", [128, 48], F32)
    rs = A("rs", [128, 48], F32)
    epsc = A("epsc", [128, 16], F32)
    xinC2 = A("xinC2", [128, D], F32)
    R1 = cur[0]
    R1_SIZE = 104 * 1024
    assert R1 + R1_SIZE <= 229376, R1
    NWA = 12
    wada_sb = [A(f"wada{i}", [128, 8, 128], BF16, at=R1 + i * 2048) for i in range(NWA)]
    o = R1
    cs_sb = A("cs_sb", [128, 2, EXT], F32, at=o); o += 24 * 1024
    qpad = [A(f"qpad{i}", [128, 2, T], BF16, at=o + i * 4096) for i in range(2)]; o += 8 * 1024
    kpair = [A(f"kpair{i}", [128, EXT], BF16, at=o + i * 6144) for i in range(2)]; o += 12 * 1024
    vbuf = A("vbuf", [128, 32, 2, 192], BF16, at=o); o += 24 * 1024
    rt = [A(f"rt{i}", [128, 512], F32, at=o + i * 2048) for i in range(4)]
    rtw = A("rtw", [128, 1024], F32, at=o); o += 8 * 1024
    acc = A("acc", [128, 2, 2, T], F32, at=o); o += 16 * 1024
    ptb = [A(f"pt{i}", [128, 2, 2, 128], BF16, at=o + i * 1024) for i in range(4)]; o += 4 * 1024
    qpad2 = [A(f"qpadB{i}", [128, 2, T], BF16, at=o + i * 4096) for i in range(2)]; o += 8 * 1024
    assert o <= R1 + R1_SIZE
    xinC = [A(f"xinC{i}", [128, D], F32, at=R1 + 96 * 1024 + i * 4096) for i in range(2)] + [xinC2]
    xsC = A("xsC", [128, 4, D], BF16, at=xnT_off + 8 * EXT * 2)
    x1 = A("x1", [128, 8, D], F32, at=R1)
    xn2T = A("xn2T", [128, 8, T], BF16, at=R1 + 32 * 1024)
    mergedT = A("mergedT", [128, 8, T], BF16, at=R1 + 48 * 1024)
    cvinT = A("cvinT", [128, 8, T], BF16, at=R1)
    ubuf = A("ubuf", [128, 1032], F32, at=R1 + 16 * 1024)
    ctmp = [A(f"ctmp{i}", [128, 512], F32, at=R1 + 16 * 1024 + 4224 + i * 2048) for i in range(1)]
    csb = A("csb", [128, 512], F32, at=R1 + 16 * 1024 + 4224 + 2048)
    hT = A("hT", [128, 32, 512], BF16, at=R1 + 48 * 1024)
    sgt = [A(f"sgt{i}", [128, 512], F32, at=R1 + 80 * 1024 + i * 2048) for i in range(4)]
    xinB = [A(f"xinB{i}", [128, D], F32, at=xnT_off + i * 4096) for i in range(4)]
    xsB = A("xsB", [128, 4, D], BF16, at=xnT_off + 16 * 1024)
    x2s = [A(f"x2s{i}", [128, D], F32, at=R1 + 88 * 1024 + i * 4096) for i in range(2)]

    banks = [nc.alloc_psum_tensor(f"bank{i}", [128, 512], F32) for i in range(8)]
    banks_bf = [b.bitcast(BF16) for b in banks]

    P = Prog(nc)
    R1_ATT = ["cs0", "cs1", "cs2", "qpad0", "qpad1", "qpadB0", "qpadB1", "kpair0", "kpair1", "vbuf", "rt0", "rt1", "rt2", "rt3", "acc",
              "pt0a", "pt1a", "pt2a", "pt3a", "pt0b", "pt1b", "pt2b", "pt3b"]
    R1_POST = ["x1", "xn2T0", "xn2T1", "mergedT", "cvinT", "ubuf", "ctmp0", "csb", "hT", "sgt0", "sgt1", "sgt2", "sgt3"]
    for a in R1_ATT:
        for b in R1_POST:
            P.add_alias(a, b)
    for a in ["hT"]:
        for b in ["mergedT"]:
            P.add_alias(a, b)
    for b in ["cvinT", "ubuf", "ctmp0", "csb"]:
        P.add_alias("x1", b)
    for a in ["sgt0", "sgt1", "sgt2", "sgt3"]:
        for b in ["ubuf", "ctmp0", "csb"]:
            P.add_alias(a, b)
    for b in ["xinB0", "xinB1", "xinB2", "xinB3", "xsB0", "xsB1", "xsB2", "xsB3"]:
        P.add_alias("xnT", b)
    for i in range(NWA):
        for b in ["cs0", "cs1", "cs2", "x1", "qpad0", "qpad1"]:
            P.add_alias(f"wada{i}", b)
    for i in range(2):
        for b in ["acc", "pt0a", "pt1a", "pt2a", "pt3a", "pt0b", "pt1b", "pt2b", "pt3b"]:
            P.add_alias(f"x2s{i}", b)
        for b in ["qpadB0", "qpadB1"]:
            P.add_alias(f"xinC{i}", b)
    for i in range(4):
        P.add_alias(f"xsC{i}", "attT")

    plan = []
    for ci in range(NCH):
        plan += chunk_tile_order()
    W = WRing(P, wall, ring, plan)

    maskb = cst_bf[:, 0:1280].rearrange("p (t h q) -> p t h q", t=5, h=2)
    ident = cst_bf[:, 1280:1408]
    ones_bf = cst_bf[:, 1408:1536]
    b_adaT = smallp[:, 0:48]
    g1T = smallp[:, 48:56]
    g2T = smallp[:, 56:64]
    cwT = smallp[:, 64:88].rearrange("p (s t) -> p s t", t=3)
    cbT = smallp[:, 88:96]

    def mm(out, lhsT, rhs, start, stop, reads, writes):
        P.op("pe", lambda e: e.matmul(out, lhsT=lhsT, rhs=rhs, start=start, stop=stop), reads=reads, writes=writes)

    P.op("pool", lambda e: e.dma_start(out=cst_bf[:, :], in_=cst_d), writes=["cst"], dma="cst")
    P.op("sp", lambda e: e.dma_start(out=smallp[:, :], in_=sp_d), writes=["smallp"], dma="smallp")
    P.op("sp", lambda e: e.dma_start(out=gfb[:, :], in_=gfb_d), writes=["gfb"], dma="gfb")
    P.op("sp", lambda e: e.dma_start(out=cT[:, :, :], in_=cT_d), writes=["cT"], dma="cT")
    P.op("sp", lambda e: e.dma_start(out=ident_f[:, :], in_=cst_d[:, 1280:1408]), writes=["ident_f"], dma="ident_f")
    P.op("sp", lambda e: e.dma_start(out=ones_f[:, :], in_=cst_d[:, 1408:1536]), writes=["ones_f"], dma="ones_f")
    W.pump()
    P.op("dve", lambda e: e.memset(epsc[:, :], EPS), writes=["epsc"])
    P.op("act", lambda e: e.activation(out=scT[:, :, :], in_=cT[:, :, :], func=AF.Silu), reads=["cT"], writes=["scT"])
    def emit_a(dst, dn, which, gT):
        P.op("dve", lambda e: e.tensor_scalar(out=tmp6[:, :, :], in0=modT[:, which * 8:which * 8 + 8, :], scalar1=1.0,
                                              scalar2=None, op0=ALU.add),
             reads=[f"modT{which}"], writes=["tmp6"])
        for cj in range(NCH):
            P.op("dve", lambda e, cj=cj: e.tensor_tensor(out=dst[:, :, cj], in0=tmp6[:, :, cj], in1=gT, op=ALU.mult),
                 reads=["tmp6", "smallp"], writes=[dn])

    def adaln_et(et):
        wb = wada_sb[et % NWA]
        wn = f"wada{et % NWA}"
        P.op("pool", lambda e: e.dma_start(out=wb[:, :, :], in_=wada[et].rearrange("p (k c) -> p k c", k=8)),
             writes=[wn], dma=wn)
        bk = banks[et % 2]
        for k in range(8):
            mm(bk[:, 0:NCH], wb[:, k, :], scT[:, k, :], k == 0, k == 7, [wn, "scT"], [f"bank{et % 2}"])
        P.op("dve", lambda e: e.tensor_scalar(out=modT[:, et, :], in0=bk[:, 0:NCH], scalar1=b_adaT[:, et:et + 1],
                                              scalar2=None, op0=ALU.add),
             reads=[f"bank{et % 2}", "smallp"], writes=[f"modT{et // 8}"])
        if et == 15:
            emit_a(a1T, "a1T", 1, g1T)
        if et == 39:
            emit_a(a2T, "a2T", 4, g2T)

    P.op("dve", lambda e: e.memset(ss[:, :], 0.0), writes=[f"ssc{c_}" for c_ in range(48)])
    for et in range(16):
        adaln_et(et)

    def norm_to_T(ci, ntiles, src_rows, xin, xin_names, xs, xs_name, ss_base, dstT, dst_name_, aT, sh_which, dst_col0,
                  src_is_dram=True, src_sb=None, src_sb_name=None, act_share=2, dst_names=None, sq_dve=False, tr_lag=1):
        def load(tt):
            if src_is_dram and tt < ntiles:
                xt = xin[tt % len(xin)]
                xname = xin_names[tt % len(xin)]
                r0 = src_rows + tt * 128
                P.op("sp", lambda e, xt=xt, r0=r0: e.dma_start(out=xt[:, :], in_=xext[ci, r0:r0 + 128, :]),
                     writes=[xname], dma=xname)

        def stage1(tt):
            tl = tt % 4
            if src_is_dram:
                xt = xin[tt % len(xin)]
                xname = xin_names[tt % len(xin)]
                src = xt[:, :]
            else:
                src = src_sb[:, tt, :]
                xname = src_sb_name + str(tt)
            col = ss_base + tt
            if sq_dve and tt % 2 == 1:
                P.op("dve", lambda e, src=src, col=col, tl=tl: e.scalar_tensor_tensor(out=xs[:, tl, :], in0=src, scalar=1.0, in1=src,
                                                                                    op0=ALU.mult, op1=ALU.mult, accum_out=ss[:, col:col + 1]),
                     reads=[xname, f"ssc{col}"], writes=[f"ssc{col}", xs_name + str(tl)])
            else:
                P.op("act", lambda e, src=src, col=col, tl=tl: e.activation(out=xs[:, tl, :], in_=src, func=AF.Square,
                                                                            accum_out=ss[:, col:col + 1]),
                     reads=[xname, f"ssc{col}"], writes=[f"ssc{col}", xs_name + str(tl)])
            P.op("act", lambda e, col=col: e.activation(out=rs[:, col:col + 1], in_=ss[:, col:col + 1], func=AF.Sqrt,
                                                        bias=epsc[:, 0:1], scale=1.0 / D),
                 reads=[f"ssc{col}", "epsc"], writes=[f"rsc{col}"])
            P.op("dve", lambda e, col=col: e.reciprocal(out=rs[:, col:col + 1], in_=rs[:, col:col + 1]),
                 reads=[f"rsc{col}"], writes=[f"rsc{col}"])
            return src, xname, col

        def stage2(tt, src, xname, col):
            tl = tt % 4
            P.op("act", lambda e, src=src, col=col, tl=tl: e.activation(out=xs[:, tl, :], in_=src, func=AF.Identity,
                                                                        scale=rs[:, col:col + 1]),
                 reads=[xname, f"rsc{col}"], writes=[xs_name + str(tl)])

        def transposes(tt):
            tl = tt % 4
            g_ = tt // 2
            b0 = 4 + 2 * (g_ % 2)
            for k in range(8):
                bi = b0 + k // 4
                c0 = (k % 4) * 256 + (tt % 2) * 128
                P.op("pe", lambda e, bi=bi, c0=c0, tl=tl, k=k: e.transpose(banks_bf[bi][:, c0:c0 + 128],
                                                                             xs[:, tl, k * 128:(k + 1) * 128], ident),
                     reads=[xs_name + str(tl), "cst"], writes=[f"bank{bi}"])

        def evac_one(g_, k):
            e0 = dst_col0 + g_ * 256
            dst_name = dst_names[g_] if dst_names is not None else dst_name_
            bi = 4 + 2 * (g_ % 2) + k // 4
            c0 = (k % 4) * 256
            sh = modT[:, sh_which * 8 + k, ci:ci + 1]
            sc = aT[:, k, ci:ci + 1]
            aT_name = "a1T" if sh_which == 0 else "a2T"
            if act_share == 0 or k % act_share != act_share - 1:
                P.op("dve", lambda e: e.tensor_scalar(
                    out=dstT[:, k, e0:e0 + 256], in0=banks_bf[bi][:, c0:c0 + 256], scalar1=sc, scalar2=sh,
                    op0=ALU.mult, op1=ALU.add), reads=[f"bank{bi}", f"modT{sh_which}", aT_name], writes=[dst_name])
            else:
                P.op("act", lambda e: e.activation(
                    out=dstT[:, k, e0:e0 + 256], in_=banks_bf[bi][:, c0:c0 + 256], func=AF.Identity, bias=sh, scale=sc),
                    reads=[f"bank{bi}", f"modT{sh_which}", aT_name], writes=[dst_name])

        class NormPipe:
            def __init__(self):
                self.n_load = self.n_s1 = self.n_s2 = self.n_tr = 0
                self.info = {}
                self.evq = []
                self.hold = False
                self.groups_done = 0
                self.nsl = len(xin) if src_is_dram else 10 ** 6
                self.avail = ntiles

            @property
            def done(self):
                return self.n_tr == ntiles and not self.evq

            @property
            def clean(self):
                return (not self.evq) and self.n_tr % 2 == 0

            def step(self):
                if src_is_dram:
                    while self.n_load < ntiles and self.n_load <= self.n_s1 + 1 and self.n_load - self.n_s2 < self.nsl:
                        load(self.n_load)
                        self.n_load += 1
                if self.n_s1 < min(ntiles, self.avail) and (not src_is_dram or self.n_s1 < self.n_load) and self.n_s1 - self.n_tr < 4:
                    self.info[self.n_s1] = stage1(self.n_s1)
                    self.n_s1 += 1
                if self.n_s2 < self.n_s1 - 1 or (self.n_s1 == ntiles and self.n_s2 < ntiles):
                    stage2(self.n_s2, *self.info[self.n_s2])
                    self.n_s2 += 1
                for _ in range(4):
                    if self.evq:
                        g_, k = self.evq.pop(0)
                        evac_one(g_, k)
                        if k == 7:
                            self.groups_done += 1
                can_tr = self.n_tr < self.n_s2 - tr_lag or (self.n_s2 == ntiles and self.n_tr < ntiles)
                if can_tr and not (self.hold and self.n_tr % 2 == 0):
                    g_ = self.n_tr // 2
                    if all(q[0] != g_ - 2 for q in self.evq):
                        transposes(self.n_tr)
                        self.n_tr += 1
                        if self.n_tr % 2 == 0:
                            self.evq += [(g_, k) for k in range(8)]

            def run(self, nsteps=None, until=None):
                c_ = 0
                while not self.done and (nsteps is None or c_ < nsteps) and not (until is not None and until()):
                    self.step()
                    c_ += 1

        return NormPipe()

    for ci in range(NCH):
        P.op("sp", lambda e, ci=ci: e.dma_start(out=kbias[:, :], in_=kb_d[ci]), writes=["kbias"], dma="kbias")
        P.op("sp", lambda e, ci=ci: e.dma_start(out=uval[:, :], in_=uv_d[ci]), writes=["uval"], dma="uval")
        def emit_gt(wi, cj, bi):
            which = (2, 5)[wi]
            for k in range(8):
                sl = k % 2
                P.op("dve", lambda e, k=k, sl=sl: e.tensor_scalar(
                    out=dg[:, sl, :], in0=ident_f[:, :], scalar1=modT[:, which * 8 + k, cj:cj + 1], scalar2=None, op0=ALU.mult),
                    reads=["ident_f", f"modT{which}"], writes=[f"dg{sl}"])
                mm(banks[bi][:, (k % 4) * 128:(k % 4) * 128 + 128], ones_f[:, :], dg[:, sl, :], True, True,
                   ["ones_f", f"dg{sl}"], [f"bank{bi}"])
                if k % 4 == 3:
                    h0 = (k // 4) * 512
                    P.op("act", lambda e, h0=h0: e.copy(out=gtb[:, wi, h0:h0 + 512], in_=banks[bi][:, :]),
                         reads=[f"bank{bi}"], writes=[f"gtb{wi}"])
        if ci == 0:
            n1p = norm_to_T(0, EXT // 128, 0, xinC, ["xinC0", "xinC1", "xinC2"], xsC, "xsC", 0, xnT, "xnT", a1T, 0, 0, act_share=0)
            et_ = 16
            while not n1p.done or et_ < 48:
                if not n1p.done:
                    n1p.step()
                if et_ < 48:
                    adaln_et(et_)
                    et_ += 1
            emit_gt(0, 0, 2)
        if n1p is not None:
            n1p.hold = False
            n1p.run()
        P.op("dve", lambda e: e.memset(ss[:, :], 0.0), writes=[f"ssc{c_}" for c_ in range(48)])
        for th_ in (1, 0, 2):
            P.op("sp", lambda e, ci=ci, th_=th_: e.dma_start(out=cs_sb[:, :, th_ * 1024:(th_ + 1) * 1024],
                                                             in_=cs_d[ci][:, :, th_ * 1024:(th_ + 1) * 1024]),
                 writes=[f"cs{th_}"], dma=f"cs{th_}")
        for i in range(2):
            P.op("pool", lambda e, i=i: e.memset(qpad[i][:, :, :], 0.0), writes=[f"qpad{i}"])
        P.op("pool", lambda e: e.memset(vbuf[:, :, :, 64:128], 1.0), writes=["vbuf"])

        def proj_rope(key1, key2, blocks, place):
            w1t, w1n = W.get(key1)
            w2t, w2n = W.get(key2)
            w1v = w1t[:, :].rearrange("p (k c) -> p k c", k=8)
            w2v = w2t[:, :].rearrange("p (k c) -> p k c", k=8)
            for bi_, (e0, wd) in enumerate(blocks):
                b1, b2 = (0, 1) if bi_ % 2 == 0 else (2, 3)
                for k in range(8):
                    mm(banks[b1][:, 0:wd], w1v[:, k, :], xnT[:, k, e0:e0 + wd], k == 0, k == 7, [w1n, "xnT"], [f"bank{b1}"])
                for k in range(8):
                    mm(banks[b2][:, 0:wd], w2v[:, k, :], xnT[:, k, e0:e0 + wd], k == 0, k == 7, [w2n, "xnT"], [f"bank{b2}"])
                cos = cs_sb[:, 0, e0:e0 + wd]
                sin = cs_sb[:, 1, e0:e0 + wd]
                for (ti, bk, tab) in ((0, b1, cos), (1, b2, sin), (2, b2, cos), (3, b1, sin)):
                    P.op("dve", lambda e, ti=ti, bk=bk, tab=tab, wd=wd: e.tensor_tensor(out=rt[ti][:, 0:wd], in0=banks[bk][:, 0:wd],
                                                                                        in1=tab, op=ALU.mult),
                         reads=[f"bank{bk}"] + [f"cs{t_}" for t_ in range(e0 // 1024, (e0 + wd - 1) // 1024 + 1)], writes=[f"rt{ti}"])
                place(e0, wd)
                yield bi_
            W.release(key1)
            W.release(key2)

        def place_q(e0, wd):
            t0 = e0 - HALO
            for j in range(4):
                pi, var = j // 2, j % 2
                r = slice(32 * j, 32 * j + 32)
                d1 = slice(32 * var, 32 * var + 32)
                d2 = slice(64 + 32 * var, 64 + 32 * var + 32)
                P.op("dve", lambda e, pi=pi, var=var, r=r, d1=d1: e.tensor_tensor(
                    out=qpad[pi][d1, var, t0:t0 + wd], in0=rt[0][r, 0:wd], in1=rt[1][r, 0:wd], op=ALU.subtract),
                    reads=["rt0", "rt1"], writes=[f"qpad{pi}"])
                P.op("dve", lambda e, pi=pi, var=var, r=r, d2=d2: e.tensor_tensor(
                    out=qpad[pi][d2, var, t0:t0 + wd], in0=rt[2][r, 0:wd], in1=rt[3][r, 0:wd], op=ALU.add),
                    reads=["rt2", "rt3"], writes=[f"qpad{pi}"])

        def place_k(e0, wd):
            for pi in range(2):
                r = slice(64 * pi, 64 * pi + 64)
                P.op("dve", lambda e, pi=pi, r=r: e.tensor_tensor(out=kpair[pi][0:64, e0:e0 + wd], in0=rt[0][r, 0:wd],
                                                                  in1=rt[1][r, 0:wd], op=ALU.subtract),
                     reads=["rt0", "rt1"], writes=[f"kpair{pi}"])
                P.op("dve", lambda e, pi=pi, r=r: e.tensor_tensor(out=kpair[pi][64:128, e0:e0 + wd], in0=rt[2][r, 0:wd],
                                                                  in1=rt[3][r, 0:wd], op=ALU.add),
                     reads=["rt2", "rt3"], writes=[f"kpair{pi}"])

        def key_tiles(g):
            def strided(es, st):
                return lambda v: v[:, es:es + 127 * st + 1:st]
            if g == 0:
                return [strided(960 + 128 * kt, 1) for kt in range(9)], 0
            if g == 1:
                return [strided(768 + 512 * kk + r, 4) for r in range(4) for kk in range(3)], 9
            out = []
            for cp in range(8):
                out.append(strided(cp, 16))
                out.append(strided(cp + 8, 16))
                out.append(strided(2048 + cp, 8))
            return out, 21

        def fin_att(u):
            for pi in range(2):
                P.op("act", lambda e, pi=pi: e.activation(out=rtw[0:64, :], in_=acc[64:128, pi, 0, :], func=AF.Ln),
                     reads=["acc"], writes=["rt0", "rt1"])
                P.op("act", lambda e, pi=pi: e.activation(out=rtw[64:128, :], in_=acc[0:64, pi, 1, :], func=AF.Ln),
                     reads=["acc"], writes=["rt0", "rt1"])
                P.op("act", lambda e: e.activation(out=rtw[:, :], in_=rtw[:, :], func=AF.Exp, scale=-1.0),
                     reads=["rt0", "rt1"], writes=["rt0", "rt1"])
                P.op("dve", lambda e, pi=pi, u=u: e.tensor_tensor(out=attT[0:64, 2 * u + pi, :], in0=acc[0:64, pi, 0, :], in1=rtw[0:64, :],
                                                                 op=ALU.mult), reads=["acc", "rt0", "rt1"], writes=["attT"])
                P.op("dve", lambda e, pi=pi, u=u: e.tensor_tensor(out=attT[64:128, 2 * u + pi, :], in0=acc[64:128, pi, 1, :], in1=rtw[64:128, :],
                                                                 op=ALU.mult), reads=["acc", "rt0", "rt1"], writes=["attT"])

        pend_fin = [None]
        for u in range(2):
            for gi, g in enumerate(G_ORDER):
                kblocks = {0: [(960, 512), (1472, 512), (1984, 128)], 1: [(768 + 512 * i, 512) for i in range(3)],
                           2: [(512 * i, 512) for i in range(6)]}[g]
                ktl, kb0 = key_tiles(g)
                for _ in proj_rope(("q1", g, u), ("q2", g, u), [(HALO, 512), (HALO + 512, 512)], place_q):
                    pass
                kgen = proj_rope(("k1", g, u), ("k2", g, u), kblocks, place_k)
                wv = []
                for nm_ in ("va", "vb"):
                    t_, n_ = W.get((nm_, g, u))
                    wv.append((t_[:, :].rearrange("p (k c) -> p k c", k=4), n_))
                vgroups = list(range(0, len(ktl), 2))
                per_blk = max(1, len(vgroups) // (len(kblocks) + 1))
                vi = 0
                kdone = False
                while vi < len(vgroups) or not kdone:
                    for _ in range(per_blk):
                        if vi >= len(vgroups):
                            break
                        kt0 = vgroups[vi]
                        vi += 1
                        bi = 4 + (kt0 // 2) % 2
                        nk = min(2, len(ktl) - kt0)
                        for kl in range(nk):
                            kap = ktl[kt0 + kl]
                            c0 = kl * 256
                            for k in range(8):
                                mm(banks[bi][:, c0:c0 + 256], kap(xnT[:, k, :]), wv[k // 4][0][:, k % 4, :], k == 0, k == 7,
                                   [wv[k // 4][1], "xnT"], [f"bank{bi}"])
                        for hh in range(2):
                            P.op("act", lambda e, bi=bi, kt0=kt0, nk=nk, hh=hh: e.copy(
                                out=vbuf[:, kt0:kt0 + nk, :, hh * 128:hh * 128 + 64],
                                in_=banks[bi][:, 0:256 * nk].rearrange("p (a b h d) -> p a b h d", a=nk, b=2, h=2)[:, :, :, hh, :]),
                                reads=[f"bank{bi}"], writes=["vbuf"])
                    if not kdone:
                        try:
                            next(kgen)
                        except StopIteration:
                            kdone = True
                W.release(("va", g, u))
                W.release(("vb", g, u))
                if pend_fin[0] is not None:
                    fin_att(pend_fin[0])
                    pend_fin[0] = None

                if g == 0:
                    qblocks = [[(128 * qt, 1, 128, qt, qt + 1, 0, 1) for qt in (2 * b, 2 * b + 1)] for b in range(4)]
                elif g == 1:
                    qblocks = [[(4 * i0 + r, 4, 128, r * 3 + i0 // 128, r * 3 + i0 // 128 + 1, 0, 1) for i0 in (0, 128)]
                               for r in range(4)]
                else:
                    qblocks = [[(r, 16, 64, (r % 8) * 3 + r // 8, (r % 8) * 3 + 2, 0, 3 + r // 8) for r in range(4 * b, 4 * b + 4)]
                               for b in range(4)]
                steps = []
                for qb_i, qb in enumerate(qblocks):
                    for pi in range(2):
                        for qi, qt_ in enumerate(qb):
                            steps.append((qb_i, pi, qi, qt_, qi == len(qb) - 1, qb))
                LA = 3

                def emit_qk(n):
                    qb_i, pi, qi, (qs, qst, nq, ta, tb, ma, mb), last, qb = steps[n]
                    sb_i = (0, 1, 2, 6)[n % 4]
                    pt = ptb[n % 4]
                    ptn = f"pt{n % 4}"
                    S = banks[sb_i]
                    qap = qpad[pi][:, :, qs:qs + (nq - 1) * qst + 1:qst]
                    sos = []
                    merged_mask = (nq == 128 and (ma, mb) == (0, 1))
                    for ti, (tid, mtype) in enumerate(((ta, ma), (tb, mb))):
                        kap = ktl[tid]
                        so = S[:, ti * 2 * nq:(ti + 1) * 2 * nq].rearrange("p (h q) -> p h q", h=2)
                        sos.append(so)
                        if merged_mask:
                            if ti == 0:
                                mm(S[:, 0:512], ident, cst_bf[:, 0:512], True, False, ["cst"], [f"bank{sb_i}"])
                        else:
                            mm(so, ident, maskb[:, mtype, :, 0:nq], True, False, ["cst"], [f"bank{sb_i}"])
                        mm(so, kap(kpair[pi][:, :]), qap, False, True, [f"kpair{pi}", f"qpad{pi}"], [f"bank{sb_i}"])
                    for ti, (tid, mtype) in enumerate(((ta, ma), (tb, mb))):
                        so = sos[ti]
                        kcol = kb0 + tid
                        P.op("act", lambda e, so=so, pt=pt, ti=ti, nq=nq, kcol=kcol: e.activation(
                            out=pt[:, ti, :, 0:nq], in_=so, func=AF.Exp,
                            bias=kbias[:, kcol:kcol + 1], scale=0.125),
                            reads=[f"bank{sb_i}", "kbias"], writes=[ptn + "ab"[ti]])

                def emit_pv(n):
                    qb_i, pi, qi, (qs, qst, nq, ta, tb, ma, mb), last, qb = steps[n]
                    pt = ptb[n % 4]
                    ptn = f"pt{n % 4}"
                    nset = (qb_i * 2 + pi) % 3
                    nd = 3 + nset
                    qc = qi * nq
                    for hh in range(2):
                        lA = vbuf[:, ta, pi, hh * 64:hh * 64 + 128]
                        lB = vbuf[:, tb, pi, hh * 64:hh * 64 + 128]
                        oo = banks[nd][:, hh * 256 + qc:hh * 256 + qc + nq]
                        mm(oo, lA, pt[:, 0, hh, 0:nq], True, False, ["vbuf", ptn + "a"], [f"bank{nd}"])
                        mm(oo, lB, pt[:, 1, hh, 0:nq], False, True, ["vbuf", ptn + "b"], [f"bank{nd}"])
                    if not last:
                        return
                    if g == 0:
                        q0 = qb[0][0]
                        d_ = acc[:, pi, :, q0:q0 + 256]
                        s_ = banks[nd][:, 0:512].rearrange("p (h c) -> p h c", h=2)
                    elif g == 1:
                        r = qb[0][0]
                        d_ = acc[:, pi, :, r:r + 1021:4]
                        s_ = banks[nd][:, 0:512].rearrange("p (h c) -> p h c", h=2)
                    else:
                        r0 = qb[0][0]
                        d_ = acc[:, pi, :, :].rearrange("p h (j r) -> p h r j", r=16)[:, :, r0:r0 + 4, :]
                        s_ = banks[nd][:, 0:512].rearrange("p (h r j) -> p h r j", h=2, r=4)
                    if gi == 0:
                        P.op("dve", lambda e, d_=d_, s_=s_: e.tensor_copy(out=d_, in_=s_), reads=[f"bank{nd}"], writes=["acc"])
                    else:
                        P.op("dve", lambda e, d_=d_, s_=s_: e.tensor_tensor(out=d_, in0=s_, in1=d_, op=ALU.add),
                             reads=[f"bank{nd}", "acc"], writes=["acc"])

                for n in range(len(steps) + LA):
                    if n < len(steps):
                        emit_qk(n)
                    if n - LA >= 0:
                        emit_pv(n - LA)
            pend_fin[0] = u

        emit_gt(1, ci, 6)
        cblocks = [(1023, 512), (1535, 512), (2047, 2)]
        for s in range(8):
            (wh, whn), (wc, wcn), (wbt, wbn) = W.get(("h", s)), W.get(("c", s)), W.get(("b", s))
            whv, wcv, wbv = [w_[:, :].rearrange("p (k c) -> p k c", k=8) for w_ in (wh, wc, wbt)]
            for bi_, (e0, wd) in enumerate(cblocks):
                b1, b2 = (0, 1) if bi_ % 2 == 0 else (2, 3)
                for k in range(8):
                    mm(banks[b1][:, 0:wd], whv[:, k, :], xnT[:, k, e0:e0 + wd], k == 0, k == 7, [whn, "xnT"], [f"bank{b1}"])
                for k in range(8):
                    mm(banks[b2][:, 0:wd], wcv[:, k, :], xnT[:, k, e0:e0 + wd], k == 0, k == 7, [wcn, "xnT"], [f"bank{b2}"])
                P.op("act", lambda e, b2=b2, wd=wd: e.copy(out=csb[:, 0:wd], in_=banks[b2][:, 0:wd]), reads=[f"bank{b2}"], writes=["csb"])
                uo = e0 - 1023
                P.op("dve", lambda e, b1=b1, wd=wd, uo=uo: e.tensor_tensor(out=ubuf[:, uo:uo + wd], in0=banks[b1][:, 0:wd],
                                                                            in1=csb[:, 0:wd], op=ALU.mult),
                     reads=[f"bank{b1}", "csb"], writes=["ubuf"])
            for (col, vi) in ((0, 0), (1025, 1)):
                P.op("dve", lambda e, col=col, vi=vi: e.tensor_scalar(out=ubuf[:, col:col + 1], in0=ubuf[:, col:col + 1],
                                                                     scalar1=uval[:, vi:vi + 1], scalar2=None, op0=ALU.mult),
                     reads=["ubuf", "uval"], writes=["ubuf"])
            for cb in range(2):
                t0 = cb * 512
                bb = 4 + cb
                for k in range(8):
                    mm(banks[bb][:, :], wbv[:, k, :], xnT[:, k, HALO + t0:HALO + t0 + 512], k == 0, k == 7, [wbn, "xnT"], [f"bank{bb}"])
                ct = ctmp[0]
                P.op("act", lambda e, t0=t0, s=s, ct=ct: e.activation(out=ct[:, :], in_=ubuf[:, t0 + 1:t0 + 513], func=AF.Identity,
                                                                     bias=cbT[:, s:s + 1], scale=cwT[:, s, 1:2]),
                     reads=["ubuf", "smallp"], writes=["ctmp0"])
                P.op("dve", lambda e, t0=t0, s=s, ct=ct: e.scalar_tensor_tensor(out=ct[:, :], in0=ubuf[:, t0:t0 + 512],
                                                                               scalar=cwT[:, s, 0:1], in1=ct[:, :],
                                                                               op0=ALU.mult, op1=ALU.add),
                     reads=["ubuf", "smallp", "ctmp0"], writes=["ctmp0"])
                P.op("dve", lambda e, t0=t0, s=s, ct=ct: e.scalar_tensor_tensor(out=ct[:, :], in0=ubuf[:, t0 + 2:t0 + 514],
                                                                               scalar=cwT[:, s, 2:3], in1=ct[:, :],
                                                                               op0=ALU.mult, op1=ALU.add),
                     reads=["ubuf", "smallp", "ctmp0"], writes=["ctmp0"])
                P.op("dve", lambda e, t0=t0, s=s, ct=ct, bb=bb: e.tensor_tensor(out=cvinT[:, s, t0:t0 + 512], in0=banks[bb][:, :],
                                                                               in1=ct[:, :], op=ALU.mult),
                     reads=[f"bank{bb}", "ctmp0"], writes=["cvinT"])
            for nm in ("h", "c", "b"):
                W.release((nm, s))
            if pend_fin[0] is not None:
                fin_att(pend_fin[0])
                pend_fin[0] = None

        aot = None
        for m in range(8):
            (wga, wgan), (wgc, wgcn) = W.get(("ga", m)), W.get(("gc", m))
            if m % 2 == 0:
                aot = W.get(("ao", m // 2))
            (wco, wcon) = W.get(("co", m))
            wgav, wgcv, wcov = [w_[:, :].rearrange("p (k c) -> p k c", k=8) for w_ in (wga, wgc, wco)]
            waov = aot[0][:, :].rearrange("p (k c) -> p k c", k=8)
            for blk in range(2):
                t0 = blk * 512
                bs = 0 if (m * 2 + blk) % 2 == 0 else 4
                for k in range(8):
                    mm(banks[bs][:, :], wgav[:, k, :], xnT[:, k, HALO + t0:HALO + t0 + 512], k == 0, k == 7, [wgan, "xnT"], [f"bank{bs}"])
                for k in range(8):
                    mm(banks[bs + 1][:, :], wgcv[:, k, :], xnT[:, k, HALO + t0:HALO + t0 + 512], k == 0, k == 7, [wgcn, "xnT"],
                       [f"bank{bs + 1}"])
                for pi in range(4):
                    mm(banks[bs + 2][:, :], waov[:, (m % 2) * 4 + pi, :], attT[:, pi, t0:t0 + 512], pi == 0, pi == 3, [aot[1], "attT"],
                       [f"bank{bs + 2}"])
                for s in range(8):
                    mm(banks[bs + 3][:, :], wcov[:, s, :], cvinT[:, s, t0:t0 + 512], s == 0, s == 7, [wcon, "cvinT"], [f"bank{bs + 3}"])
                P.op("act", lambda e, bs=bs: e.activation(out=sgt[0][:, :], in_=banks[bs][:, :], func=AF.Sigmoid),
                     reads=[f"bank{bs}"], writes=["sgt0"])
                P.op("act", lambda e, bs=bs: e.activation(out=sgt[1][:, :], in_=banks[bs + 1][:, :], func=AF.Sigmoid),
                     reads=[f"bank{bs + 1}"], writes=["sgt1"])
                P.op("dve", lambda e, bs=bs: e.tensor_tensor(out=sgt[0][:, :], in0=banks[bs + 2][:, :], in1=sgt[0][:, :], op=ALU.mult),
                     reads=[f"bank{bs + 2}", "sgt0"], writes=["sgt0"])
                P.op("dve", lambda e, bs=bs: e.tensor_tensor(out=sgt[1][:, :], in0=banks[bs + 3][:, :], in1=sgt[1][:, :], op=ALU.mult),
                     reads=[f"bank{bs + 3}", "sgt1"], writes=["sgt1"])
                P.op("dve", lambda e, m=m, t0=t0: e.tensor_tensor(out=mergedT[:, m, t0:t0 + 512], in0=sgt[0][:, :], in1=sgt[1][:, :],
                                                                  op=ALU.add),
                     reads=["sgt0", "sgt1"], writes=["mergedT"])
            W.release(("ga", m)); W.release(("gc", m)); W.release(("co", m))
            if m % 2 == 1:
                W.release(("ao", m // 2))

        wmo = [W.get(("mo", m)) for m in range(8)]
        n2p = norm_to_T(ci, 8, 0, None, None, xsB, "xsB", 24, xn2T, None, a2T, 3, 0, src_is_dram=False, src_sb=x1, src_sb_name="x1_",
                        dst_names=["xn2T0", "xn2T0", "xn2T1", "xn2T1"])
        n2p.avail = 0
        for tt in range(8):
            xt = xinB[tt % 4]
            xname = f"xinB{tt % 4}"
            P.op("sp", lambda e, xt=xt, tt=tt, ci=ci: e.dma_start(out=xt[:, :], in_=xext[ci, HALO + tt * 128:HALO + tt * 128 + 128, :]),
                 writes=[xname], dma=xname)
            for half in range(2):
                bk = (tt % 2) * 2 + half
                for m in range(8):
                    mm(banks[bk][:, :], mergedT[:, m, tt * 128:(tt + 1) * 128], wmo[m][0][:, half * 512:(half + 1) * 512], m == 0, m == 7,
                       [wmo[m][1], "mergedT"], [f"bank{bk}"])
                hs = slice(half * 512, half * 512 + 512)
                P.op("dve", lambda e, bk=bk, tt=tt, hs=hs: e.tensor_tensor(out=x1[:, tt, hs], in0=banks[bk][:, :], in1=gtb[:, 0, hs],
                                                                          op=ALU.mult),
                     reads=[f"bank{bk}", "gtb0"], writes=[f"x1_{tt}", "x1"])
                P.op("dve", lambda e, xt=xt, tt=tt, hs=hs: e.tensor_tensor(out=x1[:, tt, hs], in0=x1[:, tt, hs], in1=xt[:, hs], op=ALU.add),
                     reads=[f"x1_{tt}", xname], writes=[f"x1_{tt}", "x1"])
            n2p.avail = tt + 1
            n2p.step()
        for m in range(8):
            W.release(("mo", m))
        n2p.run(until=lambda: n2p.groups_done >= 2)
        if ci + 1 < NCH:
            n1p = norm_to_T(ci + 1, EXT // 128, 0, xinC, ["xinC0", "xinC1", "xinC2"], xsC, "xsC", 0, xnT, "xnT", a1T, 0, 0, act_share=0,
                            sq_dve=True, tr_lag=2)
        else:
            n1p = None
        n1_done = [0]

        def n1_steps(target):
            while n1p is not None and not n1p.done and n1_done[0] < target:
                n1p.step()
                n1_done[0] += 1

        def pipes_to_clean():
            for p_ in (n2p, n1p):
                if p_ is not None:
                    p_.hold = True
                    p_.run(until=lambda: p_.clean)

        def pipes_release():
            for p_ in (n2p, n1p):
                if p_ is not None:
                    p_.hold = False

        def fin_A(blk, tl):
            tt = blk * 4 + tl
            xo, xon = x2s[tl % 2], f"x2s{tl % 2}"
            for half in range(2):
                bk = tl * 2 + half
                hs = slice(half * 512, half * 512 + 512)
                P.op("dve", lambda e, bk=bk, xo=xo, hs=hs: e.tensor_tensor(out=xo[:, hs], in0=banks[bk][:, :], in1=gtb[:, 1, hs],
                                                                          op=ALU.mult),
                     reads=[f"bank{bk}", "gtb1"], writes=[xon])
                P.op("dve", lambda e, xo=xo, hs=hs, tt=tt: e.tensor_tensor(out=xo[:, hs], in0=xo[:, hs], in1=x1[:, tt, hs], op=ALU.add),
                     reads=[xon, "x1", f"x1_{tt}"], writes=[xon])

        def fin_S(blk, tl):
            tt = blk * 4 + tl
            xo, xon = x2s[tl % 2], f"x2s{tl % 2}"
            col = 32 + tt
            P.op("act", lambda e, xo=xo, col=col: e.activation(out=xn2T[:, 0:2, 0:512], in_=xo[:, :].rearrange("p (a b) -> p a b", a=2),
                                                                func=AF.Square, accum_out=ss[:, col:col + 1]),
                 reads=[xon, f"ssc{col}"], writes=[f"ssc{col}", "xn2T0"])
            P.op("act", lambda e, col=col: e.activation(out=rs[:, col:col + 1], in_=ss[:, col:col + 1], func=AF.Sqrt,
                                                        bias=epsc[:, 0:1], scale=1.0 / D),
                 reads=[f"ssc{col}", "epsc"], writes=[f"rsc{col}"])

        def fin_B(blk, tl):
            tt = blk * 4 + tl
            xo, xon = x2s[tl % 2], f"x2s{tl % 2}"
            col = 32 + tt
            P.op("dve", lambda e, col=col: e.reciprocal(out=rs[:, col:col + 1], in_=rs[:, col:col + 1]),
                 reads=[f"rsc{col}"], writes=[f"rsc{col}"])
            P.op("dve", lambda e, xo=xo, col=col: e.scalar_tensor_tensor(out=xo[:, :], in0=xo[:, :], scalar=rs[:, col:col + 1],
                                                                        in1=gfb[:, :], op0=ALU.mult, op1=ALU.mult),
                 reads=[xon, f"rsc{col}", "gfb"], writes=[xon])
            P.op("sp", lambda e, xo=xo, tt=tt, ci=ci: e.dma_start(out=y_d[ci, tt * 128:(tt + 1) * 128, :], in_=xo[:, :]),
                 reads=[xon], dma=xon)

        pending = {}
        for blk in range(2):
            t0 = blk * 512
            for f in range(32):
                (w1t, w1n) = W.get(("w1", f))
                w1v = w1t[:, :].rearrange("p (k c) -> p k c", k=8)
                bk = f % 4
                for k in range(8):
                    mm(banks[bk][:, :], w1v[:, k, :], xn2T[:, k, t0:t0 + 512], k == 0, k == 7, [w1n, f"xn2T{blk}"], [f"bank{bk}"])
                st_ = sgt[f % 4]
                stn = f"sgt{f % 4}"
                P.op("act", lambda e, bk=bk, st_=st_: e.activation(out=st_[:, :], in_=banks[bk][:, :], func=AF.Relu),
                     reads=[f"bank{bk}"], writes=[stn])
                P.op("dve", lambda e, bk=bk, st_=st_, f=f: e.tensor_tensor(out=hT[:, f, :], in0=banks[bk][:, :], in1=st_[:, :], op=ALU.mult),
                     reads=[f"bank{bk}", stn], writes=["hT"])
                W.release(("w1", f))
                for fn_ in pending.pop(f, []):
                    fn_()
                if blk == 1 and f == 4:
                    pipes_release()
                if blk == 0 and f == 12 and ci + 1 < NCH:
                    emit_gt(0, ci + 1, 3)
                if f == 27:
                    for p_ in (n2p, n1p):
                        if p_ is not None:
                            p_.hold = True
                if blk == 0:
                    if f < 8 and not n2p.done:
                        n2p.step()
                    else:
                        n1_steps((f - 7) * 15 // 24)
                else:
                    n1_steps(15 + (f + 1) * 16 // 32)
            if blk == 0:
                n2p.run()
            pipes_to_clean()
            for f in range(32):
                (w2t, w2n) = W.get(("w2", f))
                for tl in range(4):
                    for half in range(2):
                        bk = tl * 2 + half
                        mm(banks[bk][:, :], hT[:, f, tl * 128:(tl + 1) * 128], w2t[:, half * 512:(half + 1) * 512], f == 0, f == 31,
                           [w2n, "hT"], [f"bank{bk}"])
                W.release(("w2", f))
                if f % 8 == 7 and n1p is not None and not n1p.done:
                    n1p.step()
            fin_A(blk, 0); fin_S(blk, 0); fin_A(blk, 1); fin_S(blk, 1); fin_B(blk, 0); fin_B(blk, 1)
            if blk == 0:
                pending = {1: [lambda: (fin_A(0, 2), fin_S(0, 2))], 3: [lambda: (fin_A(0, 3), fin_S(0, 3))],
                           5: [lambda: fin_B(0, 2)], 7: [lambda: fin_B(0, 3)]}
            else:
                fin_A(blk, 2); fin_S(blk, 2); fin_A(blk, 3); fin_S(blk, 3); fin_B(blk, 2); fin_B(blk, 3)
                pipes_release()
    counts = P.emit()
    return nc, counts


def _chunks():
    ch = []
    for b in range(4):
        for j in range(4):
            ch.append((0, b, j))
    for b in range(2):
        for j in range(16):
            ch.append((1, b, j))
    return ch


def _lhsT_tile(Wm, cols):
    t = np.ascontiguousarray(Wm[:, cols])
    return t.reshape(8, 128, 128).transpose(1, 0, 2).reshape(128, 1024)


def _build_wall(w_in, w_ao, w_co, w_mo, w1, w2):
    wall = np.zeros((NWT, 128, 1024), np.float32)
    for u in range(2):
        for g in range(3):
            heads = [8 * g + 4 * u + j for j in range(4)]
            x1c = np.concatenate([np.arange(h * 64, h * 64 + 32) for h in heads])
            x2c = x1c + 32
            wall[WIDX[("q1", g, u)]] = _lhsT_tile(w_in, x1c)
            wall[WIDX[("q2", g, u)]] = _lhsT_tile(w_in, x2c)
            wall[WIDX[("k1", g, u)]] = _lhsT_tile(w_in, 1536 + x1c)
            wall[WIDX[("k2", g, u)]] = _lhsT_tile(w_in, 1536 + x2c)
            c0 = 3072 + (8 * g + 4 * u) * 64
            tv = np.ascontiguousarray(w_in[:, c0:c0 + 256]).reshape(8, 128, 256).transpose(1, 0, 2)
            wall[WIDX[("va", g, u)]] = tv[:, 0:4, :].reshape(128, 1024)
            wall[WIDX[("vb", g, u)]] = tv[:, 4:8, :].reshape(128, 1024)
    for s in range(8):
        wall[WIDX[("h", s)]] = _lhsT_tile(w_in, np.arange(4608 + s * 128, 4608 + s * 128 + 128))
        wall[WIDX[("c", s)]] = _lhsT_tile(w_in, np.arange(5632 + s * 128, 5632 + s * 128 + 128))
        wall[WIDX[("b", s)]] = _lhsT_tile(w_in, np.arange(6656 + s * 128, 6656 + s * 128 + 128))
    for m in range(8):
        wall[WIDX[("ga", m)]] = _lhsT_tile(w_in, np.arange(7680 + m * 128, 7680 + m * 128 + 128))
        wall[WIDX[("gc", m)]] = _lhsT_tile(w_in, np.arange(8704 + m * 128, 8704 + m * 128 + 128))
        wall[WIDX[("co", m)]] = _lhsT_tile(w_co, np.arange(m * 128, m * 128 + 128))
        wall[WIDX[("mo", m)]] = w_mo[m * 128:(m + 1) * 128, :]
    for mp in range(4):
        t = np.zeros((128, 8, 128), np.float32)
        for m2 in range(2):
            for pi in range(4):
                m = 2 * mp + m2
                t[:, m2 * 4 + pi, :] = w_ao[pi * 128:(pi + 1) * 128, m * 128:(m + 1) * 128]
        wall[WIDX[("ao", mp)]] = t.reshape(128, 1024)
    for f in range(32):
        wall[WIDX[("w1", f)]] = _lhsT_tile(w1, np.arange(f * 128, f * 128 + 128))
        wall[WIDX[("w2", f)]] = w2[f * 128:(f + 1) * 128, :]
    return wall


def _key_tile_ext():
    ii = np.arange(128)
    rows = []
    for kt in range(9):
        rows.append(960 + 128 * kt + ii)
    for r in range(4):
        for kk in range(3):
            rows.append(768 + 512 * kk + r + 4 * ii)
    for cp in range(8):
        rows.append(cp + 16 * ii)
        rows.append(cp + 8 + 16 * ii)
        rows.append(2048 + cp + 8 * ii)
    return np.stack(rows)


def _constants():
    i = np.arange(128)[:, None]
    j = np.arange(128)[None, :]
    mA = np.where(i >= j, 0.0, NEG)
    mB = np.where(i <= j, 0.0, NEG)
    mC = np.where((i >= 64) & (i <= j + 64), 0.0, NEG)
    mLo = np.where((i % 2 == 0) & (i // 2 <= j), 0.0, NEG)
    mHi = np.where((i % 2 == 1) & ((i - 1) // 2 <= j), 0.0, NEG)
    masks = np.stack([np.stack([m, m], axis=1) for m in (mA, mB, mC, mLo, mHi)], axis=1)
    cst = np.zeros((128, 1536), np.float32)
    cst[:, 0:1280] = masks.reshape(128, 1280)
    cst[:, 1280:1408] = np.eye(128, dtype=np.float32)
    cst[:, 1408:1536] = 1.0
    return cst


_PROGRAM_CACHE = {}


def kernel(x_prompt, x_sample, c_prompt, c_sample, w_ada, b_ada, norm1_g, w_in, conv_w, conv_b,
           w_attn_out, w_conv_out, w_mix_out, norm2_g, w_mlp_in, w_mlp_out, final_norm_g):
    f32 = np.float32
    xs = [np.asarray(x_prompt, f32), np.asarray(x_sample, f32)]
    cs_in = [np.asarray(c_prompt, f32), np.asarray(c_sample, f32)]
    seqlen = [xs[0].shape[1], xs[1].shape[1]]
    chunks = _chunks()
    wall = _build_wall(np.asarray(w_in, f32)[0], np.asarray(w_attn_out, f32)[0], np.asarray(w_conv_out, f32)[0],
                       np.asarray(w_mix_out, f32)[0], np.asarray(w_mlp_in, f32)[0], np.asarray(w_mlp_out, f32)[0])
    wa = np.asarray(w_ada, f32)[0]
    wada = np.stack([_lhsT_tile(wa, np.arange(et * 128, et * 128 + 128)) for et in range(48)])
    smallp = np.zeros((128, 96), f32)
    smallp[:, 0:48] = np.asarray(b_ada, f32)[0].reshape(48, 128).T
    smallp[:, 48:56] = np.asarray(norm1_g, f32)[0].reshape(8, 128).T
    smallp[:, 56:64] = np.asarray(norm2_g, f32)[0].reshape(8, 128).T
    cw = np.asarray(conv_w, f32)[0]
    smallp[:, 64:88] = cw.reshape(3, 8, 128).transpose(2, 1, 0).reshape(128, 24)
    smallp[:, 88:96] = np.asarray(conv_b, f32)[0].reshape(8, 128).T
    gfb = np.ascontiguousarray(np.broadcast_to(np.asarray(final_norm_g, f32)[None, :], (128, 1024)))
    cst = _constants()
    ktext = _key_tile_ext()
    inv = (1.0 / (np.float32(10000.0) ** (np.arange(32, dtype=f32) / np.float32(32)))).astype(f32)
    invp = inv[np.arange(128) % 32]

    in_maps = []
    for core in range(NCORES):
        xext = np.zeros((NCH, EXT, D), f32)
        cT = np.zeros((128, 8, NCH), f32)
        cst_cs = np.zeros((NCH, 128, 2, EXT), f32)
        kb = np.zeros((NCH, 128, NKT), f32)
        uv = np.zeros((NCH, 128, 2), f32)
        for ci in range(NCH):
            which, b, j = chunks[core * NCH + ci]
            S = seqlen[which]
            p0 = j * T - HALO
            lo, hi = max(p0, 0), min(p0 + EXT, S)
            xext[ci, lo - p0:hi - p0, :] = xs[which][b, lo:hi, :]
            cT[:, :, ci] = cs_in[which][b].reshape(8, 128).T
            pos = (p0 + np.arange(EXT)).astype(f32)
            ang = (pos[None, :] * invp[:, None]).astype(f32)
            cst_cs[ci, :, 0, :] = np.cos(ang)
            cst_cs[ci, :, 1, :] = np.sin(ang)
            kpos = p0 + ktext
            kb[ci] = np.where((kpos >= 0) & (kpos < S), 0.0, NEG).T
            uv[ci, :, 0] = 1.0 if j * T - 1 >= 0 else 0.0
            uv[ci, :, 1] = 1.0 if j * T + T < S else 0.0
        in_maps.append(dict(xext=xext, cT=cT, cs=cst_cs, kbias=kb, uval=uv, wall=wall, wada=wada, smallp=smallp,
                            gfb=gfb, cst=cst))

    if "nc" not in _PROGRAM_CACHE:
        _PROGRAM_CACHE["nc"] = build_program()[0]
    nc = _PROGRAM_CACHE["nc"]
    res = run_bass_kernel_spmd(nc, in_maps, core_ids=list(range(NCORES)))
    y_p = np.zeros_like(xs[0])
    y_s = np.zeros_like(xs[1])
    outs = [y_p, y_s]
    for core in range(NCORES):
        y = res.results[core]["y"]
        for ci in range(NCH):
            which, b, j = chunks[core * NCH + ci]
            outs[which][b, j * T:(j + 1) * T, :] = y[ci]
    return (y_p, y_s)
```
